# Optimizing a Trainium2 kernel written in Bass

```python
import math
import jax, jax.numpy as jnp
from jax import lax
import numpy as np

D_MODEL = 1024
BATCH = 4
SEQ = 4096
DEPTH = 2
DEC_BATCH = 128
DEC_SEQ = 4
PAST_LEN = 2048
PAGE_SIZE = 128

N_EVEN = (DEPTH + 1) // 2
N_ODD = DEPTH // 2
SB_HEADS = 8
SB_HEAD_DIM = D_MODEL // 16
SB_WIDTH = SB_HEADS * SB_HEAD_DIM
SB_BIAS_INIT = -8.0
Q_BLOCK = 128
LRU_WIDTH = D_MODEL // 2
LRU_BLOCKS = 8
LRU_BLOCK_DIM = LRU_WIDTH // LRU_BLOCKS
LRU_C = 8.0
CONV_WIDTH = 4
DN_HEADS = 8
DN_HEAD_DIM = D_MODEL // 8
DN_WIDTH = DN_HEADS * DN_HEAD_DIM
DN_CHUNK = 64
D_FF = 2 * D_MODEL
EVEN_IN = 3 * SB_WIDTH + 2 * LRU_WIDTH
ODD_IN = 4 * DN_WIDTH + 2 * DN_HEADS
EPS = 1e-6

kernel_name = 'stickbreak_rglru_gdn_macaron_step'


def rms_norm(x, g):
    xf = x.astype(jnp.float32)
    y = xf * lax.rsqrt(jnp.mean(xf * xf, axis=-1, keepdims=True) + EPS)
    return (y * g.astype(jnp.float32)).astype(x.dtype)


def l2_normalize(x):
    xf = x.astype(jnp.float32)
    return xf * lax.rsqrt(jnp.sum(xf * xf, axis=-1, keepdims=True) + EPS)


def swiglu(x, w_in, w_out):
    g, u = jnp.split(x @ w_in, 2, axis=-1)
    return (jax.nn.silu(g) * u) @ w_out


def causal_conv(buf, u, w, b=None):
    xp = jnp.concatenate([buf.astype(u.dtype), u], axis=1)
    T = u.shape[1]
    y = w[0] * xp[:, 0:T]
    for i in range(1, CONV_WIDTH):
        y = y + w[i] * xp[:, i:i + T]
    if b is not None:
        y = y + b
    return y, xp[:, -(CONV_WIDTH - 1):]


def sb_block(q, q_pos, k, v, k_pos, bias):
    z = jnp.einsum('bqhd,bshd->bhqs', q, k, preferred_element_type=jnp.float32) * (SB_HEAD_DIM ** -0.5)
    z = z + bias.astype(jnp.float32)[None, :, None, None]
    mask = k_pos[None, :] < q_pos[:, None]
    log_keep = jnp.where(mask, jax.nn.log_sigmoid(-z), 0.0)
    later = lax.cumsum(log_keep, axis=3, reverse=True) - log_keep
    w = jnp.where(mask, jnp.exp(jax.nn.log_sigmoid(z) + later), 0.0)
    return jnp.einsum('bhqs,bshd->bqhd', w.astype(v.dtype), v)


def sb_attention(q, k, v, q_pos, k_pos, bias):
    B, T, H, Dh = q.shape
    blk = Q_BLOCK if T % Q_BLOCK == 0 else T
    nb = T // blk
    qb = q.reshape(B, nb, blk, H, Dh).transpose(1, 0, 2, 3, 4)
    pb = q_pos.reshape(nb, blk)
    ob = lax.map(lambda a: sb_block(a[0], a[1], k, v, k_pos, bias), (qb, pb))
    return ob.transpose(1, 0, 2, 3, 4).reshape(B, T, H, Dh)


def rg_lru(xc, h0, w_a, b_a, w_i, b_i, lam):
    B, T, W = xc.shape
    xb = xc.reshape(B, T, LRU_BLOCKS, LRU_BLOCK_DIM)
    r = jax.nn.sigmoid((jnp.einsum('btnc,ncd->btnd', xb, w_a).reshape(B, T, W) + b_a).astype(jnp.float32))
    i = jax.nn.sigmoid((jnp.einsum('btnc,ncd->btnd', xb, w_i).reshape(B, T, W) + b_i).astype(jnp.float32))
    log_a = -LRU_C * r * jax.nn.softplus(-lam.astype(jnp.float32))
    a = jnp.exp(log_a)
    b = jnp.sqrt(-jnp.expm1(2.0 * log_a)) * (i * xc.astype(jnp.float32))
    b = b.at[:, 0].add(a[:, 0] * h0.astype(jnp.float32))

    def combine(left, right):
        a1, b1 = left
        a2, b2 = right
        return a1 * a2, a2 * b1 + b2

    _, h = lax.associative_scan(combine, (a, b), axis=1)
    return h, h[:, -1]


def sb_lru_mixer(u, past_k, past_v, conv_buf, h0, w_in, q_gain, k_gain, sb_bias, conv_w, conv_b,
                 w_a, b_a, w_i, b_i, lam, w_out):
    B, T, _ = u.shape
    proj = u @ w_in
    q, k, v, xr, xg = jnp.split(proj, [SB_WIDTH, 2 * SB_WIDTH, 3 * SB_WIDTH, 3 * SB_WIDTH + LRU_WIDTH], axis=-1)
    q = rms_norm(q.reshape(B, T, SB_HEADS, SB_HEAD_DIM), q_gain)
    k = rms_norm(k.reshape(B, T, SB_HEADS, SB_HEAD_DIM), k_gain)
    v = v.reshape(B, T, SB_HEADS, SB_HEAD_DIM)
    past_len = past_k.shape[1]
    k_all = jnp.concatenate([past_k.astype(k.dtype), k], axis=1)
    v_all = jnp.concatenate([past_v.astype(v.dtype), v], axis=1)
    q_pos = past_len + jnp.arange(T)
    k_pos = jnp.arange(past_len + T)
    attn = sb_attention(q, k_all, v_all, q_pos, k_pos, sb_bias).reshape(B, T, SB_WIDTH)
    xc, new_buf = causal_conv(conv_buf, xr, conv_w, conv_b)
    h, h_last = rg_lru(xc, h0, w_a, b_a, w_i, b_i, lam)
    rec = h.astype(u.dtype) * jax.nn.gelu(xg)
    out = jnp.concatenate([attn.astype(u.dtype), rec], axis=-1) @ w_out
    return out, (k, v, new_buf, h_last)


def gated_delta_chunked(q, k, v, beta, g, S0):
    B, T, H, DK = q.shape
    DV = v.shape[-1]
    C = min(DN_CHUNK, T)
    pad = (-T) % C
    f32 = jnp.float32

    def prep(a):
        a = a.astype(f32)
        a = jnp.pad(a, [(0, 0), (0, pad)] + [(0, 0)] * (a.ndim - 2))
        n = a.shape[1] // C
        a = a.reshape((B, n, C) + a.shape[2:])
        return jnp.swapaxes(a, 2, 3)

    qc, kc, vc, bc, gc = prep(q), prep(k), prep(v), prep(beta), prep(g)
    G = jnp.cumsum(gc, axis=-1)
    incl = jnp.tril(jnp.ones((C, C), bool))
    strict = jnp.tril(jnp.ones((C, C), bool), -1)
    decay = jnp.exp(jnp.where(incl, G[..., :, None] - G[..., None, :], -jnp.inf))
    A = jnp.where(strict, jnp.einsum('bnhcd,bnhsd->bnhcs', kc, kc) * decay, 0.0) * bc[..., None]
    rhs = jnp.concatenate([vc * bc[..., None], kc * (bc * jnp.exp(G))[..., None]], axis=-1)
    sol = lax.linalg.triangular_solve(A, rhs, left_side=True, lower=True, unit_diagonal=True)
    Uv, Wk = sol[..., :DV], sol[..., DV:]
    P = jnp.einsum('bnhcd,bnhsd->bnhcs', qc, kc) * decay
    qg = qc * jnp.exp(G)[..., None]
    kg = kc * jnp.exp(G[..., -1:] - G)[..., None]
    gl = jnp.exp(G[..., -1])

    def step(S, xs):
        uv, wk, p, qq, kk, gg = xs
        U = uv - jnp.einsum('bhcd,bhde->bhce', wk, S)
        O = jnp.einsum('bhcd,bhde->bhce', qq, S) + jnp.einsum('bhcs,bhse->bhce', p, U)
        S = gg[..., None, None] * S + jnp.einsum('bhcd,bhce->bhde', kk, U)
        return S, O

    xs = tuple(jnp.moveaxis(a, 1, 0) for a in (Uv, Wk, P, qg, kg, gl))
    S, O = lax.scan(step, S0.astype(f32), xs)
    O = jnp.transpose(O, (1, 0, 3, 2, 4)).reshape(B, -1, H, DV)[:, :T]
    return O, S


def deltanet_mixer(u, conv_buf, S0, w_in, conv_w, A_log, dt_bias, o_gain, w_out):
    B, T, _ = u.shape
    proj = u @ w_in
    qkv, z, b_logit, a_logit = jnp.split(proj, [3 * DN_WIDTH, 4 * DN_WIDTH, 4 * DN_WIDTH + DN_HEADS], axis=-1)
    qkv, new_buf = causal_conv(conv_buf, qkv, conv_w)
    q, k, v = jnp.split(jax.nn.silu(qkv), 3, axis=-1)
    q = l2_normalize(q.reshape(B, T, DN_HEADS, DN_HEAD_DIM)) * (DN_HEAD_DIM ** -0.5)
    k = l2_normalize(k.reshape(B, T, DN_HEADS, DN_HEAD_DIM))
    v = v.reshape(B, T, DN_HEADS, DN_HEAD_DIM)
    beta = jax.nn.sigmoid(b_logit.astype(jnp.float32))
    g = -jnp.exp(A_log.astype(jnp.float32)) * jax.nn.softplus(a_logit.astype(jnp.float32) + dt_bias.astype(jnp.float32))
    o, S = gated_delta_chunked(q, k, v, beta, g, S0)
    o = rms_norm(o, o_gain) * jax.nn.silu(z.reshape(B, T, DN_HEADS, DN_HEAD_DIM).astype(jnp.float32))
    return o.reshape(B, T, DN_WIDTH).astype(u.dtype) @ w_out, (new_buf, S)


def macaron_layer(x, mixer, n1, w1i, w1o, nm, n2, w2i, w2o):
    x = x + 0.5 * swiglu(rms_norm(x, n1), w1i, w1o)
    m, new_state = mixer(rms_norm(x, nm))
    x = x + m
    x = x + 0.5 * swiglu(rms_norm(x, n2), w2i, w2o)
    return x, new_state


def setup_inputs(seed: int = 0) -> dict:
    key = jax.random.key(seed)
    keys = iter(jax.random.split(key, 48))
    f32 = jnp.float32

    def nrm(shape, scale):
        return jax.random.normal(next(keys), shape, f32) * scale

    def gain(shape):
        return 1.0 + nrm(shape, 0.02)

    n_pages = PAST_LEN // PAGE_SIZE
    n_used = DEC_BATCH * n_pages
    n_pool = n_used + max(1, n_used // 4)
    bd = LRU_BLOCK_DIM
    x_prompt = nrm((BATCH, SEQ, D_MODEL), 1.0)
    x_sample = nrm((DEC_BATCH, DEC_SEQ, D_MODEL), 1.0)
    cache_k = nrm((N_EVEN, n_pool, PAGE_SIZE, SB_HEADS, SB_HEAD_DIM), 1.0)
    cache_v = nrm((N_EVEN, n_pool, PAGE_SIZE, SB_HEADS, SB_HEAD_DIM), 1.0)
    state_lru_conv = nrm((N_EVEN, DEC_BATCH, CONV_WIDTH - 1, LRU_WIDTH), 1.0)
    state_lru_h = nrm((N_EVEN, DEC_BATCH, LRU_WIDTH), 0.5)
    state_dn_conv = nrm((N_ODD, DEC_BATCH, CONV_WIDTH - 1, 3 * DN_WIDTH), 1.0)
    state_dn_S = nrm((N_ODD, DEC_BATCH, DN_HEADS, DN_HEAD_DIM, DN_HEAD_DIM), DN_HEAD_DIM ** -0.5)
    page_table = jax.random.permutation(next(keys), n_pool)[:n_used].reshape(DEC_BATCH, n_pages).astype(jnp.int32)
    a_c = jax.random.uniform(next(keys), (N_EVEN, LRU_WIDTH), f32, 0.9, 0.999)
    a_base = a_c ** (1.0 / LRU_C)
    lru_lambda = jnp.log(a_base) - jnp.log1p(-a_base)
    dn_A_log = jnp.log(jax.random.uniform(next(keys), (N_ODD, DN_HEADS), f32, 1.0, 16.0))
    dt0 = jnp.exp(jax.random.uniform(next(keys), (N_ODD, DN_HEADS), f32, math.log(1e-3), math.log(1e-1)))
    dn_dt_bias = dt0 + jnp.log(-jnp.expm1(-dt0))
    return {
        'x_prompt': x_prompt, 'x_sample': x_sample,
        'cache_k': cache_k, 'cache_v': cache_v,
        'state_lru_conv': state_lru_conv, 'state_lru_h': state_lru_h,
        'state_dn_conv': state_dn_conv, 'state_dn_S': state_dn_S,
        'page_table': page_table,
        'norm_ffn1': gain((DEPTH, D_MODEL)),
        'w_ffn1_in': nrm((DEPTH, D_MODEL, 2 * D_FF), D_MODEL ** -0.5),
        'w_ffn1_out': nrm((DEPTH, D_FF, D_MODEL), D_FF ** -0.5),
        'norm_mix': gain((DEPTH, D_MODEL)),
        'norm_ffn2': gain((DEPTH, D_MODEL)),
        'w_ffn2_in': nrm((DEPTH, D_MODEL, 2 * D_FF), D_MODEL ** -0.5),
        'w_ffn2_out': nrm((DEPTH, D_FF, D_MODEL), D_FF ** -0.5),
        'w_in_even': nrm((N_EVEN, D_MODEL, EVEN_IN), D_MODEL ** -0.5),
        'sb_q_gain': gain((N_EVEN, SB_HEAD_DIM)),
        'sb_k_gain': gain((N_EVEN, SB_HEAD_DIM)),
        'sb_bias': SB_BIAS_INIT + nrm((N_EVEN, SB_HEADS), 0.1),
        'lru_conv_w': nrm((N_EVEN, CONV_WIDTH, LRU_WIDTH), CONV_WIDTH ** -0.5),
        'lru_conv_b': nrm((N_EVEN, LRU_WIDTH), 0.01),
        'lru_w_a': nrm((N_EVEN, LRU_BLOCKS, bd, bd), bd ** -0.5),
        'lru_b_a': nrm((N_EVEN, LRU_WIDTH), 0.01),
        'lru_w_i': nrm((N_EVEN, LRU_BLOCKS, bd, bd), bd ** -0.5),
        'lru_b_i': nrm((N_EVEN, LRU_WIDTH), 0.01),
        'lru_lambda': lru_lambda,
        'w_out_even': nrm((N_EVEN, SB_WIDTH + LRU_WIDTH, D_MODEL), (SB_WIDTH + LRU_WIDTH) ** -0.5),
        'w_in_odd': nrm((N_ODD, D_MODEL, ODD_IN), D_MODEL ** -0.5),
        'dn_conv_w': nrm((N_ODD, CONV_WIDTH, 3 * DN_WIDTH), CONV_WIDTH ** -0.5),
        'dn_A_log': dn_A_log,
        'dn_dt_bias': dn_dt_bias,
        'dn_o_gain': gain((N_ODD, DN_HEAD_DIM)),
        'w_out_odd': nrm((N_ODD, DN_WIDTH, D_MODEL), DN_WIDTH ** -0.5),
    }


def reference(x_prompt, x_sample, cache_k, cache_v, state_lru_conv, state_lru_h, state_dn_conv, state_dn_S,
              page_table, norm_ffn1, w_ffn1_in, w_ffn1_out, norm_mix, norm_ffn2, w_ffn2_in, w_ffn2_out,
              w_in_even, sb_q_gain, sb_k_gain, sb_bias, lru_conv_w, lru_conv_b, lru_w_a, lru_b_a, lru_w_i, lru_b_i,
              lru_lambda, w_out_even, w_in_odd, dn_conv_w, dn_A_log, dn_dt_bias, dn_o_gain, w_out_odd):
    dt = x_prompt.dtype
    nb_p = x_prompt.shape[0]
    nb_s = x_sample.shape[0]
    n_pages = page_table.shape[1]
    empty_kv = jnp.zeros((nb_p, 0, SB_HEADS, SB_HEAD_DIM), dt)
    zero_lru_conv = jnp.zeros((nb_p, CONV_WIDTH - 1, LRU_WIDTH), dt)
    zero_lru_h = jnp.zeros((nb_p, LRU_WIDTH), jnp.float32)
    zero_dn_conv = jnp.zeros((nb_p, CONV_WIDTH - 1, 3 * DN_WIDTH), dt)
    zero_dn_S = jnp.zeros((nb_p, DN_HEADS, DN_HEAD_DIM, DN_HEAD_DIM), jnp.float32)
    yp, ys = x_prompt, x_sample
    kp, vp, lcp, lhp, dcp, dsp = [], [], [], [], [], []
    ks, vs, lcs, lhs, dcs, dss = [], [], [], [], [], []
    for l in range(DEPTH):
        ffn = (norm_ffn1[l], w_ffn1_in[l], w_ffn1_out[l], norm_mix[l], norm_ffn2[l], w_ffn2_in[l], w_ffn2_out[l])
        if l % 2 == 0:
            e = l // 2
            wts = (w_in_even[e], sb_q_gain[e], sb_k_gain[e], sb_bias[e], lru_conv_w[e], lru_conv_b[e], lru_w_a[e],
                   lru_b_a[e], lru_w_i[e], lru_b_i[e], lru_lambda[e], w_out_even[e])
            past_k = cache_k[e][page_table].reshape(nb_s, n_pages * PAGE_SIZE, SB_HEADS, SB_HEAD_DIM)
            past_v = cache_v[e][page_table].reshape(nb_s, n_pages * PAGE_SIZE, SB_HEADS, SB_HEAD_DIM)
            yp, (k1, v1, c1, h1) = macaron_layer(
                yp, lambda u: sb_lru_mixer(u, empty_kv, empty_kv, zero_lru_conv, zero_lru_h, *wts), *ffn)
            ys, (k2, v2, c2, h2) = macaron_layer(
                ys, lambda u: sb_lru_mixer(u, past_k, past_v, state_lru_conv[e], state_lru_h[e], *wts), *ffn)
            kp.append(k1); vp.append(v1); lcp.append(c1); lhp.append(h1)
            ks.append(k2); vs.append(v2); lcs.append(c2); lhs.append(h2)
        else:
            o = l // 2
            wts = (w_in_odd[o], dn_conv_w[o], dn_A_log[o], dn_dt_bias[o], dn_o_gain[o], w_out_odd[o])
            yp, (c1, S1) = macaron_layer(yp, lambda u: deltanet_mixer(u, zero_dn_conv, zero_dn_S, *wts), *ffn)
            ys, (c2, S2) = macaron_layer(
                ys, lambda u: deltanet_mixer(u, state_dn_conv[o], state_dn_S[o], *wts), *ffn)
            dcp.append(c1); dsp.append(S1)
            dcs.append(c2); dss.append(S2)
    return (yp, ys,
            jnp.stack(kp), jnp.stack(vp), jnp.stack(lcp), jnp.stack(lhp), jnp.stack(dcp), jnp.stack(dsp),
            jnp.stack(ks), jnp.stack(vs), jnp.stack(lcs), jnp.stack(lhs), jnp.stack(dcs), jnp.stack(dss))
```

```python
from contextlib import ExitStack
import os
import numpy as np
import concourse.bass as bass
import concourse.mybir as mybir
from concourse.bass_utils import run_bass_kernel_spmd

F32 = mybir.dt.float32
BF16 = mybir.dt.bfloat16
I32 = mybir.dt.int32
ALU = mybir.AluOpType
AF = mybir.ActivationFunctionType

SAME_ENGINE_SYNC = True
SAME_SYNC_ENGS = ('pool',)
NCORES = 8
D = 1024
TP = 2048
NS = 16
TS = NS * 4
T = TP + TS
TT = [(0, 512), (512, 512), (1024, 512), (1536, 512), (2048, 64)]
EPS = 1e-6
POOLN = 2560
DBG = os.environ.get('KDBG', '').split(',')


class Op:
    __slots__ = ("eng", "fn", "deps", "dma", "sig", "need", "pre", "idx", "cc")

    def __init__(self, eng, fn, dma, cc=None):
        self.eng = eng
        self.fn = fn
        self.dma = dma
        self.cc = cc
        self.deps = []
        self.sig = None
        self.need = dma
        self.pre = None


class Prog:
    ENGS = ("pe", "act", "dve", "pool", "sp")

    def __init__(self, nc):
        self.nc = nc
        self.ops = []
        self.last_w = {}
        self.readers = {}
        self.all_dma = []
        self.force_sync = True

    def add(self, eng, fn, reads=(), writes=(), dma=False, cc=None):
        op = Op(eng, fn, dma or cc is not None, cc)
        writes = list(writes) + [k for k in reads if isinstance(k, tuple) and k[0] == "ps" and k not in writes]
        deps = {}
        for k in reads:
            w = self.last_w.get(k)
            if w is not None:
                deps[id(w)] = w
        for k in writes:
            w = self.last_w.get(k)
            if w is not None:
                deps[id(w)] = w
            for r in self.readers.get(k, ()):
                deps[id(r)] = r
        for d in deps.values():
            if d.dma or d.eng != eng:
                d.need = True
                op.deps.append(d)
            elif (SAME_ENGINE_SYNC and eng in SAME_SYNC_ENGS) or (self.force_sync and eng != "pe"):
                d.need = True
                op.deps.append(d)
        for k in writes:
            self.last_w[k] = op
            self.readers[k] = []
        for k in reads:
            lst = self.readers.setdefault(k, [])
            if not op.dma:
                lst[:] = [r for r in lst if r.dma or r.eng != eng]
            lst.append(op)
        op.idx = len(self.ops)
        self.ops.append(op)
        if op.dma:
            self.all_dma.append(op)
        return op

    def barrier(self):
        last = {}
        for op in self.ops:
            if not op.dma and op.fn is not None:
                last[op.eng] = op
        pend = list(self.all_dma)
        self.all_dma = []
        for e in self.ENGS:
            op = Op(e, None, False)
            for o in last.values():
                if o.eng != e or (SAME_ENGINE_SYNC and e != "pe"):
                    o.need = True
                    op.deps.append(o)
            op.deps.extend(pend)
            op.idx = len(self.ops)
            self.ops.append(op)
        self.last_w = {}
        self.readers = {}

    def finalize_dma_wait(self, eng="sp"):
        op = Op(eng, None, False)
        op.deps.extend(self.all_dma)
        op.idx = len(self.ops)
        self.ops.append(op)

    def emit(self, block, sems):
        cnt = {e: 0 for e in self.ENGS}
        dcnt = {e: 0 for e in self.ENGS}
        duse = {}
        for op in self.ops:
            if op.cc is not None:
                op.sig = (op.cc, 1)
            elif op.dma:
                pool = sems["dma"][op.eng]
                s = pool[dcnt[op.eng] % len(pool)]
                dcnt[op.eng] += 1
                u = duse.get(id(s), 0)
                if u > 0:
                    op.pre = (s, 16 * u)
                duse[id(s)] = u + 1
                op.sig = (s, 16 * (u + 1))
            elif op.need:
                cnt[op.eng] += 1
                op.sig = (sems["eng"][op.eng], cnt[op.eng])
        stats = {e: [0, 0] for e in self.ENGS}

        def run(eng_name):
            def body(eng):
                waited = {}
                for op in self.ops:
                    if op.eng != eng_name:
                        continue
                    w = {}
                    for d in op.deps:
                        s, v = d.sig
                        if w.get(id(s), (None, 0))[1] < v:
                            w[id(s)] = (s, v)
                    if op.pre is not None:
                        s, v = op.pre
                        if w.get(id(s), (None, 0))[1] < v:
                            w[id(s)] = (s, v)
                    for s, v in w.values():
                        if waited.get(id(s), 0) < v:
                            eng.wait_ge(s, v)
                            waited[id(s)] = v
                            stats[eng_name][1] += 1
                    if op.fn is None:
                        continue
                    ins = op.fn(eng)
                    stats[eng_name][0] += 1
                    if op.sig is not None:
                        ins.then_inc(op.sig[0], 16 if (op.dma and op.cc is None) else 1)
            return body

        block.tensor(run("pe"))
        block.scalar(run("act"))
        block.vector(run("dve"))
        block.gpsimd(run("pool"))
        block.sync(run("sp"))
        return stats


class Arena:
    def __init__(self, ap_f32, nwords):
        self.ap = ap_f32
        self.n = nwords
        self.off = 0

    def mark(self):
        return self.off

    def release(self, m):
        self.off = m

    def alloc(self, shape, dt):
        ne = int(np.prod(shape))
        nw = ne if dt in (F32, I32) else (ne + 1) // 2
        nw = (nw + 7) // 8 * 8
        assert self.off + nw <= self.n, f"arena overflow {self.off}+{nw}>{self.n}"
        v = self.ap[:, self.off:self.off + nw]
        self.off += nw
        if dt != F32:
            v = v.bitcast(dt)
        v = v[:, 0:ne]
        if len(shape) == 2:
            v = v.rearrange("p (a b) -> p a b", a=shape[0])
        elif len(shape) == 3:
            v = v.rearrange("p (a b c) -> p a b c", a=shape[0], b=shape[1])
        return v


class Builder:
    def __init__(self, stage):
        self.stage = stage
        self.nc = bass.Bass("TRN2", target_bir_lowering=False)
        self.es = ExitStack()
        self.P = Prog(self.nc)
        self.din = {}
        self.dout = {}
        self.psrot = 0

    def inp(self, name, shape, dt=F32):
        t = self.nc.dram_tensor(name, list(shape), dt, kind="ExternalInput")
        self.din[name] = t
        return t

    def outp(self, name, shape, dt=F32):
        t = self.nc.dram_tensor(name, list(shape), dt, kind="ExternalOutput")
        self.dout[name] = t
        return t

    def sb(self, name, shape, dt):
        return self.es.enter_context(self.nc.sbuf_tensor(name, list(shape), dt))

    def dma(self, eng, out, in_, reads, writes):
        return self.P.add(eng, lambda e: e.dma_start(out=out, in_=in_), reads=reads, writes=writes, dma=True)

    def mmgroup(self, out, pairs, reads, writes, start=True, stop=True):
        def fn(e):
            ins = None
            n = len(pairs)
            for i, (l, r) in enumerate(pairs):
                ins = e.matmul(out, lhsT=l, rhs=r, start=(start and i == 0), stop=(stop and i == n - 1))
            return ins
        return self.P.add("pe", fn, reads=reads, writes=writes)

    def bank(self, lo=0, hi=8):
        b = lo + self.psrot % (hi - lo)
        self.psrot += 1
        return b

    def build(self):
        nc, P, es = self.nc, self.P, self.es
        xT = self.inp("xT", [D, T])
        gains = self.inp("gains", [128, 6 * 8])
        ident_d = self.inp("ident", [128, 128])
        cbf_d = self.inp("cbf", [128, 5 * 128])
        wfi = self.inp("wfi", [2 * 2 * 4 * 2 * 128, 4096])
        wfo = self.inp("wfo", [2 * 2 * 2 * 2 * 128, 4096])
        yT = self.outp("yT", [D, T])

        x = self.sb("x", [128, 8, T], F32)
        gn = self.sb("gn", [128, 48], F32)
        ident = self.sb("identf", [128, 128], F32)
        cbf = self.sb("cbfs", [128, 5, 128], BF16)
        cols = self.sb("cols", [128, 8], F32)
        NW = 3
        wsl = [self.sb(f"wsl{i}", [128, 8, 512], BF16) for i in range(NW)]
        AW = 24 * 1024
        arena_t = self.sb("arena", [128, AW], F32)
        A = Arena(arena_t[:, :], AW)
        ps = [es.enter_context(nc.psum_tensor(f"ps{i}", [128, 512], F32)) for i in range(8)]
        sems = {"eng": {e: es.enter_context(nc.semaphore("s_" + e)) for e in Prog.ENGS},
                "dma": {e: [es.enter_context(nc.semaphore(f"d_{e}{i}")) for i in range(n)]
                        for e, n in {"sp": 20, "pool": 12, "act": 4}.items()}}
        ones_bf = cbf[:, 0, :]
        eps_col = cols[:, 0:1]
        one_col = cols[:, 1:2]

        self.dma("sp", gn[:], gains.ap(), [], ["gn"])
        self.dma("sp", ident[:], ident_d.ap(), [], ["ident"])
        self.dma("pool", cbf[:], cbf_d.ap().rearrange("p (a b) -> p a b", a=5), [], ["cbf"])
        P.add("dve", lambda e: e.memset(cols[:, 0:1], EPS), writes=["cols"])
        P.add("dve", lambda e: e.memset(cols[:, 1:2], 1.0), reads=["cols"], writes=["cols"])
        for kc in range(8):
            self.dma("sp", x[:, kc, :], xT[kc * 128:(kc + 1) * 128, :], [], [("x", kc, ti) for ti in range(5)])

        wstate = {"i": 0}

        def load_w(src_ap):
            s = wstate["i"] % NW
            wstate["i"] += 1
            self.dma("pool", wsl[s][:], src_ap.rearrange("p (a b) -> p a b", a=8), [], [("w", s)])
            return wsl[s], ("w", s)

        def rmsnorm(n, xn):
            m = A.mark()
            sq = [A.alloc([512], BF16) for _ in range(3)]
            rt = A.alloc([512], F32)
            rinv = A.alloc([512], F32)
            for ti, (t0, nn) in enumerate(TT):
                P.force_sync = nn < 256
                b = self.bank(6, 8)
                for kc in range(8):
                    s = (ti * 8 + kc) % 3
                    P.add("act", lambda e, s=s, kc=kc, t0=t0, nn=nn: e.activation(out=sq[s][:, :nn], in_=x[:, kc, t0:t0 + nn], func=AF.Square),
                          reads=[("x", kc, ti)], writes=[("sq", s)])
                    self.mmgroup(ps[b][:, :nn], [(ones_bf, sq[s][:, :nn])], reads=[("sq", s), "cbf"], writes=[("ps", b)],
                                 start=(kc == 0), stop=(kc == 7))
                P.add("act", lambda e, b=b, nn=nn: e.activation(out=rt[:, :nn], in_=ps[b][:, :nn], func=AF.Sqrt, bias=eps_col, scale=1.0 / D),
                      reads=[("ps", b), "cols"], writes=["rt"])
                P.add("dve", lambda e, nn=nn: e.reciprocal(out=rinv[:, :nn], in_=rt[:, :nn]), reads=["rt"], writes=["rinv"])
                for kc in range(8):
                    eng = "dve"
                    P.add(eng, lambda e, kc=kc, t0=t0, nn=nn: e.scalar_tensor_tensor(
                        out=xn[:, kc, t0:t0 + nn], in0=x[:, kc, t0:t0 + nn], scalar=gn[:, n * 8 + kc:n * 8 + kc + 1], in1=rinv[:, :nn],
                        op0=ALU.mult, op1=ALU.mult), reads=[("x", kc, ti), "rinv", "gn"], writes=[("xn", kc, ti)])
            P.force_sync = False
            A.release(m)

        def ffn(l, f):
            m = A.mark()
            xn = A.alloc([8, T], BF16)
            h = A.alloc([8, T], BF16)
            sg = [A.alloc([512], F32) for _ in range(2)]
            rmsnorm(l * 3 + (0 if f == 0 else 2), xn)
            sgi = 0
            for hf in range(2):
                for fg2 in range(2):
                    fg = hf * 2 + fg2
                    base = (((l * 2 + f) * 4 + fg) * 2) * 128
                    wg, kg = load_w(wfi[base:base + 128, :])
                    wu, ku = load_w(wfi[base + 128:base + 256, :])
                    for j in range(4):
                        for ti, (t0, nn) in enumerate(TT):
                            P.force_sync = nn < 256
                            bg = self.bank(0, 4)
                            bu = self.bank(0, 4)
                            rk = [("xn", kc, ti) for kc in range(8)]
                            self.mmgroup(ps[bg][:, :nn], [(wg[:, kc, j * 128:(j + 1) * 128], xn[:, kc, t0:t0 + nn]) for kc in range(8)],
                                         reads=rk + [kg], writes=[("ps", bg)])
                            self.mmgroup(ps[bu][:, :nn], [(wu[:, kc, j * 128:(j + 1) * 128], xn[:, kc, t0:t0 + nn]) for kc in range(8)],
                                         reads=rk + [ku], writes=[("ps", bu)])
                            s = sgi % 2
                            sgi += 1
                            P.add("act", lambda e, s=s, bg=bg, nn=nn: e.activation(out=sg[s][:, :nn], in_=ps[bg][:, :nn], func=AF.Silu),
                                  reads=[("ps", bg)], writes=[("sg", s)])
                            fc = fg2 * 4 + j
                            P.add("dve", lambda e, s=s, bu=bu, fc=fc, t0=t0, nn=nn: e.tensor_tensor(
                                out=h[:, fc, t0:t0 + nn], in0=sg[s][:, :nn], in1=ps[bu][:, :nn], op=ALU.mult),
                                reads=[("sg", s), ("ps", bu)], writes=[("h", fc, ti)])
                for cg in range(2):
                    base = ((((l * 2 + f) * 2 + hf) * 2) + cg) * 128
                    wo, ko = load_w(wfo[base:base + 128, :])
                    for c4 in range(4):
                        c = cg * 4 + c4
                        for ti, (t0, nn) in enumerate(TT):
                            P.force_sync = nn < 256
                            b = self.bank(4, 6)
                            self.mmgroup(ps[b][:, :nn], [(wo[:, fc, c4 * 128:(c4 + 1) * 128], h[:, fc, t0:t0 + nn]) for fc in range(8)],
                                         reads=[("h", fc, ti) for fc in range(8)] + [ko], writes=[("ps", b)])
                            P.add("dve", lambda e, b=b, c=c, t0=t0, nn=nn: e.scalar_tensor_tensor(
                                out=x[:, c, t0:t0 + nn], in0=ps[b][:, :nn], scalar=0.5, in1=x[:, c, t0:t0 + nn], op0=ALU.mult, op1=ALU.add),
                                reads=[("ps", b), ("x", c, ti)], writes=[("x", c, ti)])
            P.force_sync = False
            A.release(m)


        PAIRS = [[0, 1], [2, 3], [4, 5], [6, 7]]
        w0own = self.inp("w0own", [3 * 128, 4096])
        w0full = self.inp("w0full", [5 * 128, 4096])
        wo0loc = self.inp("wo0loc", [2 * 128, 4096])
        wo0g = self.inp("wo0g", [2 * 128, 4096])
        gw_d = self.inp("gw", [128, 12 * 128])
        lc_d = self.inp("lc", [128, 6 * 8])
        qk_d = self.inp("qkcol", [128, 2])
        sbb_d = self.inp("sbb", [128, 12])
        sbrow_d = self.inp("sbrow", [128, 32])
        maskd_d = self.inp("maskd", [128, 4 * 512])
        maskn_d = self.inp("maskn", [128, 16 * 32])
        idxy_d = self.inp("idxy", [128, 8], I32)
        ptb_d = self.inp("ptb", [128, 256], I32)
        iota_d = self.inp("iota", [128, 1])
        lcs_d = self.inp("lcs", [128, 4 * 16 * 3])
        lhs_d = self.inp("lhs", [128, 4 * 16])
        ck_d = self.inp("ck", [POOLN * 128, 512])
        cv_d = self.inp("cv", [POOLN * 128, 512])
        kT_out = self.outp("kT_out", [256, 4096])
        v_out = self.outp("v_out", [4096, 256])
        lruc_out = self.outp("lruc_out", [256, 3])
        lruh_out = self.outp("lruh_out", [256, 1])
        ks_out = self.outp("ks_out", [512, 64])
        vs_out = self.outp("vs_out", [64, 512])
        lrucs_out = self.outp("lrucs_out", [512, 48])
        lruhs_out = self.outp("lruhs_out", [512, 16])
        xin0 = [nc.dram_tensor(f"xin0{h}", [512, 2048], BF16) for h in range(2)]
        xall0 = [nc.dram_tensor(f"xall0{h}", [1024, 2048], BF16) for h in range(2)]
        yloc0 = [nc.dram_tensor(f"yloc0{h}", [256, 4096], BF16) for h in range(2)]
        yall0 = [nc.dram_tensor(f"yall0{h}", [512, 4096], BF16) for h in range(2)]
        ccs = [es.enter_context(nc.semaphore(f"cc{i}")) for i in range(8)]

        def allgather(src, dst, reads, writes, sem):
            if "nocc" in DBG:
                n = src.shape[0]
                self.dma("sp", dst[0:n, :], src.ap(), reads, [("ccfake", writes[0])])
                self.dma("sp", dst[n:2 * n, :], src.ap(), reads + [("ccfake", writes[0])], writes)
            else:
                P.add("pool", lambda e: e.collective_compute("AllGather", ALU.bypass, replica_groups=PAIRS, ins=[src.ap().opt()], outs=[dst.ap().opt()]),
                      reads=reads, writes=writes, cc=sem)

        gw = self.sb("gws", [128, 12, 128], BF16)
        lc = self.sb("lcs_", [128, 6, 8], F32)
        lc2 = self.sb("lc2", [128, 6], F32)
        qkcol = self.sb("qkc", [128, 2], F32)
        sbb = self.sb("sbbs", [128, 12], F32)
        sbrow = self.sb("sbrows", [128, 32], F32)
        maskd = self.sb("maskds", [128, 4, 512], BF16)
        maskn = self.sb("masknS", [128, 16, 32], BF16)
        idxy = self.sb("idxys", [128, 8], I32)
        ptb = self.sb("ptbs", [128, 256], I32)
        pidx = self.sb("pidx", [128, 256], I32)
        iota = self.sb("iotas", [128, 1], F32)
        triS = cbf[:, 1, :]
        triC = cbf[:, 2, :]
        blk64 = cbf[:, 3, :]
        zero_bf = self.sb("zerobf", [128, 128], BF16)

        self.dma("pool", gw[:], gw_d.ap().rearrange("p (a b) -> p a b", a=12), [], ["gw"])
        self.dma("sp", lc[:], lc_d.ap().rearrange("p (a b) -> p a b", a=6), [], ["lc"])
        self.dma("sp", qkcol[:], qk_d.ap(), [], ["qkcol"])
        self.dma("sp", sbb[:], sbb_d.ap(), [], ["sbb"])
        self.dma("sp", sbrow[:], sbrow_d.ap(), [], ["sbrow"])
        self.dma("pool", maskd[:], maskd_d.ap().rearrange("p (a b) -> p a b", a=4), [], ["maskd"])
        self.dma("pool", maskn[:], maskn_d.ap().rearrange("p (a b) -> p a b", a=16), [], ["maskn"])
        self.dma("sp", idxy[:], idxy_d.ap(), [], ["idxy"])
        self.dma("sp", ptb[:], ptb_d.ap(), [], ["ptb"])
        self.dma("sp", iota[:], iota_d.ap(), [], ["iota"])
        P.add("dve", lambda e: e.tensor_scalar(out=pidx[:], in0=ptb[:], scalar1=128.0, scalar2=iota[:, 0:1], op0=ALU.mult, op1=ALU.add),
              reads=["ptb", "iota"], writes=["pidx"])
        P.add("pool", lambda e: e.memset(zero_bf[:], 0.0), writes=["zero_bf"])
        P.add("act", lambda e: e.activation(out=lc2[:], in_=lc[:, :, 7], func=AF.Exp, scale=-1.0), reads=["lc"], writes=["lc2"])
        P.add("act", lambda e: e.activation(out=lc2[:], in_=lc2[:], func=AF.Ln, bias=one_col, scale=1.0), reads=["lc2", "cols"], writes=["lc2"])
        P.add("dve", lambda e: e.tensor_scalar(out=lc[:, :, 7], in0=lc2[:], scalar1=-8.0, scalar2=None, op0=ALU.mult), reads=["lc2", "lc"], writes=["lc"])
        P.add("dve", lambda e: e.tensor_scalar(out=lc2[:], in0=lc2[:], scalar1=-16.0, scalar2=None, op0=ALU.mult), reads=["lc2"], writes=["lc2"])

        def act(out, in_, func, reads, writes, **kw):
            return P.add("act", lambda e: e.activation(out=out, in_=in_, func=func, **kw), reads=reads, writes=writes)

        def tt_(eng, out, in0, in1, op, reads, writes):
            return P.add(eng, lambda e: e.tensor_tensor(out=out, in0=in0, in1=in1, op=op), reads=reads, writes=writes)

        def ts_(eng, out, in0, s1, s2, op0, op1, reads, writes):
            if s2 is None:
                return P.add(eng, lambda e: e.tensor_scalar(out=out, in0=in0, scalar1=s1, scalar2=None, op0=op0), reads=reads, writes=writes)
            return P.add(eng, lambda e: e.tensor_scalar(out=out, in0=in0, scalar1=s1, scalar2=s2, op0=op0, op1=op1), reads=reads, writes=writes)

        def stt_(out, in0, scalar, in1, op0, op1, reads, writes):
            return P.add("dve", lambda e: e.scalar_tensor_tensor(out=out, in0=in0, scalar=scalar, in1=in1, op0=op0, op1=op1), reads=reads, writes=writes)

        def cp_(eng, out, in_, reads, writes):
            if eng == "act":
                return act(out, in_, AF.Copy, reads, writes)
            return P.add(eng, lambda e: e.tensor_copy(out=out, in_=in_), reads=reads, writes=writes)

        uid = {"n": 0}

        def U(name):
            uid["n"] += 1
            return (name, uid["n"])

        def headnorm(src_ps, kps, gcol, out_ap, kout, nn, tmp):
            sq, rt, rinv = tmp
            b = 7
            act(sq[:, :nn], src_ps, AF.Square, [kps], ["hn_sq"])
            self.mmgroup(ps[b][:, :nn], [(blk64, sq[:, :nn])], reads=["hn_sq", "cbf"], writes=[("ps", b)])
            act(rt[:, :nn], ps[b][:, :nn], AF.Sqrt, [("ps", b), "cols"], ["hn_rt"], bias=eps_col, scale=1.0)
            P.add("dve", lambda e: e.reciprocal(out=rinv[:, :nn], in_=rt[:, :nn]), reads=["hn_rt"], writes=["hn_rinv"])
            stt_(out_ap, src_ps, gcol, rinv[:, :nn], ALU.mult, ALU.mult, [kps, "hn_rinv", "qkcol"], [kout])

        def lru_core(ch, nn, xc, xg_ps, kxg, tmp, hinit, hout, rec_out, krec, scan3d=None):
            xcb, r_, i_, a_, a2, gx, bb, t1, t2, sgm = tmp
            wa = gw[:, (0 if ch < 2 else 2) + ch, :]
            wi = gw[:, (2 if ch < 2 else 6) + ch, :]
            cp_("act", xcb[:, :nn], xc, ["xc"], ["xcb"])
            ba, bi = self.bank(4, 6), self.bank(4, 6)
            self.mmgroup(ps[ba][:, :nn], [(wa, xcb[:, :nn])], reads=["xcb", "gw"], writes=[("ps", ba)])
            self.mmgroup(ps[bi][:, :nn], [(wi, xcb[:, :nn])], reads=["xcb", "gw"], writes=[("ps", bi)])
            act(r_[:, :nn], ps[ba][:, :nn], AF.Sigmoid, [("ps", ba), "lc"], ["lr"], bias=lc[:, ch, 5:6], scale=1.0)
            act(i_[:, :nn], ps[bi][:, :nn], AF.Sigmoid, [("ps", bi), "lc"], ["li"], bias=lc[:, ch, 6:7], scale=1.0)
            act(a_[:, :nn], r_[:, :nn], AF.Exp, ["lr", "lc"], ["la"], scale=lc[:, ch, 7:8])
            act(a2[:, :nn], r_[:, :nn], AF.Exp, ["lr", "lc2"], ["la2"], scale=lc2[:, ch:ch + 1])
            act(a2[:, :nn], a2[:, :nn], AF.Sqrt, ["la2", "cols"], ["la2"], bias=one_col, scale=-1.0)
            tt_("dve", gx[:, :nn], i_[:, :nn], xc, ALU.mult, ["li", "xc"], ["lgx"])
            tt_("dve", bb[:, :nn], a2[:, :nn], gx[:, :nn], ALU.mult, ["la2", "lgx"], ["lbb"])
            if scan3d is None:
                P.add("dve", lambda e: e.tensor_tensor_scan(out=hout, data0=a_[:, :nn], data1=bb[:, :nn], initial=hinit, op0=ALU.mult, op1=ALU.add),
                      reads=["la", "lbb", "lh_prev"], writes=["lh"])
            else:
                nb, hs0 = scan3d
                a3 = a_[:, :nn].rearrange("p (b t) -> p b t", t=4)
                b3 = bb[:, :nn].rearrange("p (b t) -> p b t", t=4)
                h3 = hout.rearrange("p (b t) -> p b t", t=4)
                for t in range(4):
                    prev = hs0 if t == 0 else h3[:, :, t - 1]
                    tt_("dve", h3[:, :, t], a3[:, :, t], prev, ALU.mult, ["la", "lh", "hst"], ["lh"])
                    tt_("dve", h3[:, :, t], h3[:, :, t], b3[:, :, t], ALU.add, ["lh", "lbb"], ["lh"])
            act(t1[:, :nn], xg_ps, AF.Square, [kxg], ["lt1"])
            ts_("dve", t1[:, :nn], t1[:, :nn], 0.044715, 1.0, ALU.mult, ALU.add, ["lt1"], ["lt1"])
            tt_("dve", t2[:, :nn], t1[:, :nn], xg_ps, ALU.mult, ["lt1", kxg], ["lt2"])
            act(sgm[:, :nn], t2[:, :nn], AF.Sigmoid, ["lt2"], ["lsg"], scale=1.5957691216057308)
            tt_("dve", t2[:, :nn], sgm[:, :nn], xg_ps, ALU.mult, ["lsg", kxg, "lt2"], ["lt2"])
            tt_("dve", rec_out, hout, t2[:, :nn], ALU.mult, ["lh", "lt2"], [krec])

        def conv4(out, src, ch, reads, writes, three_d=False):
            def sl(k):
                return src[:, :, k:k + 4] if three_d else src[:, k:k + 512]
            ts_("dve", out, sl(0), lc[:, ch, 0:1], lc[:, ch, 4:5], ALU.mult, ALU.add, reads + ["lc"], writes)
            for k in range(1, 4):
                stt_(out, sl(k), lc[:, ch, k:k + 1], out, ALU.mult, ALU.add, reads + ["lc"] + writes, writes)

        def mixer0():
            m0 = A.mark()
            xs_pad = A.alloc([8, 128], BF16)
            ymix_s = A.alloc([8, 64], BF16)
            mx = A.mark()
            xn = A.alloc([8, T], BF16)
            rmsnorm(1, xn)
            for kc in range(8):
                self.dma("sp", xin0[kc // 4][(kc % 4) * 128:(kc % 4 + 1) * 128, :], xn[:, kc, 0:TP], [("xn", kc, ti) for ti in range(4)], [("xin0", kc)])
            for h in range(2):
                allgather(xin0[h], xall0[h], [("xin0", kc) for kc in range(4 * h, 4 * h + 4)], [("xall0", h)], ccs[h])
            P.add("pool", lambda e: e.memset(xs_pad[:, :, :], 0.0), writes=["xs_pad"])
            P.add("pool", lambda e: e.tensor_copy(out=xs_pad[:, :, 0:64], in_=xn[:, :, TP:T]), reads=[("xn", kc, 4) for kc in range(8)] + ["xs_pad"], writes=["xs_pad"])
            P.add("pool", lambda e: e.memset(ymix_s[:, :, :], 0.0), writes=[("ymix", j, 4) for j in range(8)])
            P.barrier()
            A.release(mx)
            m1 = A.mark()
            xf = [A.alloc([8, 512], BF16) for _ in range(2)]
            xfi = {"i": 0}

            def load_xf(tt):
                s = xfi["i"] % 2
                xfi["i"] += 1
                rho, c0 = tt // 4, (tt % 4) * 512
                for h in range(2):
                    self.dma("sp", xf[s][:, 4 * h:4 * h + 4, :], xall0[h][rho * 512:(rho + 1) * 512, c0:c0 + 512].rearrange("(kc p) t -> p kc t", p=128),
                             [("xall0", h)], [("xf", s, h)])
                return xf[s], ("xf", s)

            m2 = A.mark()
            xrh = [A.alloc([515], F32) for _ in range(2)]
            xc = A.alloc([512], F32)
            tmp = [A.alloc([512], BF16)] + [A.alloc([512], F32) for _ in range(9)]
            hb = [[A.alloc([512], F32) for _ in range(2)] for _ in range(2)]
            rec = [A.alloc([512], BF16) for _ in range(2)]
            for ch in range(2):
                P.add("pool", lambda e, ch=ch: e.memset(xrh[ch][:, 0:3], 0.0), writes=[("xrh", ch)])
            wl, kwl = load_w(w0own[256:384, :])
            for tt in range(8):
                xft, kxf = load_xf(tt)
                for ch in range(2):
                    br, bg = self.bank(0, 4), self.bank(0, 4)
                    self.mmgroup(ps[br][:, :], [(wl[:, kc, ch * 128:(ch + 1) * 128], xft[:, kc, :]) for kc in range(8)], reads=[kxf + (0,), kxf + (1,), kwl], writes=[("ps", br)])
                    self.mmgroup(ps[bg][:, :], [(wl[:, kc, 256 + ch * 128:256 + (ch + 1) * 128], xft[:, kc, :]) for kc in range(8)], reads=[kxf + (0,), kxf + (1,), kwl], writes=[("ps", bg)])
                    cp_("act", xrh[ch][:, 3:515], ps[br][:, :], [("ps", br)], [("xrh", ch)])
                    conv4(xc[:, :], xrh[ch], ch, [("xrh", ch)], ["xc"])
                    if tt == 7:
                        self.dma("sp", lruc_out[ch * 128:(ch + 1) * 128, :], xrh[ch][:, 512:515], [("xrh", ch)], [("lruc_out", ch)])
                    cp_("pool", xrh[ch][:, 0:3], xrh[ch][:, 512:515], ["xc", ("xrh", ch)], [("xrh", ch)])
                    hcur, hprev = hb[ch][tt % 2], hb[ch][(tt + 1) % 2]
                    hinit = 0.0 if tt == 0 else hprev[:, 511:512]
                    lru_core(ch, 512, xc[:, :], ps[bg][:, :], ("ps", bg), tmp, hinit, hcur[:, :], rec[ch][:, :], ("rec", ch))
                    self.dma("sp", yloc0[1][ch * 128:(ch + 1) * 128, tt * 512:(tt + 1) * 512], rec[ch][:, :], [("rec", ch)], [("yloc0", 2 + ch, tt)])
                    if tt == 7:
                        self.dma("sp", lruh_out[ch * 128:(ch + 1) * 128, :], hcur[:, 511:512], ["lh"], [("lruh_out", ch)])
            P.barrier()
            A.release(m2)

            m3 = A.mark()
            qT = A.alloc([4096], BF16)
            kpad = [A.alloc([4096], BF16) for _ in range(2)]
            vpad = [A.alloc([32, 128], BF16) for _ in range(2)]
            hn_tmp = (A.alloc([512], BF16), A.alloc([512], F32), A.alloc([512], F32))
            kst = A.alloc([512], F32)
            vst = A.alloc([4, 128], F32)
            ebuf = [[A.alloc([512], F32) for _ in range(2)] for _ in range(2)]
            spb = [[A.alloc([512], BF16) for _ in range(2)] for _ in range(2)]
            tbuf = [[A.alloc([512], F32) for _ in range(2)] for _ in range(2)]
            wbuf = [[A.alloc([512], BF16) for _ in range(2)] for _ in range(2)]
            yat = [A.alloc([512], BF16) for _ in range(2)]
            for X in range(2):
                P.add("pool", lambda e, X=X: e.memset(kpad[X][:, :], 0.0), writes=[("kpad", X)])
                P.add("pool", lambda e, X=X: e.memset(vpad[X][:, :, :], 0.0), writes=[("vpad", X)])
            for lp in range(2):
                wp, kwp = load_w(w0own[lp * 128:(lp + 1) * 128, :])
                for tt in range(8):
                    xft, kxf = load_xf(tt)
                    c0 = tt * 512
                    bq, bk, bv = 0, 1, 2
                    self.mmgroup(ps[bq][:, :], [(wp[:, kc, 0:128], xft[:, kc, :]) for kc in range(8)], reads=[kxf + (0,), kxf + (1,), kwp], writes=[("ps", bq)])
                    self.mmgroup(ps[bk][:, :], [(wp[:, kc, 128:256], xft[:, kc, :]) for kc in range(8)], reads=[kxf + (0,), kxf + (1,), kwp], writes=[("ps", bk)])
                    for blk in range(4):
                        self.mmgroup(ps[bv][:, blk * 128:(blk + 1) * 128], [(xft[:, kc, blk * 128:(blk + 1) * 128], wp[:, kc, 256:384]) for kc in range(8)],
                                     reads=[kxf + (0,), kxf + (1,), kwp], writes=[("ps", bv)])
                    headnorm(ps[bq][:, :], ("ps", bq), qkcol[:, 0:1], qT[:, c0:c0 + 512], "qT", 512, hn_tmp)
                    headnorm(ps[bk][:, :], ("ps", bk), qkcol[:, 1:2], kst[:, :], "kst", 512, hn_tmp)
                    self.dma("sp", kT_out[lp * 128:(lp + 1) * 128, c0:c0 + 512], kst[:, :], ["kst"], [("kT_out", lp, tt)])
                    cp_("act", kpad[0][0:64, c0:c0 + 512], kst[0:64, :], ["kst"], [("kpad", 0)])
                    cp_("act", kpad[1][64:128, c0:c0 + 512], kst[64:128, :], ["kst"], [("kpad", 1)])
                    v3 = ps[bv][:, :].rearrange("p (a b) -> p a b", a=4)
                    cp_("act", vst[:, :, :], v3, [("ps", bv)], ["vst"])
                    self.dma("sp", v_out[c0:c0 + 512, lp * 128:(lp + 1) * 128].rearrange("(a p) f -> p a f", p=128), vst[:, :, :], ["vst"], [("v_out", lp, tt)])
                    cp_("dve", vpad[0][:, tt * 4:(tt + 1) * 4, 0:64], v3[:, :, 0:64], [("ps", bv)], [("vpad", 0)])
                    cp_("dve", vpad[1][:, tt * 4:(tt + 1) * 4, 64:128], v3[:, :, 64:128], [("ps", bv)], [("vpad", 1)])
                steps = [(qs, kb) for qs in range(8) for kb in range(4 * qs + 3, -1, -1)]
                ZB = [[0, 1], [2, 3]]
                RB = [4, 5]
                OB = 6

                def zmm(i):
                    qs, kb = steps[i]
                    for X in range(2):
                        b = ZB[X][i % 2]
                        self.mmgroup(ps[b][:, :], [(kpad[X][:, kb * 128:(kb + 1) * 128], qT[:, qs * 512:(qs + 1) * 512])],
                                     reads=[("kpad", X), "qT"], writes=[("ps", b)])
                zmm(0)
                for i, (qs, kb) in enumerate(steps):
                    first = (kb == 4 * qs + 3)
                    last = (kb == 0)
                    diag = kb >= 4 * qs
                    mi = kb - 4 * qs
                    if i + 1 < len(steps):
                        zmm(i + 1)
                    for X in range(2):
                        zb = ZB[X][i % 2]
                        bias = sbb[:, lp * 2 + X:lp * 2 + X + 1]
                        e_, sp_, t_, w_ = ebuf[X][i % 2], spb[X][i % 2], tbuf[X][i % 2], wbuf[X][i % 2]
                        ke, ksp, kt, kw = ("e", X, i % 2), ("sp", X, i % 2), ("t", X, i % 2), ("w", X, i % 2)
                        act(e_[:, :], ps[zb][:, :], AF.Exp, [("ps", zb), "sbb"], [ke], bias=bias, scale=0.125)
                        act(sp_[:, :], e_[:, :], AF.Ln, [ke, "cols"], [ksp], bias=one_col, scale=1.0)
                        if diag:
                            tt_("pool", sp_[:, :], sp_[:, :], maskd[:, mi, :], ALU.mult, [ksp, "maskd"], [ksp])
                        self.mmgroup(ps[RB[X]][:, :], [(triS, sp_[:, :])], reads=[ksp, "cbf"], writes=[("ps", RB[X])], start=first, stop=False)
                        stt_(t_[:, :], ps[zb][:, :], 0.125, sp_[:, :], ALU.mult, ALU.subtract, [("ps", zb), ksp], [kt])
                        tt_("dve", t_[:, :], t_[:, :], ps[RB[X]][:, :], ALU.subtract, [kt, ("ps", RB[X])], [kt])
                        act(w_[:, :], t_[:, :], AF.Exp, [kt, "sbb"], [kw], bias=bias, scale=1.0)
                        if diag:
                            tt_("pool", w_[:, :], w_[:, :], maskd[:, mi, :], ALU.mult, [kw, "maskd"], [kw])
                        self.mmgroup(ps[RB[X]][:, :], [(triC, sp_[:, :])], reads=[ksp, "cbf", ("ps", RB[X])], writes=[("ps", RB[X])], start=False, stop=last)
                        self.mmgroup(ps[OB][:, :], [(vpad[X][:, kb, :], w_[:, :])], reads=[kw, ("vpad", X)], writes=[("ps", OB)],
                                     start=(first and X == 0), stop=(last and X == 1))
                    if last:
                        ya = yat[qs % 2]
                        cp_("act", ya[:, :], ps[OB][:, :], [("ps", OB)], [("yat", qs % 2)])
                        self.dma("sp", yloc0[0][lp * 128:(lp + 1) * 128, qs * 512:(qs + 1) * 512], ya[:, :], [("yat", qs % 2)], [("yloc0", lp, qs)])
            P.barrier()
            A.release(m3)
            A.release(m1)
            if "nosample" not in DBG:
                P.force_sync = True
                ms = A.mark()
                qpadS = A.alloc([8, 64], BF16)
                knew = A.alloc([8, 128], BF16)
                vnew = A.alloc([512], BF16)
                hn_s = (A.alloc([64], BF16), A.alloc([64], F32), A.alloc([64], F32))
                qst = A.alloc([64], F32)
                ksts = A.alloc([64], F32)
                vsts = A.alloc([512], F32)
                xrhs = A.alloc([16, 7], F32)
                xcs = A.alloc([64], F32)
                tmps = [A.alloc([64], BF16)] + [A.alloc([64], F32) for _ in range(9)]
                hs_ = A.alloc([64], F32)
                hst = A.alloc([4, 16], F32)
                hls = A.alloc([16], F32)
                kpg = [A.alloc([512], F32) for _ in range(2)]
                vpg = [A.alloc([512], BF16) for _ in range(2)]
                KT = [A.alloc([4, 128], BF16) for _ in range(2)]
                zs_ = [A.alloc([32], F32) for _ in range(2)]
                es_ = [A.alloc([32], F32) for _ in range(2)]
                sps = [A.alloc([32], F32) for _ in range(2)]
                spm = [A.alloc([32], BF16) for _ in range(2)]
                ts2 = [A.alloc([32], F32) for _ in range(2)]
                ws_ = [A.alloc([32], BF16) for _ in range(2)]
                P.add("pool", lambda e: e.memset(qpadS[:, :, :], 0.0), writes=["qpadS"])
                P.add("pool", lambda e: e.memset(knew[:, :, :], 0.0), writes=["knew"])
                self.dma("sp", hst[:, :, :], lhs_d.ap().rearrange("p (a b) -> p a b", a=4), [], ["hst"])
                wq, kwq = load_w(w0full[0:128, :])
                for c in range(4):
                    b_ = self.bank(0, 4)
                    self.mmgroup(ps[b_][:, :64], [(wq[:, kc, c * 128:(c + 1) * 128], xs_pad[:, kc, 0:64]) for kc in range(8)], reads=["xs_pad", kwq], writes=[("ps", b_)])
                    headnorm(ps[b_][:, :64], ("ps", b_), qkcol[:, 0:1], qst[:, :], "qst", 64, hn_s)
                    cp_("act", qpadS[0:64, 2 * c, :], qst[0:64, :], ["qst"], ["qpadS"])
                    cp_("act", qpadS[64:128, 2 * c + 1, :], qst[64:128, :], ["qst", "qpadS"], ["qpadS"])
                wk, kwk = load_w(w0full[128:256, :])
                for c in range(4):
                    b_ = self.bank(0, 4)
                    self.mmgroup(ps[b_][:, :64], [(wk[:, kc, c * 128:(c + 1) * 128], xs_pad[:, kc, 0:64]) for kc in range(8)], reads=["xs_pad", kwk], writes=[("ps", b_)])
                    headnorm(ps[b_][:, :64], ("ps", b_), qkcol[:, 1:2], ksts[:, :], "ksts", 64, hn_s)
                    self.dma("sp", ks_out[c * 128:(c + 1) * 128, :], ksts[:, :], ["ksts"], [("ks_out", c)])
                    cp_("act", knew[0:64, 2 * c, 0:64], ksts[0:64, :], ["ksts"], ["knew"])
                    cp_("act", knew[64:128, 2 * c + 1, 0:64], ksts[64:128, :], ["ksts", "knew"], ["knew"])
                wv, kwv = load_w(w0full[256:384, :])
                b_ = self.bank(0, 4)
                self.mmgroup(ps[b_][:, :], [(xs_pad[:, kc, :], wv[:, kc, :]) for kc in range(8)], reads=["xs_pad", kwv], writes=[("ps", b_)])
                cp_("act", vsts[:, :], ps[b_][:, :], [("ps", b_)], ["vsts"])
                cp_("dve", vnew[:, :], ps[b_][:, :], [("ps", b_)], ["vnew"])
                self.dma("sp", vs_out.ap(), vsts[0:64, :], ["vsts"], ["vs_out"])
                wxr, kwxr = load_w(w0full[384:512, :])
                wxg, kwxg = load_w(w0full[512:640, :])
                xrh3 = xrhs
                xc3 = xcs[:, :].rearrange("p (b t) -> p b t", t=4)
                for ch in range(4):
                    br, bg = self.bank(0, 4), self.bank(0, 4)
                    self.mmgroup(ps[br][:, :64], [(wxr[:, kc, ch * 128:(ch + 1) * 128], xs_pad[:, kc, 0:64]) for kc in range(8)], reads=["xs_pad", kwxr], writes=[("ps", br)])
                    self.mmgroup(ps[bg][:, :64], [(wxg[:, kc, ch * 128:(ch + 1) * 128], xs_pad[:, kc, 0:64]) for kc in range(8)], reads=["xs_pad", kwxg], writes=[("ps", bg)])
                    self.dma("sp", xrh3[:, :, 0:3], lcs_d[:, ch * 48:(ch + 1) * 48].rearrange("p (b k) -> p b k", k=3), [], [("xrhs", 0)])
                    cp_("act", xrh3[:, :, 3:7], ps[br][:, :64].rearrange("p (b t) -> p b t", t=4), [("ps", br)], [("xrhs", 1)])
                    conv4(xc3, xrh3, 2 + ch, [("xrhs", 0), ("xrhs", 1)], ["xc"], three_d=True)
                    self.dma("sp", lrucs_out[ch * 128:(ch + 1) * 128, :].rearrange("p (b k) -> p b k", k=3), xrh3[:, :, 4:7], [("xrhs", 0), ("xrhs", 1)], [("lrucs_out", ch)])
                    lru_core(2 + ch, 64, xcs[:, :], ps[bg][:, :64], ("ps", bg), tmps, None, hs_[:, :], ymix_s[:, 4 + ch, :], ("ymix", 4 + ch, 4),
                             scan3d=(16, hst[:, ch, :]))
                    cp_("dve", hls[:, :], hs_[:, :].rearrange("p (b t) -> p b t", t=4)[:, :, 3], ["lh"], ["hls"])
                    self.dma("sp", lruhs_out[ch * 128:(ch + 1) * 128, :], hls[:, :], ["hls"], [("lruhs_out", ch)])
                ZBs, RBs, OBs, KBs = [0, 1], 2, 3, [4, 5]
                gi = {"n": 0}

                def gather(b, j):
                    s_ = gi["n"] % 2
                    gi["n"] += 1
                    col = b * 16 + j
                    P.add("pool", lambda e: e.indirect_dma_start(out=kpg[s_][:, :], out_offset=None, in_=ck_d.ap(),
                                                                 in_offset=bass.IndirectOffsetOnAxis(ap=pidx[:, col:col + 1], axis=0)),
                          reads=["pidx"], writes=[("kpg", s_)], dma=True)
                    P.add("pool", lambda e: e.indirect_dma_start(out=vpg[s_][:, :], out_offset=None, in_=cv_d.ap(),
                                                                 in_offset=bass.IndirectOffsetOnAxis(ap=pidx[:, col:col + 1], axis=0)),
                          reads=["pidx"], writes=[("vpg", s_)], dma=True)
                    return s_
                seq_steps = [(b, st) for b in range(NS) for st in range(17)]
                pend = {}
                for idx_, (b, st) in enumerate(seq_steps):
                    nxt = idx_ + 1
                    if idx_ == 0:
                        pend[(0, 1)] = gather(0, 15)
                    if nxt < len(seq_steps):
                        nb, nst = seq_steps[nxt]
                        if nst >= 1 and (nb, nst) not in pend:
                            pend[(nb, nst)] = gather(nb, 16 - nst)
                    i2 = idx_ % 2
                    if st == 0:
                        self.mmgroup(ps[OBs][:, 0:128], [(zero_bf[:, :], cbf[:, 4, :])], reads=["zero_bf", "cbf"], writes=[("ps", OBs)], start=True, stop=False)
                        vblk, kvb = vnew, "vnew"
                    else:
                        s_ = pend.pop((b, st))
                        kb_ = KBs[s_]
                        for c in range(4):
                            P.add("pe", lambda e, c=c, kb_=kb_, s_=s_: e.transpose(ps[kb_][:, c * 128:(c + 1) * 128], kpg[s_][:, c * 128:(c + 1) * 128], ident[:]),
                                  reads=[("kpg", s_), "ident"], writes=[("ps", kb_)])
                        cp_("act", KT[s_][:, :, :], ps[kb_][:, :].rearrange("p (a b) -> p a b", a=4), [("ps", kb_)], [("KT", s_)])
                        vblk, kvb = vpg[s_], ("vpg", s_)
                    zb = ZBs[i2]

                    def zfn(e, st=st, b=b, zb=zb, s_=(None if st == 0 else s_)):
                        ins = None
                        for h in range(8):
                            l = knew[:, h, :] if st == 0 else KT[s_][:, h // 2, :]
                            ins = e.matmul(ps[zb][:, h * 4:(h + 1) * 4], lhsT=l, rhs=qpadS[:, h, b * 4:(b + 1) * 4], start=True, stop=True)
                        return ins
                    P.add("pe", zfn, reads=["qpadS", "knew" if st == 0 else ("KT", s_)], writes=[("ps", zb)])
                    z_, e_, sp_, sm_, t_, w_ = zs_[i2], es_[i2], sps[i2], spm[i2], ts2[i2], ws_[i2]
                    kz, ke, ksp, ksm, kt, kw = ("szs", i2), ("ses", i2), ("ssp", i2), ("ssm", i2), ("sts", i2), ("sws", i2)
                    stt_(z_[:, :], ps[zb][:, 0:32], 0.125, sbrow[:, :], ALU.mult, ALU.add, [("ps", zb), "sbrow"], [kz])
                    act(e_[:, :], z_[:, :], AF.Exp, [kz], [ke])
                    if st == 0:
                        act(sp_[:, :], e_[:, :], AF.Ln, [ke, "cols"], [ksp], bias=one_col, scale=1.0)
                        tt_("dve", sm_[:, :], sp_[:, :], maskn[:, b, :], ALU.mult, [ksp, "maskn"], [ksm])
                    else:
                        act(sm_[:, :], e_[:, :], AF.Ln, [ke, "cols"], [ksm], bias=one_col, scale=1.0)
                    self.mmgroup(ps[RBs][:, 0:32], [(triS, sm_[:, :])], reads=[ksm, "cbf"], writes=[("ps", RBs)], start=(st == 0), stop=False)
                    tt_("dve", t_[:, :], z_[:, :], sm_[:, :], ALU.subtract, [kz, ksm], [kt])
                    tt_("dve", t_[:, :], t_[:, :], ps[RBs][:, 0:32], ALU.subtract, [kt, ("ps", RBs)], [kt])
                    act(w_[:, :], t_[:, :], AF.Exp, [kt], [kw])
                    if st == 0:
                        tt_("dve", w_[:, :], w_[:, :], maskn[:, b, :], ALU.mult, [kw, "maskn"], [kw])
                    self.mmgroup(ps[RBs][:, 0:32], [(triC, sm_[:, :])], reads=[ksm, "cbf"], writes=[("ps", RBs)], start=False, stop=(st == 16))

                    def pvfn(e, vblk=vblk, w_=w_, st=st):
                        ins = None
                        for c in range(4):
                            ins = e.matmul(ps[OBs][:, c * 32:(c + 1) * 32], lhsT=vblk[:, c * 128:(c + 1) * 128], rhs=w_[:, :], start=False, stop=(st == 16))
                        return ins
                    P.add("pe", pvfn, reads=[kw, kvb], writes=[("ps", OBs)])
                    if st == 16:
                        for c in range(4):
                            cp_("act", ymix_s[0:64, c, b * 4:(b + 1) * 4], ps[OBs][0:64, c * 32 + 8 * c:c * 32 + 8 * c + 4], [("ps", OBs)], [("ymix", c, 4)])
                            cp_("act", ymix_s[64:128, c, b * 4:(b + 1) * 4], ps[OBs][64:128, c * 32 + 8 * c + 4:c * 32 + 8 * c + 8], [("ps", OBs), ("ymix", c, 4)], [("ymix", c, 4)])
                P.barrier()
                P.force_sync = False
                A.release(ms)
            ymix = A.alloc([8, TP], BF16)

            for h in range(2):
                allgather(yloc0[h], yall0[h], [("yloc0", 2 * h + a, b) for a in range(2) for b in range(8)], [("yall0", h)], ccs[2 + h])
            for j in range(8):
                h = (j % 4) // 2
                yv = yall0[h].ap().rearrange("r (h t) -> (r h) t", h=2)
                P.add("pool", lambda e, j=j, yv=yv: e.indirect_dma_start(out=ymix[:, j, 0:TP], out_offset=None, in_=yv,
                                                                        in_offset=bass.IndirectOffsetOnAxis(ap=idxy[:, j:j + 1], axis=0)),
                      reads=[("yall0", h), "idxy"], writes=[("ymix", j, ti) for ti in range(4)], dma=True)
            if "dbgy" in DBG:
                dbg = self.outp("dbg", [1024, 2048])
                for j in range(8):
                    self.dma("pool", dbg[j * 128:(j + 1) * 128, :], ymix[:, j, :], [("ymix", j, ti) for ti in range(4)], [("dbg", j)])
            for grp, (wsrc, tis) in enumerate(((wo0loc, range(4)), (wo0g, range(4, 5)))):
                for cg in range(2):
                    wo, ko = load_w(wsrc[cg * 128:(cg + 1) * 128, :])
                    for c4 in range(4):
                        c = cg * 4 + c4
                        for ti in tis:
                            t0, nn = TT[ti]
                            P.force_sync = nn < 256
                            b = self.bank(4, 6)
                            srcs = [(ymix[:, j, t0:t0 + nn] if ti < 4 else ymix_s[:, j, :]) for j in range(8)]
                            self.mmgroup(ps[b][:, :nn], [(wo[:, j, c4 * 128:(c4 + 1) * 128], srcs[j]) for j in range(8)],
                                         reads=[("ymix", j, ti) for j in range(8)] + [ko], writes=[("ps", b)])
                            tt_("dve", x[:, c, t0:t0 + nn], ps[b][:, :nn], x[:, c, t0:t0 + nn], ALU.add, [("ps", b), ("x", c, ti)], [("x", c, ti)])
            P.force_sync = False
            P.barrier()
            A.release(m0)


        w1own = self.inp("w1own", [5 * 128, 4096])
        w1full = self.inp("w1full", [9 * 128, 4096])
        wo1loc = self.inp("wo1loc", [2 * 128, 4096])
        wo1g = self.inp("wo1g", [2 * 128, 4096])
        dcw_d = self.inp("dcw", [128, 36 * 4])
        bac_d = self.inp("bac", [128, 4])
        ogc_d = self.inp("ogc", [128, 1])
        dnk_d = self.inp("dnk", [128, (2 + 12) * 128 + 512 + 64])
        dcs_d = self.inp("dcs", [128, 24 * 48])
        dS_d = self.inp("dS", [NS * 8 * 128, 128])
        dnc_out = self.outp("dnc_out", [1536, 3])
        dnS_out = self.outp("dnS_out", [512, 128])
        dncs_out = self.outp("dncs_out", [3072, 48])
        dnSs_out = self.outp("dnSs_out", [NS * 8 * 128, 128])
        xin1 = [nc.dram_tensor(f"xin1{h}", [512, 2048], BF16) for h in range(2)]
        xall1 = [nc.dram_tensor(f"xall1{h}", [1024, 2048], BF16) for h in range(2)]
        yloc1 = [nc.dram_tensor(f"yloc1{h}", [256, 4096], BF16) for h in range(2)]
        yall1 = [nc.dram_tensor(f"yall1{h}", [512, 4096], BF16) for h in range(2)]
        dcw = self.sb("dcws", [128, 36, 4], F32)
        bac = self.sb("bacs", [128, 4], F32)
        nega = self.sb("negas", [128, 2], F32)
        ogc = self.sb("ogcs", [128, 1], F32)
        DK = {}

        def selrow(k):
            return DK["dnk"][:, (2 + k) * 128:(3 + k) * 128]
        self.dma("sp", dcw[:], dcw_d.ap().rearrange("p (a b) -> p a b", a=36), [], ["dcw"])
        self.dma("sp", bac[:], bac_d.ap(), [], ["bac"])
        self.dma("sp", ogc[:], ogc_d.ap(), [], ["ogc"])
        P.add("act", lambda e: e.activation(out=nega[:, 0:1], in_=bac[:, 1:2], func=AF.Exp), reads=["bac"], writes=["nega"])
        P.add("act", lambda e: e.activation(out=nega[:, 1:2], in_=bac[:, 3:4], func=AF.Exp), reads=["bac", "nega"], writes=["nega"])
        P.add("dve", lambda e: e.tensor_scalar(out=nega[:], in0=nega[:], scalar1=-1.0, scalar2=None, op0=ALU.mult), reads=["nega"], writes=["nega"])

        def tr_(out_ps, in_sb, reads, writes):
            return self.mmgroup(out_ps, [(in_sb, ident[:])], reads=reads + ["ident"], writes=writes)

        def mm1(out, lhsT, rhs, reads, writes, start=True, stop=True):
            return self.mmgroup(out, [(lhsT, rhs)], reads=reads, writes=writes, start=start, stop=stop)

        class DNU:
            def __init__(self, i):
                self.i = i
                self.t = {n: A.alloc([128], F32) for n in ("d1", "d2", "M", "MT", "IM", "TT", "PT", "kg", "vb", "V", "U", "O1", "O", "sq", "On")}
                self.c = A.alloc([16], F32)

        def dn_unit(W, qTc, kTc, vTc, kq, fld_beta, fld_G, kfld, Grhs, kG, sel, S, kS, nlev, yout, kyout, zsil, kz):
            t, c = W.t, W.c
            B = lambda: self.bank(0, 7)
            STOP = int(os.environ.get("UT_STOP", "99"))
            bG = B()
            mm1(ps[bG][:, 0:128], sel, Grhs, [kG, "dnk"], [("ps", bG)])
            if STOP <= 1:
                return
            ts_("dve", t["d1"][:, :], ps[bG][:, 0:128], fld_G, 0.0, ALU.subtract, ALU.max, [("ps", bG), kfld], [("u_d1", W.i)])
            ts_("dve", t["d2"][:, :], ps[bG][:, 0:128], fld_G, 0.0, ALU.subtract, ALU.min, [("ps", bG), kfld], [("u_d2", W.i)])
            ts_("dve", c[:, 0:1], ps[bG][:, 127:128], fld_G, None, ALU.subtract, None, [("ps", bG), kfld], [("u_c0", W.i)])
            act(c[:, 1:2], c[:, 0:1], AF.Exp, [("u_c0", W.i)], [("u_c1", W.i)])
            act(c[:, 2:3], ps[bG][:, 127:128], AF.Exp, [("ps", bG)], [("u_c2", W.i)])
            act(c[:, 3:4], fld_G, AF.Exp, [kfld], [("u_c3", W.i)])
            ts_("dve", c[:, 4:5], fld_beta, -1.0, None, ALU.mult, None, [kfld], [("u_c4", W.i)])
            tt_("dve", c[:, 5:6], c[:, 4:5], c[:, 3:4], ALU.mult, [("u_c4", W.i), ("u_c3", W.i)], [("u_c5", W.i)])
            act(t["d1"][:, :], t["d1"][:, :], AF.Exp, [("u_d1", W.i)], [("u_d1", W.i)], scale=-1.0)
            act(t["d2"][:, :], t["d2"][:, :], AF.Exp, [("u_d2", W.i)], [("u_d2", W.i)])
            tt_("pool", t["d1"][:, :], t["d1"][:, :], DK["dnk"][:, 0:128], ALU.mult, [("u_d1", W.i), "dnk"], [("u_d1", W.i)])
            tt_("pool", t["d2"][:, :], t["d2"][:, :], DK["dnk"][:, 128:256], ALU.mult, [("u_d2", W.i), "dnk"], [("u_d2", W.i)])
            if STOP <= 2:
                return
            bKK, bKQ = B(), B()
            mm1(ps[bKK][:, 0:128], kTc, kTc, [kq], [("ps", bKK)])
            mm1(ps[bKQ][:, 0:128], kTc, qTc, [kq], [("ps", bKQ)])
            stt_(t["M"][:, :], ps[bKK][:, 0:128], c[:, 4:5], t["d1"][:, :], ALU.mult, ALU.mult, [("ps", bKK), ("u_c4", W.i), ("u_d1", W.i)], [("u_M", W.i)])
            tt_("dve", t["PT"][:, :], ps[bKQ][:, 0:128], t["d2"][:, :], ALU.mult, [("ps", bKQ), ("u_d2", W.i)], [("u_PT", W.i)])
            if STOP <= 3:
                return
            b0 = B()
            tr_(ps[b0][:, 0:128], t["M"][:, :], [("u_M", W.i)], [("ps", b0)])
            if int(os.environ.get("UT_SUB", "99")) == -1:
                return
            cp_("act", t["MT"][:, :], ps[b0][:, 0:128], [("ps", b0)], [("u_MT", W.i)])
            if int(os.environ.get("UT_SUB", "99")) == -2:
                return
            tt_("dve", t["TT"][:, :], ps[b0][:, 0:128], ident[:], ALU.add, [("ps", b0), "ident"], [("u_TT", W.i)])
            SUB = int(os.environ.get("UT_SUB", "99"))
            if SUB <= 0:
                return
            for k in range(1, min(nlev, SUB) + 1):
                bm = B()
                mm1(ps[bm][:, 0:128], t["MT"][:, :], t["M"][:, :], [("u_MT", W.i), ("u_M", W.i)], [("ps", bm)])
                if k < nlev:
                    bt_ = B()
                    mm1(ps[bt_][:, 0:128], t["M"][:, :], t["MT"][:, :], [("u_MT", W.i), ("u_M", W.i)], [("ps", bt_)])
                tt_("dve", t["IM"][:, :], ps[bm][:, 0:128], ident[:], ALU.add, [("ps", bm), "ident"], [("u_IM", W.i)])
                if k < nlev:
                    cp_("act", t["M"][:, :], ps[bm][:, 0:128], [("ps", bm)], [("u_M", W.i)])
                    cp_("act", t["MT"][:, :], ps[bt_][:, 0:128], [("ps", bt_)], [("u_MT", W.i)])
                bx = B()
                mm1(ps[bx][:, 0:128], t["IM"][:, :], t["TT"][:, :], [("u_IM", W.i), ("u_TT", W.i)], [("ps", bx)])
                cp_("act", t["TT"][:, :], ps[bx][:, 0:128], [("ps", bx)], [("u_TT", W.i)])
            if STOP <= 4:
                return
            bk_, bv_ = B(), B()
            tr_(ps[bk_][:, 0:128], kTc, [kq], [("ps", bk_)])
            act(t["kg"][:, :], ps[bk_][:, 0:128], AF.Copy, [("ps", bk_), ("u_c1", W.i)], [("u_kg", W.i)], scale=c[:, 1:2])
            tr_(ps[bv_][:, 0:128], vTc, [kq], [("ps", bv_)])
            act(t["vb"][:, :], ps[bv_][:, 0:128], AF.Copy, [("ps", bv_), kfld], [("u_vb", W.i)], scale=fld_beta)
            if STOP <= 5:
                return
            b1 = B()
            mm1(ps[b1][:, 0:128], kTc, S, [kq, kS], [("ps", b1)])
            stt_(t["V"][:, :], ps[b1][:, 0:128], c[:, 5:6], t["vb"][:, :], ALU.mult, ALU.add, [("ps", b1), ("u_c5", W.i), ("u_vb", W.i)], [("u_V", W.i)])
            b2 = B()
            mm1(ps[b2][:, 0:128], t["TT"][:, :], t["V"][:, :], [("u_TT", W.i), ("u_V", W.i)], [("ps", b2)])
            cp_("act", t["U"][:, :], ps[b2][:, 0:128], [("ps", b2)], [("u_U", W.i)])
            b3, b4, b5 = B(), B(), B()
            mm1(ps[b3][:, 0:128], qTc, S, [kq, kS], [("ps", b3)])
            mm1(ps[b4][:, 0:128], t["PT"][:, :], t["U"][:, :], [("u_PT", W.i), ("u_U", W.i)], [("ps", b4)])
            mm1(ps[b5][:, 0:128], t["kg"][:, :], t["U"][:, :], [("u_kg", W.i), ("u_U", W.i)], [("ps", b5)])
            act(t["O1"][:, :], ps[b3][:, 0:128], AF.Copy, [("ps", b3), ("u_c3", W.i)], [("u_O1", W.i)], scale=c[:, 3:4])
            tt_("dve", t["O"][:, :], t["O1"][:, :], ps[b4][:, 0:128], ALU.add, [("u_O1", W.i), ("ps", b4)], [("u_O", W.i)])
            stt_(S, S, c[:, 2:3], ps[b5][:, 0:128], ALU.mult, ALU.add, [kS, ("u_c2", W.i), ("ps", b5)], [kS])
            if STOP <= 6:
                return
            act(t["sq"][:, :], t["O"][:, :], AF.Square, [("u_O", W.i)], [("u_sq", W.i)])
            P.add("dve", lambda e: e.reduce_sum(out=c[:, 6:7], in_=t["sq"][:, :], axis=mybir.AxisListType.X), reads=[("u_sq", W.i)], writes=[("u_c6", W.i)])
            act(c[:, 7:8], c[:, 6:7], AF.Sqrt, [("u_c6", W.i), "cols"], [("u_c7", W.i)], bias=eps_col, scale=1.0 / 128.0)
            P.add("dve", lambda e: e.reciprocal(out=c[:, 8:9], in_=c[:, 7:8]), reads=[("u_c7", W.i)], writes=[("u_c8", W.i)])
            act(t["On"][:, :], t["O"][:, :], AF.Copy, [("u_O", W.i), ("u_c8", W.i)], [("u_On", W.i)], scale=c[:, 8:9])
            b6 = B()
            tr_(ps[b6][:, 0:128], t["On"][:, :], [("u_On", W.i)], [("ps", b6)])
            stt_(yout, ps[b6][:, 0:128], ogc[:, 0:1], zsil, ALU.mult, ALU.mult, [("ps", b6), "ogc", kz], [kyout])


        def conv_dn(out, src, ci, reads, writes, three_d=False):
            def sl(k):
                return src[:, :, k:k + 4] if three_d else src[:, k:k + 512]
            ts_("dve", out, sl(0), dcw[:, ci, 0:1], None, ALU.mult, None, reads + ["dcw"], writes)
            for k in range(1, 4):
                stt_(out, sl(k), dcw[:, ci, k:k + 1], out, ALU.mult, ALU.add, reads + ["dcw"] + writes, writes)

        def l2n(buf, kbuf, nn, scale, tmp):
            sq, rt, rinv = tmp
            act(sq[:, :nn], buf, AF.Square, [kbuf], ["l2_sq"])
            self.mmgroup(ps[7][:, :nn], [(ones_bf, sq[:, :nn])], reads=["l2_sq", "cbf"], writes=[("ps", 7)])
            act(rt[:, :nn], ps[7][:, :nn], AF.Sqrt, [("ps", 7), "cols"], ["l2_rt"], bias=eps_col, scale=1.0)
            P.add("dve", lambda e: e.reciprocal(out=rinv[:, :nn], in_=rt[:, :nn]), reads=["l2_rt"], writes=["l2_ri"])
            stt_(buf, buf, scale, rinv[:, :nn], ALU.mult, ALU.mult, [kbuf, "l2_ri"], [kbuf])

        def ba_fields(ba_ps, kps, nn, bt, gt, e1, Gt, col, rm):
            act(bt[:, :nn], ba_ps, AF.Sigmoid, [kps], ["bt"])
            act(e1[:, :nn], ba_ps, AF.Exp, [kps, "bac"], ["e1"], bias=bac[:, 2 * col:2 * col + 1], scale=1.0)
            act(e1[:, :nn], e1[:, :nn], AF.Ln, ["e1", "cols"], ["e1"], bias=one_col, scale=1.0)
            ts_("dve", gt[:, :nn], e1[:, :nn], nega[:, col:col + 1], None, ALU.mult, None, ["e1", "nega"], ["gt"])
            P.add("dve", lambda e: e.tensor_tensor_scan(out=Gt[:, :nn], data0=rm, data1=gt[:, :nn], initial=0.0, op0=ALU.mult, op1=ALU.add),
                  reads=["gt", "dnk"], writes=["Gt"])

        def mixer1():
            m0 = A.mark()
            xs_pad = A.alloc([8, 128], BF16)
            ymix_s = A.alloc([8, 64], BF16)
            dnk = A.alloc([14 * 128 + 576], F32)
            DK["dnk"] = dnk
            rmask = dnk[:, 14 * 128:14 * 128 + 512]
            rmask_s = dnk[:, 14 * 128 + 512:14 * 128 + 576]
            self.dma("sp", dnk[:, :], dnk_d.ap(), [], ["dnk"])
            mx = A.mark()
            xn = A.alloc([8, T], BF16)
            rmsnorm(4, xn)
            for kc in range(8):
                self.dma("sp", xin1[kc // 4][(kc % 4) * 128:(kc % 4 + 1) * 128, :], xn[:, kc, 0:TP], [("xn", kc, ti) for ti in range(4)], [("xin1", kc)])
            for h in range(2):
                allgather(xin1[h], xall1[h], [("xin1", kc) for kc in range(4 * h, 4 * h + 4)], [("xall1", h)], ccs[4 + h])
            P.add("pool", lambda e: e.memset(xs_pad[:, :, :], 0.0), writes=["xs_pad"])
            P.add("pool", lambda e: e.tensor_copy(out=xs_pad[:, :, 0:64], in_=xn[:, :, TP:T]), reads=[("xn", kc, 4) for kc in range(8)] + ["xs_pad"], writes=["xs_pad"])
            P.add("pool", lambda e: e.memset(ymix_s[:, :, :], 0.0), writes=[("ymix", j, 4) for j in range(8)])
            P.barrier()
            A.release(mx)
            m1 = A.mark()
            hn = (A.alloc([512], BF16), A.alloc([512], F32), A.alloc([512], F32))
            bt, gt, e1, Gt = [A.alloc([512], F32) for _ in range(4)]
            W = [DNU(0), DNU(1)]
            if "noprompt1" not in DBG:
                m2 = A.mark()
                xf = A.alloc([8, 512], BF16)
                stage = [A.alloc([515], F32) for _ in range(2)]
                halo = A.alloc([12, 4], F32)
                qkv = [A.alloc([512], F32) for _ in range(12)]
                sz = [A.alloc([512], F32) for _ in range(4)]
                fld = A.alloc([4, 16], F32)
                Sst = [A.alloc([128], F32) for _ in range(4)]
                yt = [A.alloc([512], BF16) for _ in range(4)]
                P.add("pool", lambda e: e.memset(halo[:, :, :], 0.0), writes=["halo"])
                for hh in range(4):
                    P.add("pool", lambda e, hh=hh: e.memset(Sst[hh][:, :], 0.0), writes=[("S", hh)])
                sti = 0
                for tt in range(8):
                    rho, c0 = tt // 4, (tt % 4) * 512
                    for h in range(2):
                        self.dma("sp", xf[:, 4 * h:4 * h + 4, :], xall1[h][rho * 512:(rho + 1) * 512, c0:c0 + 512].rearrange("(kc p) t -> p kc t", p=128),
                                 [("xall1", h)], [("xf1", h)])
                    kxf = [("xf1", 0), ("xf1", 1)]
                    for grp in range(3):
                        wt, kwt = load_w(w1own[grp * 128:(grp + 1) * 128, :])
                        for hh in range(4):
                            ci = grp * 4 + hh
                            b_ = self.bank(0, 7)
                            self.mmgroup(ps[b_][:, :], [(wt[:, kc, hh * 128:(hh + 1) * 128], xf[:, kc, :]) for kc in range(8)], reads=kxf + [kwt], writes=[("ps", b_)])
                            st_ = stage[sti % 2]
                            kst_ = ("stage", sti % 2)
                            sti += 1
                            cp_("pool", st_[:, 0:3], halo[:, ci, 0:3], ["halo"], [kst_])
                            cp_("act", st_[:, 3:515], ps[b_][:, :], [("ps", b_), kst_], [kst_])
                            conv_dn(qkv[ci][:, :], st_, ci, [kst_], [("qkv", ci)])
                            cp_("pool", halo[:, ci, 0:3], st_[:, 512:515], [kst_, "halo"], ["halo"])
                            if tt == 7:
                                self.dma("sp", dnc_out[ci * 128:(ci + 1) * 128, :], st_[:, 512:515], [kst_], [("dnc_out", ci)])
                            act(qkv[ci][:, :], qkv[ci][:, :], AF.Silu, [("qkv", ci)], [("qkv", ci)])
                            if grp < 2:
                                l2n(qkv[ci][:, :], ("qkv", ci), 512, (128.0 ** -0.5) if grp == 0 else 1.0, hn)
                    wt, kwt = load_w(w1own[384:512, :])
                    for hh in range(4):
                        b_ = self.bank(0, 7)
                        self.mmgroup(ps[b_][:, :], [(wt[:, kc, hh * 128:(hh + 1) * 128], xf[:, kc, :]) for kc in range(8)], reads=kxf + [kwt], writes=[("ps", b_)])
                        act(sz[hh][:, :], ps[b_][:, :], AF.Silu, [("ps", b_)], [("sz", hh)])
                    wt, kwt = load_w(w1own[512:640, :])
                    b_ = self.bank(0, 7)
                    self.mmgroup(ps[b_][:, :], [(wt[:, kc, 0:128], xf[:, kc, :]) for kc in range(8)], reads=kxf + [kwt], writes=[("ps", b_)])
                    ba_fields(ps[b_][:, :], ("ps", b_), 512, bt, gt, e1, Gt, 0, rmask)
                    b_ = self.bank(0, 7)
                    for n in range(4):
                        mm1(ps[b_][:, n * 16:n * 16 + 8], bt[:, n * 128:(n + 1) * 128], ident[:, 0:8], ["bt", "ident"], [("ps", b_)])
                        mm1(ps[b_][:, n * 16 + 8:n * 16 + 16], Gt[:, n * 128:(n + 1) * 128], ident[:, 0:8], ["Gt", "ident"], [("ps", b_)])
                    cp_("act", fld[:, :, :], ps[b_][:, 0:64].rearrange("p (a b) -> p a b", a=4), [("ps", b_)], ["fld"])
                    u = 0
                    for n in range(4):
                        for hh in range(4):
                            cs = slice(n * 128, (n + 1) * 128)
                            dn_unit(W[u % 2], qkv[hh][:, cs], qkv[4 + hh][:, cs], qkv[8 + hh][:, cs], ("qkv", hh), fld[:, n, hh:hh + 1], fld[:, n, 12 + hh:13 + hh], "fld",
                                    Gt[:, cs], "Gt", selrow(hh), Sst[hh][:, :], ("S", hh), 6, yt[hh][:, cs], ("yt", hh), sz[hh][:, cs], ("sz", hh))
                            u += 1
                    for hh in range(4):
                        self.dma("sp", yloc1[hh // 2][(hh % 2) * 128:(hh % 2 + 1) * 128, tt * 512:(tt + 1) * 512], yt[hh][:, :], [("yt", hh)], [("yloc1", hh, tt)])
                for hh in range(4):
                    self.dma("sp", dnS_out[hh * 128:(hh + 1) * 128, :], Sst[hh][:, :], [("S", hh)], [("dnS_out", hh)])
                P.barrier()
                A.release(m2)
            if "nosample1" not in DBG:
                P.force_sync = True
                m3 = A.mark()
                cq = A.alloc([24, 64], F32)
                szs = A.alloc([8, 64], F32)
                st3 = [A.alloc([16, 7], F32) for _ in range(2)]
                qp, kp, vp, zp = [A.alloc([8, 128], F32) for _ in range(4)]
                ysp = A.alloc([8, 128], BF16)
                btp = A.alloc([128], F32)
                Gtp = A.alloc([128], F32)
                flds = A.alloc([32], F32)
                Ss = [A.alloc([128], F32) for _ in range(8)]
                W = W + [DNU(2), DNU(3)]
                for bufz in (qp, kp, vp, zp):
                    P.add("pool", lambda e, bufz=bufz: e.memset(bufz[:, :, :], 0.0), writes=["qkvp"])
                P.add("pool", lambda e: e.memset(btp[:, :], 0.0), writes=["btp"])
                sti = 0
                for grp in range(3):
                    for half in range(2):
                        wt, kwt = load_w(w1full[(grp * 2 + half) * 128:(grp * 2 + half + 1) * 128, :])
                        for h4 in range(4):
                            gc = grp * 8 + half * 4 + h4
                            b_ = self.bank(0, 7)
                            self.mmgroup(ps[b_][:, :64], [(wt[:, kc, h4 * 128:(h4 + 1) * 128], xs_pad[:, kc, 0:64]) for kc in range(8)], reads=["xs_pad", kwt], writes=[("ps", b_)])
                            st_ = st3[sti % 2]
                            kst_ = ("st3", sti % 2)
                            sti += 1
                            self.dma("sp", st_[:, :, 0:3], dcs_d[:, gc * 48:(gc + 1) * 48].rearrange("p (b k) -> p b k", k=3), [], [kst_ + (0,)])
                            cp_("act", st_[:, :, 3:7], ps[b_][:, :64].rearrange("p (b t) -> p b t", t=4), [("ps", b_)], [kst_ + (1,)])
                            conv_dn(cq[:, gc, :].rearrange("p (b t) -> p b t", t=4), st_, 12 + gc, [kst_ + (0,), kst_ + (1,)], [("cq", gc)], three_d=True)
                            self.dma("sp", dncs_out[gc * 128:(gc + 1) * 128, :].rearrange("p (b k) -> p b k", k=3), st_[:, :, 4:7], [kst_ + (0,), kst_ + (1,)], [("dncs_out", gc)])
                            act(cq[:, gc, :], cq[:, gc, :], AF.Silu, [("cq", gc)], [("cq", gc)])
                            if grp < 2:
                                l2n(cq[:, gc, :], ("cq", gc), 64, (128.0 ** -0.5) if grp == 0 else 1.0, hn)
                for half in range(2):
                    wt, kwt = load_w(w1full[(6 + half) * 128:(7 + half) * 128, :])
                    for h4 in range(4):
                        h = half * 4 + h4
                        b_ = self.bank(0, 7)
                        self.mmgroup(ps[b_][:, :64], [(wt[:, kc, h4 * 128:(h4 + 1) * 128], xs_pad[:, kc, 0:64]) for kc in range(8)], reads=["xs_pad", kwt], writes=[("ps", b_)])
                        act(szs[:, h, :], ps[b_][:, :64], AF.Silu, [("ps", b_)], ["szs"])
                wt, kwt = load_w(w1full[8 * 128:9 * 128, :])
                b_ = self.bank(0, 7)
                self.mmgroup(ps[b_][:, :64], [(wt[:, kc, 0:128], xs_pad[:, kc, 0:64]) for kc in range(8)], reads=["xs_pad", kwt], writes=[("ps", b_)])
                ba_fields(ps[b_][:, :64], ("ps", b_), 64, bt, gt, e1, Gt, 1, rmask_s)
                for b in range(NS):
                    cs = slice(4 * b, 4 * b + 4)
                    rk = [("cq", gc) for gc in range(24)]
                    cp_("dve", qp[:, :, 0:4], cq[:, 0:8, cs], rk + ["qkvp"], ["qkvp"])
                    cp_("dve", kp[:, :, 0:4], cq[:, 8:16, cs], rk + ["qkvp"], ["qkvp"])
                    cp_("pool", vp[:, :, 0:4], cq[:, 16:24, cs], rk + ["qkvp"], ["qkvp"])
                    cp_("pool", zp[:, :, 0:4], szs[:, :, cs], ["szs", "qkvp"], ["qkvp"])
                    cp_("pool", btp[:, 0:4], bt[:, cs], ["bt", "btp"], ["btp"])
                    cp_("pool", Gtp[:, 0:4], Gt[:, cs], ["Gt", "Gtp"], ["Gtp"])
                    cp_("pool", Gtp[:, 4:128], Gt[:, 4 * b + 3:4 * b + 4].to_broadcast([128, 124]), ["Gt", "Gtp"], ["Gtp"])
                    b_ = self.bank(0, 7)
                    mm1(ps[b_][:, 0:16], btp[:, :], ident[:, 0:16], ["btp", "ident"], [("ps", b_)])
                    mm1(ps[b_][:, 16:32], Gtp[:, :], ident[:, 0:16], ["Gtp", "ident"], [("ps", b_)])
                    cp_("act", flds[:, :], ps[b_][:, 0:32], [("ps", b_)], ["flds"])
                    for h in range(8):
                        self.dma("sp", Ss[h][:, :], dS_d[(b * 8 + h) * 128:(b * 8 + h + 1) * 128, :], [], [("Ss", h)])
                        dn_unit(W[h % 4], qp[:, h, :], kp[:, h, :], vp[:, h, :], "qkvp", flds[:, h:h + 1], flds[:, 24 + h:25 + h], "flds",
                                Gtp[:, :], "Gtp", selrow(4 + h), Ss[h][:, :], ("Ss", h), 1, ysp[:, h, :], "ysp", zp[:, h, :], "qkvp")
                        self.dma("sp", dnSs_out[(b * 8 + h) * 128:(b * 8 + h + 1) * 128, :], Ss[h][:, :], [("Ss", h)], [("dnSs_out", b, h)])
                    cp_("dve", ymix_s[:, :, cs], ysp[:, :, 0:4], ["ysp"], [("ymix", j, 4) for j in range(8)])
                P.barrier()
                P.force_sync = False
                A.release(m3)
            P.barrier()
            A.release(m1)
            ymix = A.alloc([8, TP], BF16)
            for h in range(2):
                allgather(yloc1[h], yall1[h], [("yloc1", 2 * h + a, b) for a in range(2) for b in range(8)], [("yall1", h)], ccs[6 + h])
            for j in range(8):
                h = (j % 4) // 2
                yv = yall1[h].ap().rearrange("r (h t) -> (r h) t", h=2)
                P.add("pool", lambda e, j=j, yv=yv: e.indirect_dma_start(out=ymix[:, j, 0:TP], out_offset=None, in_=yv,
                                                                        in_offset=bass.IndirectOffsetOnAxis(ap=idxy[:, j:j + 1], axis=0)),
                      reads=[("yall1", h), "idxy"], writes=[("ymix", j, ti) for ti in range(4)], dma=True)
            for grp, (wsrc, tis) in enumerate(((wo1loc, range(4)), (wo1g, range(4, 5)))):
                for cg in range(2):
                    wo, ko = load_w(wsrc[cg * 128:(cg + 1) * 128, :])
                    for c4 in range(4):
                        c = cg * 4 + c4
                        for ti in tis:
                            t0, nn = TT[ti]
                            P.force_sync = nn < 256
                            b = self.bank(4, 6)
                            srcs = [(ymix[:, j, t0:t0 + nn] if ti < 4 else ymix_s[:, j, :]) for j in range(8)]
                            self.mmgroup(ps[b][:, :nn], [(wo[:, j, c4 * 128:(c4 + 1) * 128], srcs[j]) for j in range(8)],
                                         reads=[("ymix", j, ti) for j in range(8)] + [ko], writes=[("ps", b)])
                            tt_("dve", x[:, c, t0:t0 + nn], ps[b][:, :nn], x[:, c, t0:t0 + nn], ALU.add, [("ps", b), ("x", c, ti)], [("x", c, ti)])
            P.force_sync = False
            P.barrier()
            A.release(m0)

        def finish():
            for kc in range(8):
                self.dma("sp", yT[kc * 128:(kc + 1) * 128, :], x[:, kc, :], [("x", kc, ti) for ti in range(5)], [("yT", kc)])
            P.finalize_dma_wait()
            with nc.Block() as block:
                st = P.emit(block, sems)
            self.stats = st

        if "unittest" in DBG:
            ut_in = self.inp("ut_in", [128, 7 * 128])
            ut_out = self.outp("ut_out", [128, 3 * 128])
            dnk = A.alloc([14 * 128 + 576], F32)
            DK["dnk"] = dnk
            self.dma("sp", dnk[:, :], dnk_d.ap(), [], ["dnk"])
            ui = A.alloc([7, 128], F32)
            uo = A.alloc([3, 128], F32)
            yb = A.alloc([128], BF16)
            fld = A.alloc([16], F32)
            self.dma("sp", ui[:, :, :], ut_in.ap().rearrange("p (a b) -> p a b", a=7), [], ["ui"])
            W0 = DNU(0)
            b_ = 7
            mm1(ps[b_][:, 0:8], ui[:, 4, :], ident[:, 0:8], ["ui", "ident"], [("ps", b_)])
            mm1(ps[b_][:, 8:16], ui[:, 5, :], ident[:, 0:8], ["ui", "ident"], [("ps", b_)])
            cp_("act", fld[:, :], ps[b_][:, 0:16], [("ps", b_)], ["fld"])
            nl = int(os.environ.get("UT_NLEV", "6"))
            dn_unit(W0, ui[:, 0, :], ui[:, 1, :], ui[:, 2, :], "ui", fld[:, 1:2], fld[:, 12 + 1:12 + 2], "fld", ui[:, 5, :], "ui", selrow(1),
                    ui[:, 3, :], "ui", nl, yb[:, :], "yb", ui[:, 6, :], "ui")
            cp_("dve", uo[:, 0, :], yb[:, :], ["yb"], ["uo"])
            cp_("dve", uo[:, 1, :], ui[:, 3, :], ["ui", "uo"], ["uo"])
            cp_("dve", uo[:, 2, :], W0.t["TT"][:, :], [("u_TT", 0), "uo"], ["uo"])
            self.dma("sp", ut_out.ap().rearrange("p (a b) -> p a b", a=3), uo[:, :, :], ["uo"], ["ut_out"])
            P.finalize_dma_wait()
            with nc.Block() as block:
                self.stats = P.emit(block, sems)
            return
        P.force_sync = False
        ffn(0, 0)
        if self.stage <= 1:
            return finish()
        P.barrier()
        mixer0()
        if self.stage <= 2:
            return finish()
        P.barrier()
        ffn(0, 1)
        if self.stage <= 3:
            return finish()
        P.barrier()
        ffn(1, 0)
        if self.stage <= 4:
            return finish()
        P.barrier()
        mixer1()
        if self.stage <= 5:
            return finish()
        P.barrier()
        ffn(1, 1)
        return finish()


_CACHE = {}


def _consts():
    ident = np.eye(128, dtype=np.float32)
    j = np.arange(128)[:, None]
    s = np.arange(128)[None, :]
    ones = np.ones((128, 128), np.float32)
    triS = (j > s).astype(np.float32)
    triC = (j <= s).astype(np.float32)
    blk = ((j // 64) == (s // 64)).astype(np.float32) / 64.0
    cbf = np.concatenate([ones, triS, triC, blk, ident], axis=1)
    return ident, cbf


def _host_prep(inputs):
    ident, cbf = _consts()
    gl = []
    for l in range(2):
        for nm in ("norm_ffn1", "norm_mix", "norm_ffn2"):
            gl.append(np.asarray(inputs[nm][l], np.float32).reshape(8, 128).T)
    gains = np.ascontiguousarray(np.concatenate(gl, axis=1))
    wfi = np.empty((2, 2, 4, 2, 128, 8, 512), np.float32)
    wfo = np.empty((2, 2, 2, 2, 128, 8, 512), np.float32)
    for l in range(2):
        for f, (ni, no) in enumerate((("w_ffn1_in", "w_ffn1_out"), ("w_ffn2_in", "w_ffn2_out"))):
            wi = np.asarray(inputs[ni][l], np.float32).reshape(8, 128, 2, 4, 512)
            wfi[l, f] = wi.transpose(3, 2, 1, 0, 4)
            wo = np.asarray(inputs[no][l], np.float32).reshape(2, 8, 128, 2, 512)
            wfo[l, f] = wo.transpose(0, 3, 2, 1, 4)
    common = {"gains": gains, "ident": ident, "cbf": cbf,
              "wfi": wfi.reshape(-1, 4096), "wfo": wfo.reshape(-1, 4096)}
    f32 = lambda a: np.asarray(a, np.float32)

    def wtile(w):
        return np.ascontiguousarray(w.reshape(8, 128, 512).transpose(1, 0, 2)).reshape(128, 4096)

    def wotiles(w):
        return np.ascontiguousarray(w.reshape(8, 128, 2, 512).transpose(2, 1, 0, 3)).reshape(256, 4096)

    def blockdiag(w8, gc):
        o = np.zeros((128, 128), np.float32)
        o[0:64, 0:64] = w8[2 * gc]
        o[64:128, 64:128] = w8[2 * gc + 1]
        return o

    We = f32(inputs["w_in_even"][0])
    Woe = f32(inputs["w_out_even"][0])
    wa8, wi8 = f32(inputs["lru_w_a"][0]), f32(inputs["lru_w_i"][0])
    cw, cb = f32(inputs["lru_conv_w"][0]), f32(inputs["lru_conv_b"][0])
    b_a, b_i, lam = f32(inputs["lru_b_a"][0]), f32(inputs["lru_b_i"][0]), f32(inputs["lru_lambda"][0])
    qg, kg = f32(inputs["sb_q_gain"][0]), f32(inputs["sb_k_gain"][0])
    sbias = f32(inputs["sb_bias"][0])

    def lccols(gc):
        sl = slice(gc * 128, (gc + 1) * 128)
        return np.stack([cw[0, sl], cw[1, sl], cw[2, sl], cw[3, sl], cb[sl], b_a[sl], b_i[sl], lam[sl]], axis=1)

    pp = np.arange(128)
    maskd = np.concatenate([(np.arange(512)[None, :] > (pp[:, None] + 128 * m)).astype(np.float32) for m in range(4)], axis=1)
    maskn = np.zeros((128, 16, 8, 4), np.float32)
    for b in range(16):
        for t in range(4):
            maskn[4 * b + t, b, :, :] = (t < np.arange(4))[None, :]
    common.update({
        "w0full": np.concatenate([wtile(We[:, i * 512:(i + 1) * 512]) for i in range(5)], axis=0),
        "wo0g": wotiles(Woe), "qkcol": np.stack([np.tile(qg, 2), np.tile(kg, 2)], axis=1),
        "sbrow": np.ascontiguousarray(np.broadcast_to(np.repeat(sbias, 4)[None, :], (128, 32))),
        "maskd": maskd, "maskn": maskn.reshape(128, 512),
        "iota": pp.astype(np.float32)[:, None].copy(),
        "ck": f32(inputs["cache_k"][0][:POOLN]).reshape(POOLN * 128, 512), "cv": f32(inputs["cache_v"][0][:POOLN]).reshape(POOLN * 128, 512),
    })
    Wd = f32(inputs["w_in_odd"][0])
    Wod = f32(inputs["w_out_odd"][0])
    dconv = f32(inputs["dn_conv_w"][0])
    Alog, dtb = f32(inputs["dn_A_log"][0]), f32(inputs["dn_dt_bias"][0])
    bag = np.zeros((1024, 512), np.float32)
    bag[:, 0:8] = Wd[:, 4096:4104]
    bag[:, 8:16] = Wd[:, 4104:4112]
    w1full = [wtile(Wd[:, i * 512:(i + 1) * 512]) for i in range(8)] + [wtile(bag)]
    jj = np.arange(128)[:, None]
    ff = np.arange(128)[None, :]
    dnk = [(jj > ff).astype(np.float32), (ff >= jj).astype(np.float32)]
    for k in range(12):
        sel = np.zeros((128, 128), np.float32)
        sel[4 + k, :] = 1.0
        dnk.append(sel)
    rm = np.ones((128, 512), np.float32)
    rm[:, ::128] = 0.0
    rms = np.ones((128, 64), np.float32)
    rms[:, ::4] = 0.0
    dnk += [rm, rms]
    common.update({"w1full": np.concatenate(w1full, axis=0), "wo1g": wotiles(Wod), "ogc": f32(inputs["dn_o_gain"][0])[:, None].copy(),
                   "dnk": np.concatenate(dnk, axis=1)})
    dcs_all = f32(inputs["state_dn_conv"][0])
    dS_all = f32(inputs["state_dn_S"][0])
    pt = np.asarray(inputs["page_table"], np.int32)
    lcs_all = f32(inputs["state_lru_conv"][0])
    lhs_all = f32(inputs["state_lru_h"][0])
    percore = []
    for c in range(NCORES):
        r = c % 2
        own = [2 * r, 2 * r + 1]
        oth = [2 * (1 - r), 2 * (1 - r) + 1]
        tiles = []
        for lp in range(2):
            gc = own[lp]
            tiles.append(wtile(np.concatenate([We[:, gc * 128:(gc + 1) * 128], We[:, 512 + gc * 128:512 + (gc + 1) * 128],
                                               We[:, 1024 + gc * 128:1024 + (gc + 1) * 128], np.zeros((1024, 128), np.float32)], axis=1)))
        tiles.append(wtile(np.concatenate([We[:, 1536 + own[0] * 128:1536 + own[0] * 128 + 128], We[:, 1536 + own[1] * 128:1536 + own[1] * 128 + 128],
                                           We[:, 2048 + own[0] * 128:2048 + own[0] * 128 + 128], We[:, 2048 + own[1] * 128:2048 + own[1] * 128 + 128]], axis=1)))
        rows = []
        for grp in (own, oth):
            rows += [Woe[g * 128:(g + 1) * 128] for g in grp] + [Woe[512 + g * 128:512 + (g + 1) * 128] for g in grp]
        gwl = [blockdiag(wa8, g) for g in own] + [blockdiag(wi8, g) for g in own] + [blockdiag(wa8, g) for g in range(4)] + [blockdiag(wi8, g) for g in range(4)]
        lcl = [lccols(g) for g in own] + [lccols(g) for g in range(4)]
        heads_own = [4 * r + i for i in range(4)]
        sbb = np.concatenate([np.broadcast_to(sbias[heads_own][None, :], (128, 4)), np.broadcast_to(sbias[None, :], (128, 8))], axis=1)
        idxy = np.zeros((128, 8), np.int32)
        for j in range(8):
            rho = r if j < 4 else 1 - r
            idxy[:, j] = (rho * 256 + (j % 2) * 128 + pp) * 2 + r
        sl = slice(c * NS, (c + 1) * NS)
        lcs = lcs_all[sl].reshape(NS, 3, 4, 128).transpose(3, 2, 0, 1)
        lhs = lhs_all[sl].reshape(NS, 4, 128).transpose(2, 1, 0)
        hown = [4 * r + i for i in range(4)]
        hoth = [4 * (1 - r) + i for i in range(4)]
        bao = np.zeros((1024, 512), np.float32)
        for i, h in enumerate(hown):
            bao[:, i] = Wd[:, 4096 + h]
            bao[:, 4 + i] = Wd[:, 4104 + h]
        w1own = [wtile(Wd[:, g * 1024 + 4 * r * 128:g * 1024 + 4 * r * 128 + 512]) for g in range(4)] + [wtile(bao)]
        dcwl = [dconv[:, g * 1024 + h * 128:g * 1024 + (h + 1) * 128].T for g in range(3) for h in hown] + \
               [dconv[:, gc * 128:(gc + 1) * 128].T for gc in range(24)]
        bacc = np.zeros((128, 4), np.float32)
        for i, h in enumerate(hown):
            bacc[4 + i, 0] = dtb[h]
            bacc[4 + i, 1] = Alog[h]
        bacc[8:16, 2] = dtb
        bacc[8:16, 3] = Alog
        rows1 = [Wod[h * 128:(h + 1) * 128] for h in hown + hoth]
        percore_l1 = {
            "w1own": np.concatenate(w1own, axis=0), "wo1loc": wotiles(np.concatenate(rows1, axis=0)),
            "dcw": np.ascontiguousarray(np.stack(dcwl, axis=1)).reshape(128, 144), "bac": bacc,
            "dcs": np.ascontiguousarray(dcs_all[c * NS:(c + 1) * NS].reshape(NS, 3, 24, 128).transpose(3, 2, 0, 1)).reshape(128, 24 * 48),
            "dS": np.ascontiguousarray(dS_all[c * NS:(c + 1) * NS]).reshape(NS * 8 * 128, 128),
        }
        percore.append({
            **percore_l1,
            "w0own": np.concatenate(tiles, axis=0), "wo0loc": wotiles(np.concatenate(rows, axis=0)),
            "gw": np.concatenate(gwl, axis=1), "lc": np.concatenate(lcl, axis=1), "sbb": np.ascontiguousarray(sbb),
            "idxy": idxy, "ptb": np.ascontiguousarray(np.broadcast_to(pt[sl].reshape(1, 256), (128, 256))),
            "lcs": np.ascontiguousarray(lcs).reshape(128, 192), "lhs": np.ascontiguousarray(lhs).reshape(128, 64),
        })
    xp = np.asarray(inputs["x_prompt"], np.float32)
    xs = np.asarray(inputs["x_sample"], np.float32)
    maps = []
    for c in range(NCORES):
        p, r = c // 2, c % 2
        xt = np.concatenate([xp[p, r * TP:(r + 1) * TP], xs[c * NS:(c + 1) * NS].reshape(TS, D)], axis=0)
        m = dict(common)
        m.update(percore[c])
        m["xT"] = np.ascontiguousarray(xt.T)
        maps.append(m)
    return maps


def _run(inputs, stage=99):
    if stage not in _CACHE:
        b = Builder(stage)
        b.build()
        _CACHE[stage] = b
    b = _CACHE[stage]
    maps = _host_prep(inputs)
    res = run_bass_kernel_spmd(b.nc, maps, core_ids=list(range(NCORES)))
    return res.results


def _assemble_dn(o, p, r, sl, dcp, dsp, dcs, dss):
    dnc = o["dnc_out"].reshape(3, 4, 128, 3)
    for g in range(3):
        dcp[0, p, :, g * 1024 + 4 * r * 128:g * 1024 + (4 * r + 4) * 128] = dnc[g].reshape(512, 3).T
    dsp[0, p, 4 * r:4 * r + 4] = o["dnS_out"].reshape(4, 128, 128)
    dcs[0, sl] = o["dncs_out"].reshape(3072, NS, 3).transpose(1, 2, 0)
    dss[0, sl] = o["dnSs_out"].reshape(NS, 8, 128, 128)


def kernel(**inputs):
    res = _run(inputs)
    f = np.float32
    yp = np.empty((4, 4096, D), f)
    ys = np.empty((128, 4, D), f)
    kp = np.zeros((1, 4, 4096, 8, 64), f); vp = np.zeros((1, 4, 4096, 8, 64), f)
    lcp = np.zeros((1, 4, 3, 512), f); lhp = np.zeros((1, 4, 512), f)
    dcp = np.zeros((1, 4, 3, 3072), f); dsp = np.zeros((1, 4, 8, 128, 128), f)
    ksm = np.zeros((1, 128, 4, 8, 64), f); vsm = np.zeros((1, 128, 4, 8, 64), f)
    lcs = np.zeros((1, 128, 3, 512), f); lhs = np.zeros((1, 128, 512), f)
    dcs = np.zeros((1, 128, 3, 3072), f); dss = np.zeros((1, 128, 8, 128, 128), f)
    for c in range(NCORES):
        p, r = c // 2, c % 2
        o = res[c]
        yt = o["yT"].T
        yp[p, r * TP:(r + 1) * TP] = yt[:TP]
        ys[c * NS:(c + 1) * NS] = yt[TP:].reshape(NS, 4, D)
        sl = slice(c * NS, (c + 1) * NS)
        kp[0, p, :, 4 * r:4 * r + 4, :] = o["kT_out"].reshape(4, 64, 4096).transpose(2, 0, 1)
        vp[0, p, :, 4 * r:4 * r + 4, :] = o["v_out"].reshape(4096, 4, 64)
        lcp[0, p, :, 256 * r:256 * r + 256] = o["lruc_out"].T
        lhp[0, p, 256 * r:256 * r + 256] = o["lruh_out"][:, 0]
        ksm[0, sl] = o["ks_out"].T.reshape(NS, 4, 8, 64)
        vsm[0, sl] = o["vs_out"].reshape(NS, 4, 8, 64)
        lcs[0, sl] = o["lrucs_out"].reshape(512, NS, 3).transpose(1, 2, 0)
        lhs[0, sl] = o["lruhs_out"].T
        if "dnc_out" in o:
            _assemble_dn(o, p, r, sl, dcp, dsp, dcs, dss)
    return (yp, ys, kp, vp, lcp, lhp, dcp, dsp, ksm, vsm, lcs, lhs, dcs, dss)
```

```python
from contextlib import ExitStack
import os
import numpy as np
import concourse.bass as bass
import concourse.mybir as mybir
from concourse.bass_utils import run_bass_kernel_spmd

F32 = mybir.dt.float32
BF16 = mybir.dt.bfloat16
I32 = mybir.dt.int32
ALU = mybir.AluOpType
AF = mybir.ActivationFunctionType

SAME_ENGINE_SYNC = True
SAME_SYNC_ENGS = ('pool',)
NCORES = 8
D = 1024
TP = 2048
NS = 16
TS = NS * 4
T = TP + TS
TT = [(0, 512), (512, 512), (1024, 512), (1536, 512), (2048, 64)]
EPS = 1e-6
POOLN = 2560
DBG = os.environ.get('KDBG', '').split(',')


class Op:
    __slots__ = ("eng", "fn", "deps", "dma", "sig", "need", "pre", "idx", "cc")

    def __init__(self, eng, fn, dma, cc=None):
        self.eng = eng
        self.fn = fn
        self.dma = dma
        self.cc = cc
        self.deps = []
        self.sig = None
        self.need = dma
        self.pre = None


class Prog:
    ENGS = ("pe", "act", "dve", "pool", "sp")

    def __init__(self, nc):
        self.nc = nc
        self.ops = []
        self.last_w = {}
        self.readers = {}
        self.all_dma = []
        self.force_sync = True

    def add(self, eng, fn, reads=(), writes=(), dma=False, cc=None):
        op = Op(eng, fn, dma or cc is not None, cc)
        writes = list(writes) + [k for k in reads if isinstance(k, tuple) and k[0] == "ps" and k not in writes]
        deps = {}
        for k in reads:
            w = self.last_w.get(k)
            if w is not None:
                deps[id(w)] = w
        for k in writes:
            w = self.last_w.get(k)
            if w is not None:
                deps[id(w)] = w
            for r in self.readers.get(k, ()):
                deps[id(r)] = r
        for d in deps.values():
            if d.dma or d.eng != eng:
                d.need = True
                op.deps.append(d)
            elif (SAME_ENGINE_SYNC and eng in SAME_SYNC_ENGS) or (self.force_sync and eng != "pe"):
                d.need = True
                op.deps.append(d)
        for k in writes:
            self.last_w[k] = op
            self.readers[k] = []
        for k in reads:
            lst = self.readers.setdefault(k, [])
            if not op.dma:
                lst[:] = [r for r in lst if r.dma or r.eng != eng]
            lst.append(op)
        op.idx = len(self.ops)
        self.ops.append(op)
        if op.dma:
            self.all_dma.append(op)
        return op

    def barrier(self):
        last = {}
        for op in self.ops:
            if not op.dma and op.fn is not None:
                last[op.eng] = op
        pend = list(self.all_dma)
        self.all_dma = []
        for e in self.ENGS:
            op = Op(e, None, False)
            for o in last.values():
                if o.eng != e or (SAME_ENGINE_SYNC and e != "pe"):
                    o.need = True
                    op.deps.append(o)
            op.deps.extend(pend)
            op.idx = len(self.ops)
            self.ops.append(op)
        self.last_w = {}
        self.readers = {}

    def finalize_dma_wait(self, eng="sp"):
        op = Op(eng, None, False)
        op.deps.extend(self.all_dma)
        op.idx = len(self.ops)
        self.ops.append(op)

    def emit(self, block, sems):
        cnt = {e: 0 for e in self.ENGS}
        dcnt = {e: 0 for e in self.ENGS}
        duse = {}
        for op in self.ops:
            if op.cc is not None:
                op.sig = (op.cc, 1)
            elif op.dma:
                pool = sems["dma"][op.eng]
                s = pool[dcnt[op.eng] % len(pool)]
                dcnt[op.eng] += 1
                u = duse.get(id(s), 0)
                if u > 0:
                    op.pre = (s, 16 * u)
                duse[id(s)] = u + 1
                op.sig = (s, 16 * (u + 1))
            elif op.need:
                cnt[op.eng] += 1
                op.sig = (sems["eng"][op.eng], cnt[op.eng])
        stats = {e: [0, 0] for e in self.ENGS}

        def run(eng_name):
            def body(eng):
                waited = {}
                for op in self.ops:
                    if op.eng != eng_name:
                        continue
                    w = {}
                    for d in op.deps:
                        s, v = d.sig
                        if w.get(id(s), (None, 0))[1] < v:
                            w[id(s)] = (s, v)
                    if op.pre is not None:
                        s, v = op.pre
                        if w.get(id(s), (None, 0))[1] < v:
                            w[id(s)] = (s, v)
                    for s, v in w.values():
                        if waited.get(id(s), 0) < v:
                            eng.wait_ge(s, v)
                            waited[id(s)] = v
                            stats[eng_name][1] += 1
                    if op.fn is None:
                        continue
                    ins = op.fn(eng)
                    stats[eng_name][0] += 1
                    if op.sig is not None:
                        ins.then_inc(op.sig[0], 16 if (op.dma and op.cc is None) else 1)
            return body

        block.tensor(run("pe"))
        block.scalar(run("act"))
        block.vector(run("dve"))
        block.gpsimd(run("pool"))
        block.sync(run("sp"))
        return stats


class Arena:
    def __init__(self, ap_f32, nwords):
        self.ap = ap_f32
        self.n = nwords
        self.off = 0

    def mark(self):
        return self.off

    def release(self, m):
        self.off = m

    def alloc(self, shape, dt):
        ne = int(np.prod(shape))
        nw = ne if dt in (F32, I32) else (ne + 1) // 2
        nw = (nw + 7) // 8 * 8
        assert self.off + nw <= self.n, f"arena overflow {self.off}+{nw}>{self.n}"
        v = self.ap[:, self.off:self.off + nw]
        self.off += nw
        if dt != F32:
            v = v.bitcast(dt)
        v = v[:, 0:ne]
        if len(shape) == 2:
            v = v.rearrange("p (a b) -> p a b", a=shape[0])
        elif len(shape) == 3:
            v = v.rearrange("p (a b c) -> p a b c", a=shape[0], b=shape[1])
        return v


class Builder:
    def __init__(self, stage):
        self.stage = stage
        self.nc = bass.Bass("TRN2", target_bir_lowering=False)
        self.es = ExitStack()
        self.P = Prog(self.nc)
        self.din = {}
        self.dout = {}
        self.psrot = 0

    def inp(self, name, shape, dt=F32):
        t = self.nc.dram_tensor(name, list(shape), dt, kind="ExternalInput")
        self.din[name] = t
        return t

    def outp(self, name, shape, dt=F32):
        t = self.nc.dram_tensor(name, list(shape), dt, kind="ExternalOutput")
        self.dout[name] = t
        return t

    def sb(self, name, shape, dt):
        return self.es.enter_context(self.nc.sbuf_tensor(name, list(shape), dt))

    def dma(self, eng, out, in_, reads, writes):
        return self.P.add(eng, lambda e: e.dma_start(out=out, in_=in_), reads=reads, writes=writes, dma=True)

    def mmgroup(self, out, pairs, reads, writes, start=True, stop=True):
        def fn(e):
            ins = None
            n = len(pairs)
            for i, (l, r) in enumerate(pairs):
                ins = e.matmul(out, lhsT=l, rhs=r, start=(start and i == 0), stop=(stop and i == n - 1))
            return ins
        return self.P.add("pe", fn, reads=reads, writes=writes)

    def bank(self, lo=0, hi=8):
        b = lo + self.psrot % (hi - lo)
        self.psrot += 1
        return b

    def build(self):
        nc, P, es = self.nc, self.P, self.es
        xT = self.inp("xT", [D, T])
        gains = self.inp("gains", [128, 6 * 8])
        ident_d = self.inp("ident", [128, 128])
        cbf_d = self.inp("cbf", [128, 5 * 128])
        wfi = self.inp("wfi", [2 * 2 * 4 * 2 * 128, 4096])
        wfo = self.inp("wfo", [2 * 2 * 2 * 2 * 128, 4096])
        yT = self.outp("yT", [D, T])

        x = self.sb("x", [128, 8, T], F32)
        gn = self.sb("gn", [128, 48], F32)
        ident = self.sb("identf", [128, 128], F32)
        cbf = self.sb("cbfs", [128, 5, 128], BF16)
        cols = self.sb("cols", [128, 8], F32)
        NW = 3
        wsl = [self.sb(f"wsl{i}", [128, 8, 512], BF16) for i in range(NW)]
        AW = 26 * 1024
        arena_t = self.sb("arena", [128, AW], F32)
        A = Arena(arena_t[:, :], AW)
        ps = [es.enter_context(nc.psum_tensor(f"ps{i}", [128, 512], F32)) for i in range(8)]
        sems = {"eng": {e: es.enter_context(nc.semaphore("s_" + e)) for e in Prog.ENGS},
                "dma": {e: [es.enter_context(nc.semaphore(f"d_{e}{i}")) for i in range(n)]
                        for e, n in {"sp": 20, "pool": 12, "act": 4}.items()}}
        ones_bf = cbf[:, 0, :]
        eps_col = cols[:, 0:1]
        one_col = cols[:, 1:2]

        self.dma("sp", gn[:], gains.ap(), [], ["gn"])
        self.dma("sp", ident[:], ident_d.ap(), [], ["ident"])
        self.dma("pool", cbf[:], cbf_d.ap().rearrange("p (a b) -> p a b", a=5), [], ["cbf"])
        P.add("dve", lambda e: e.memset(cols[:, 0:1], EPS), writes=["cols"])
        P.add("dve", lambda e: e.memset(cols[:, 1:2], 1.0), reads=["cols"], writes=["cols"])
        for kc in range(8):
            self.dma("sp", x[:, kc, :], xT[kc * 128:(kc + 1) * 128, :], [], [("x", kc, ti) for ti in range(5)])

        wstate = {"i": 0}

        def load_w(src_ap):
            s = wstate["i"] % NW
            wstate["i"] += 1
            self.dma("pool", wsl[s][:], src_ap.rearrange("p (a b) -> p a b", a=8), [], [("w", s)])
            return wsl[s], ("w", s)

        def rmsnorm(n, xn):
            m = A.mark()
            sq = [A.alloc([512], BF16) for _ in range(3)]
            rt = A.alloc([512], F32)
            rinv = A.alloc([512], F32)
            for ti, (t0, nn) in enumerate(TT):
                P.force_sync = nn < 256
                b = self.bank(6, 8)
                for kc in range(8):
                    s = (ti * 8 + kc) % 3
                    P.add("act", lambda e, s=s, kc=kc, t0=t0, nn=nn: e.activation(out=sq[s][:, :nn], in_=x[:, kc, t0:t0 + nn], func=AF.Square),
                          reads=[("x", kc, ti)], writes=[("sq", s)])
                    self.mmgroup(ps[b][:, :nn], [(ones_bf, sq[s][:, :nn])], reads=[("sq", s), "cbf"], writes=[("ps", b)],
                                 start=(kc == 0), stop=(kc == 7))
                P.add("act", lambda e, b=b, nn=nn: e.activation(out=rt[:, :nn], in_=ps[b][:, :nn], func=AF.Sqrt, bias=eps_col, scale=1.0 / D),
                      reads=[("ps", b), "cols"], writes=["rt"])
                P.add("dve", lambda e, nn=nn: e.reciprocal(out=rinv[:, :nn], in_=rt[:, :nn]), reads=["rt"], writes=["rinv"])
                for kc in range(8):
                    eng = "dve"
                    P.add(eng, lambda e, kc=kc, t0=t0, nn=nn: e.scalar_tensor_tensor(
                        out=xn[:, kc, t0:t0 + nn], in0=x[:, kc, t0:t0 + nn], scalar=gn[:, n * 8 + kc:n * 8 + kc + 1], in1=rinv[:, :nn],
                        op0=ALU.mult, op1=ALU.mult), reads=[("x", kc, ti), "rinv", "gn"], writes=[("xn", kc, ti)])
            P.force_sync = False
            A.release(m)

        def ffn(l, f):
            m = A.mark()
            xn = A.alloc([8, T], BF16)
            h = A.alloc([8, T], BF16)
            sg = [A.alloc([512], F32) for _ in range(2)]
            rmsnorm(l * 3 + (0 if f == 0 else 2), xn)
            sgi = 0
            for hf in range(2):
                for fg2 in range(2):
                    fg = hf * 2 + fg2
                    base = (((l * 2 + f) * 4 + fg) * 2) * 128
                    wg, kg = load_w(wfi[base:base + 128, :])
                    wu, ku = load_w(wfi[base + 128:base + 256, :])
                    for j in range(4):
                        for ti, (t0, nn) in enumerate(TT):
                            P.force_sync = nn < 256
                            bg = self.bank(0, 4)
                            bu = self.bank(0, 4)
                            rk = [("xn", kc, ti) for kc in range(8)]
                            self.mmgroup(ps[bg][:, :nn], [(wg[:, kc, j * 128:(j + 1) * 128], xn[:, kc, t0:t0 + nn]) for kc in range(8)],
                                         reads=rk + [kg], writes=[("ps", bg)])
                            self.mmgroup(ps[bu][:, :nn], [(wu[:, kc, j * 128:(j + 1) * 128], xn[:, kc, t0:t0 + nn]) for kc in range(8)],
                                         reads=rk + [ku], writes=[("ps", bu)])
                            s = sgi % 2
                            sgi += 1
                            P.add("act", lambda e, s=s, bg=bg, nn=nn: e.activation(out=sg[s][:, :nn], in_=ps[bg][:, :nn], func=AF.Silu),
                                  reads=[("ps", bg)], writes=[("sg", s)])
                            fc = fg2 * 4 + j
                            P.add("dve", lambda e, s=s, bu=bu, fc=fc, t0=t0, nn=nn: e.tensor_tensor(
                                out=h[:, fc, t0:t0 + nn], in0=sg[s][:, :nn], in1=ps[bu][:, :nn], op=ALU.mult),
                                reads=[("sg", s), ("ps", bu)], writes=[("h", fc, ti)])
                for cg in range(2):
                    base = ((((l * 2 + f) * 2 + hf) * 2) + cg) * 128
                    wo, ko = load_w(wfo[base:base + 128, :])
                    for c4 in range(4):
                        c = cg * 4 + c4
                        for ti, (t0, nn) in enumerate(TT):
                            P.force_sync = nn < 256
                            b = self.bank(4, 6)
                            self.mmgroup(ps[b][:, :nn], [(wo[:, fc, c4 * 128:(c4 + 1) * 128], h[:, fc, t0:t0 + nn]) for fc in range(8)],
                                         reads=[("h", fc, ti) for fc in range(8)] + [ko], writes=[("ps", b)])
                            P.add("dve", lambda e, b=b, c=c, t0=t0, nn=nn: e.scalar_tensor_tensor(
                                out=x[:, c, t0:t0 + nn], in0=ps[b][:, :nn], scalar=0.5, in1=x[:, c, t0:t0 + nn], op0=ALU.mult, op1=ALU.add),
                                reads=[("ps", b), ("x", c, ti)], writes=[("x", c, ti)])
            P.force_sync = False
            A.release(m)


        PAIRS = [[0, 1], [2, 3], [4, 5], [6, 7]]
        w0own = self.inp("w0own", [3 * 128, 4096])
        w0full = self.inp("w0full", [5 * 128, 4096])
        wo0loc = self.inp("wo0loc", [2 * 128, 4096])
        wo0g = self.inp("wo0g", [2 * 128, 4096])
        gw_d = self.inp("gw", [128, 12 * 128])
        lc_d = self.inp("lc", [128, 6 * 8])
        qk_d = self.inp("qkcol", [128, 2])
        sbb_d = self.inp("sbb", [128, 12])
        sbrow_d = self.inp("sbrow", [128, 32])
        maskd_d = self.inp("maskd", [128, 4 * 512])
        maskn_d = self.inp("maskn", [128, 16 * 32])
        idxy_d = self.inp("idxy", [128, 8], I32)
        ptb_d = self.inp("ptb", [128, 256], I32)
        iota_d = self.inp("iota", [128, 1])
        lcs_d = self.inp("lcs", [128, 4 * 16 * 3])
        lhs_d = self.inp("lhs", [128, 4 * 16])
        ck_d = self.inp("ck", [POOLN * 128, 512])
        cv_d = self.inp("cv", [POOLN * 128, 512])
        kT_out = self.outp("kT_out", [256, 4096])
        v_out = self.outp("v_out", [4096, 256])
        lruc_out = self.outp("lruc_out", [256, 3])
        lruh_out = self.outp("lruh_out", [256, 1])
        ks_out = self.outp("ks_out", [512, 64])
        vs_out = self.outp("vs_out", [64, 512])
        lrucs_out = self.outp("lrucs_out", [512, 48])
        lruhs_out = self.outp("lruhs_out", [512, 16])
        xin0 = [nc.dram_tensor(f"xin0{h}", [512, 2048], BF16) for h in range(2)]
        xall0 = [nc.dram_tensor(f"xall0{h}", [1024, 2048], BF16) for h in range(2)]
        yloc0 = [nc.dram_tensor(f"yloc0{h}", [256, 4096], BF16) for h in range(2)]
        yall0 = [nc.dram_tensor(f"yall0{h}", [512, 4096], BF16) for h in range(2)]
        ccs = [es.enter_context(nc.semaphore(f"cc{i}")) for i in range(8)]

        def allgather(src, dst, reads, writes, sem):
            if "nocc" in DBG:
                n = src.shape[0]
                self.dma("sp", dst[0:n, :], src.ap(), reads, [("ccfake", writes[0])])
                self.dma("sp", dst[n:2 * n, :], src.ap(), reads + [("ccfake", writes[0])], writes)
            else:
                P.add("pool", lambda e: e.collective_compute("AllGather", ALU.bypass, replica_groups=PAIRS, ins=[src.ap().opt()], outs=[dst.ap().opt()]),
                      reads=reads, writes=writes, cc=sem)

        gw = self.sb("gws", [128, 12, 128], BF16)
        lc = self.sb("lcs_", [128, 6, 8], F32)
        lc2 = self.sb("lc2", [128, 6], F32)
        qkcol = self.sb("qkc", [128, 2], F32)
        sbb = self.sb("sbbs", [128, 12], F32)
        sbrow = self.sb("sbrows", [128, 32], F32)
        maskd = self.sb("maskds", [128, 4, 512], BF16)
        maskn = self.sb("masknS", [128, 16, 32], BF16)
        idxy = self.sb("idxys", [128, 8], I32)
        ptb = self.sb("ptbs", [128, 256], I32)
        pidx = self.sb("pidx", [128, 256], I32)
        iota = self.sb("iotas", [128, 1], F32)
        triS = cbf[:, 1, :]
        triC = cbf[:, 2, :]
        blk64 = cbf[:, 3, :]
        zero_bf = self.sb("zerobf", [128, 128], BF16)

        self.dma("pool", gw[:], gw_d.ap().rearrange("p (a b) -> p a b", a=12), [], ["gw"])
        self.dma("sp", lc[:], lc_d.ap().rearrange("p (a b) -> p a b", a=6), [], ["lc"])
        self.dma("sp", qkcol[:], qk_d.ap(), [], ["qkcol"])
        self.dma("sp", sbb[:], sbb_d.ap(), [], ["sbb"])
        self.dma("sp", sbrow[:], sbrow_d.ap(), [], ["sbrow"])
        self.dma("pool", maskd[:], maskd_d.ap().rearrange("p (a b) -> p a b", a=4), [], ["maskd"])
        self.dma("pool", maskn[:], maskn_d.ap().rearrange("p (a b) -> p a b", a=16), [], ["maskn"])
        self.dma("sp", idxy[:], idxy_d.ap(), [], ["idxy"])
        self.dma("sp", ptb[:], ptb_d.ap(), [], ["ptb"])
        self.dma("sp", iota[:], iota_d.ap(), [], ["iota"])
        P.add("dve", lambda e: e.tensor_scalar(out=pidx[:], in0=ptb[:], scalar1=128.0, scalar2=iota[:, 0:1], op0=ALU.mult, op1=ALU.add),
              reads=["ptb", "iota"], writes=["pidx"])
        P.add("pool", lambda e: e.memset(zero_bf[:], 0.0), writes=["zero_bf"])
        P.add("act", lambda e: e.activation(out=lc2[:], in_=lc[:, :, 7], func=AF.Exp, scale=-1.0), reads=["lc"], writes=["lc2"])
        P.add("act", lambda e: e.activation(out=lc2[:], in_=lc2[:], func=AF.Ln, bias=one_col, scale=1.0), reads=["lc2", "cols"], writes=["lc2"])
        P.add("dve", lambda e: e.tensor_scalar(out=lc[:, :, 7], in0=lc2[:], scalar1=-8.0, scalar2=None, op0=ALU.mult), reads=["lc2", "lc"], writes=["lc"])
        P.add("dve", lambda e: e.tensor_scalar(out=lc2[:], in0=lc2[:], scalar1=-16.0, scalar2=None, op0=ALU.mult), reads=["lc2"], writes=["lc2"])

        def act(out, in_, func, reads, writes, **kw):
            return P.add("act", lambda e: e.activation(out=out, in_=in_, func=func, **kw), reads=reads, writes=writes)

        def tt_(eng, out, in0, in1, op, reads, writes):
            return P.add(eng, lambda e: e.tensor_tensor(out=out, in0=in0, in1=in1, op=op), reads=reads, writes=writes)

        def ts_(eng, out, in0, s1, s2, op0, op1, reads, writes):
            if s2 is None:
                return P.add(eng, lambda e: e.tensor_scalar(out=out, in0=in0, scalar1=s1, scalar2=None, op0=op0), reads=reads, writes=writes)
            return P.add(eng, lambda e: e.tensor_scalar(out=out, in0=in0, scalar1=s1, scalar2=s2, op0=op0, op1=op1), reads=reads, writes=writes)

        def stt_(out, in0, scalar, in1, op0, op1, reads, writes):
            return P.add("dve", lambda e: e.scalar_tensor_tensor(out=out, in0=in0, scalar=scalar, in1=in1, op0=op0, op1=op1), reads=reads, writes=writes)

        def cp_(eng, out, in_, reads, writes):
            if eng == "act":
                return act(out, in_, AF.Copy, reads, writes)
            return P.add(eng, lambda e: e.tensor_copy(out=out, in_=in_), reads=reads, writes=writes)

        uid = {"n": 0}

        def U(name):
            uid["n"] += 1
            return (name, uid["n"])

        def headnorm(src_ps, kps, gcol, out_ap, kout, nn, tmp):
            sq, rt, rinv = tmp
            b = 7
            act(sq[:, :nn], src_ps, AF.Square, [kps], ["hn_sq"])
            self.mmgroup(ps[b][:, :nn], [(blk64, sq[:, :nn])], reads=["hn_sq", "cbf"], writes=[("ps", b)])
            act(rt[:, :nn], ps[b][:, :nn], AF.Sqrt, [("ps", b), "cols"], ["hn_rt"], bias=eps_col, scale=1.0)
            P.add("dve", lambda e: e.reciprocal(out=rinv[:, :nn], in_=rt[:, :nn]), reads=["hn_rt"], writes=["hn_rinv"])
            stt_(out_ap, src_ps, gcol, rinv[:, :nn], ALU.mult, ALU.mult, [kps, "hn_rinv", "qkcol"], [kout])

        def lru_core(ch, nn, xc, xg_ps, kxg, tmp, hinit, hout, rec_out, krec, scan3d=None):
            xcb, r_, i_, a_, a2, gx, bb, t1, t2, sgm = tmp
            wa = gw[:, (0 if ch < 2 else 2) + ch, :]
            wi = gw[:, (2 if ch < 2 else 6) + ch, :]
            cp_("act", xcb[:, :nn], xc, ["xc"], ["xcb"])
            ba, bi = self.bank(4, 6), self.bank(4, 6)
            self.mmgroup(ps[ba][:, :nn], [(wa, xcb[:, :nn])], reads=["xcb", "gw"], writes=[("ps", ba)])
            self.mmgroup(ps[bi][:, :nn], [(wi, xcb[:, :nn])], reads=["xcb", "gw"], writes=[("ps", bi)])
            act(r_[:, :nn], ps[ba][:, :nn], AF.Sigmoid, [("ps", ba), "lc"], ["lr"], bias=lc[:, ch, 5:6], scale=1.0)
            act(i_[:, :nn], ps[bi][:, :nn], AF.Sigmoid, [("ps", bi), "lc"], ["li"], bias=lc[:, ch, 6:7], scale=1.0)
            act(a_[:, :nn], r_[:, :nn], AF.Exp, ["lr", "lc"], ["la"], scale=lc[:, ch, 7:8])
            act(a2[:, :nn], r_[:, :nn], AF.Exp, ["lr", "lc2"], ["la2"], scale=lc2[:, ch:ch + 1])
            act(a2[:, :nn], a2[:, :nn], AF.Sqrt, ["la2", "cols"], ["la2"], bias=one_col, scale=-1.0)
            tt_("dve", gx[:, :nn], i_[:, :nn], xc, ALU.mult, ["li", "xc"], ["lgx"])
            tt_("dve", bb[:, :nn], a2[:, :nn], gx[:, :nn], ALU.mult, ["la2", "lgx"], ["lbb"])
            if scan3d is None:
                P.add("dve", lambda e: e.tensor_tensor_scan(out=hout, data0=a_[:, :nn], data1=bb[:, :nn], initial=hinit, op0=ALU.mult, op1=ALU.add),
                      reads=["la", "lbb", "lh_prev"], writes=["lh"])
            else:
                nb, hs0 = scan3d
                a3 = a_[:, :nn].rearrange("p (b t) -> p b t", t=4)
                b3 = bb[:, :nn].rearrange("p (b t) -> p b t", t=4)
                h3 = hout.rearrange("p (b t) -> p b t", t=4)
                for t in range(4):
                    prev = hs0 if t == 0 else h3[:, :, t - 1]
                    tt_("dve", h3[:, :, t], a3[:, :, t], prev, ALU.mult, ["la", "lh", "hst"], ["lh"])
                    tt_("dve", h3[:, :, t], h3[:, :, t], b3[:, :, t], ALU.add, ["lh", "lbb"], ["lh"])
            act(t1[:, :nn], xg_ps, AF.Square, [kxg], ["lt1"])
            ts_("dve", t1[:, :nn], t1[:, :nn], 0.044715, 1.0, ALU.mult, ALU.add, ["lt1"], ["lt1"])
            tt_("dve", t2[:, :nn], t1[:, :nn], xg_ps, ALU.mult, ["lt1", kxg], ["lt2"])
            act(sgm[:, :nn], t2[:, :nn], AF.Sigmoid, ["lt2"], ["lsg"], scale=1.5957691216057308)
            tt_("dve", t2[:, :nn], sgm[:, :nn], xg_ps, ALU.mult, ["lsg", kxg, "lt2"], ["lt2"])
            tt_("dve", rec_out, hout, t2[:, :nn], ALU.mult, ["lh", "lt2"], [krec])

        def conv4(out, src, ch, reads, writes, three_d=False):
            def sl(k):
                return src[:, :, k:k + 4] if three_d else src[:, k:k + 512]
            ts_("dve", out, sl(0), lc[:, ch, 0:1], lc[:, ch, 4:5], ALU.mult, ALU.add, reads + ["lc"], writes)
            for k in range(1, 4):
                stt_(out, sl(k), lc[:, ch, k:k + 1], out, ALU.mult, ALU.add, reads + ["lc"] + writes, writes)

        def mixer0():
            m0 = A.mark()
            xs_pad = A.alloc([8, 128], BF16)
            ymix_s = A.alloc([8, 64], BF16)
            mx = A.mark()
            xn = A.alloc([8, T], BF16)
            rmsnorm(1, xn)
            for kc in range(8):
                self.dma("sp", xin0[kc // 4][(kc % 4) * 128:(kc % 4 + 1) * 128, :], xn[:, kc, 0:TP], [("xn", kc, ti) for ti in range(4)], [("xin0", kc)])
            for h in range(2):
                allgather(xin0[h], xall0[h], [("xin0", kc) for kc in range(4 * h, 4 * h + 4)], [("xall0", h)], ccs[h])
            P.add("pool", lambda e: e.memset(xs_pad[:, :, :], 0.0), writes=["xs_pad"])
            P.add("pool", lambda e: e.tensor_copy(out=xs_pad[:, :, 0:64], in_=xn[:, :, TP:T]), reads=[("xn", kc, 4) for kc in range(8)] + ["xs_pad"], writes=["xs_pad"])
            P.add("pool", lambda e: e.memset(ymix_s[:, :, :], 0.0), writes=[("ymix", j, 4) for j in range(8)])
            P.barrier()
            A.release(mx)
            m1 = A.mark()
            xf = [A.alloc([8, 512], BF16) for _ in range(2)]
            xfi = {"i": 0}

            def load_xf(tt):
                s = xfi["i"] % 2
                xfi["i"] += 1
                rho, c0 = tt // 4, (tt % 4) * 512
                for h in range(2):
                    self.dma("sp", xf[s][:, 4 * h:4 * h + 4, :], xall0[h][rho * 512:(rho + 1) * 512, c0:c0 + 512].rearrange("(kc p) t -> p kc t", p=128),
                             [("xall0", h)], [("xf", s, h)])
                return xf[s], ("xf", s)

            m2 = A.mark()
            xrh = [A.alloc([515], F32) for _ in range(2)]
            xc = A.alloc([512], F32)
            tmp = [A.alloc([512], BF16)] + [A.alloc([512], F32) for _ in range(9)]
            hb = [[A.alloc([512], F32) for _ in range(2)] for _ in range(2)]
            rec = [A.alloc([512], BF16) for _ in range(2)]
            for ch in range(2):
                P.add("pool", lambda e, ch=ch: e.memset(xrh[ch][:, 0:3], 0.0), writes=[("xrh", ch)])
            wl, kwl = load_w(w0own[256:384, :])
            for tt in range(8):
                xft, kxf = load_xf(tt)
                for ch in range(2):
                    br, bg = self.bank(0, 4), self.bank(0, 4)
                    self.mmgroup(ps[br][:, :], [(wl[:, kc, ch * 128:(ch + 1) * 128], xft[:, kc, :]) for kc in range(8)], reads=[kxf + (0,), kxf + (1,), kwl], writes=[("ps", br)])
                    self.mmgroup(ps[bg][:, :], [(wl[:, kc, 256 + ch * 128:256 + (ch + 1) * 128], xft[:, kc, :]) for kc in range(8)], reads=[kxf + (0,), kxf + (1,), kwl], writes=[("ps", bg)])
                    cp_("act", xrh[ch][:, 3:515], ps[br][:, :], [("ps", br)], [("xrh", ch)])
                    conv4(xc[:, :], xrh[ch], ch, [("xrh", ch)], ["xc"])
                    if tt == 7:
                        self.dma("sp", lruc_out[ch * 128:(ch + 1) * 128, :], xrh[ch][:, 512:515], [("xrh", ch)], [("lruc_out", ch)])
                    cp_("pool", xrh[ch][:, 0:3], xrh[ch][:, 512:515], ["xc", ("xrh", ch)], [("xrh", ch)])
                    hcur, hprev = hb[ch][tt % 2], hb[ch][(tt + 1) % 2]
                    hinit = 0.0 if tt == 0 else hprev[:, 511:512]
                    lru_core(ch, 512, xc[:, :], ps[bg][:, :], ("ps", bg), tmp, hinit, hcur[:, :], rec[ch][:, :], ("rec", ch))
                    self.dma("sp", yloc0[1][ch * 128:(ch + 1) * 128, tt * 512:(tt + 1) * 512], rec[ch][:, :], [("rec", ch)], [("yloc0", 2 + ch, tt)])
                    if tt == 7:
                        self.dma("sp", lruh_out[ch * 128:(ch + 1) * 128, :], hcur[:, 511:512], ["lh"], [("lruh_out", ch)])
            P.barrier()
            A.release(m2)

            m3 = A.mark()
            qT = A.alloc([4096], BF16)
            kpad = [A.alloc([4096], BF16) for _ in range(2)]
            vpad = [A.alloc([32, 128], BF16) for _ in range(2)]
            hn_tmp = (A.alloc([512], BF16), A.alloc([512], F32), A.alloc([512], F32))
            kst = A.alloc([512], F32)
            vst = A.alloc([4, 128], F32)
            ebuf = [[A.alloc([512], F32) for _ in range(2)] for _ in range(2)]
            spb = [[A.alloc([512], BF16) for _ in range(2)] for _ in range(2)]
            tbuf = [[A.alloc([512], F32) for _ in range(2)] for _ in range(2)]
            wbuf = [[A.alloc([512], BF16) for _ in range(2)] for _ in range(2)]
            yat = [A.alloc([512], BF16) for _ in range(2)]
            for X in range(2):
                P.add("pool", lambda e, X=X: e.memset(kpad[X][:, :], 0.0), writes=[("kpad", X)])
                P.add("pool", lambda e, X=X: e.memset(vpad[X][:, :, :], 0.0), writes=[("vpad", X)])
            for lp in range(2):
                wp, kwp = load_w(w0own[lp * 128:(lp + 1) * 128, :])
                for tt in range(8):
                    xft, kxf = load_xf(tt)
                    c0 = tt * 512
                    bq, bk, bv = 0, 1, 2
                    self.mmgroup(ps[bq][:, :], [(wp[:, kc, 0:128], xft[:, kc, :]) for kc in range(8)], reads=[kxf + (0,), kxf + (1,), kwp], writes=[("ps", bq)])
                    self.mmgroup(ps[bk][:, :], [(wp[:, kc, 128:256], xft[:, kc, :]) for kc in range(8)], reads=[kxf + (0,), kxf + (1,), kwp], writes=[("ps", bk)])
                    for blk in range(4):
                        self.mmgroup(ps[bv][:, blk * 128:(blk + 1) * 128], [(xft[:, kc, blk * 128:(blk + 1) * 128], wp[:, kc, 256:384]) for kc in range(8)],
                                     reads=[kxf + (0,), kxf + (1,), kwp], writes=[("ps", bv)])
                    headnorm(ps[bq][:, :], ("ps", bq), qkcol[:, 0:1], qT[:, c0:c0 + 512], "qT", 512, hn_tmp)
                    headnorm(ps[bk][:, :], ("ps", bk), qkcol[:, 1:2], kst[:, :], "kst", 512, hn_tmp)
                    self.dma("sp", kT_out[lp * 128:(lp + 1) * 128, c0:c0 + 512], kst[:, :], ["kst"], [("kT_out", lp, tt)])
                    cp_("act", kpad[0][0:64, c0:c0 + 512], kst[0:64, :], ["kst"], [("kpad", 0)])
                    cp_("act", kpad[1][64:128, c0:c0 + 512], kst[64:128, :], ["kst"], [("kpad", 1)])
                    v3 = ps[bv][:, :].rearrange("p (a b) -> p a b", a=4)
                    cp_("act", vst[:, :, :], v3, [("ps", bv)], ["vst"])
                    self.dma("sp", v_out[c0:c0 + 512, lp * 128:(lp + 1) * 128].rearrange("(a p) f -> p a f", p=128), vst[:, :, :], ["vst"], [("v_out", lp, tt)])
                    cp_("dve", vpad[0][:, tt * 4:(tt + 1) * 4, 0:64], v3[:, :, 0:64], [("ps", bv)], [("vpad", 0)])
                    cp_("dve", vpad[1][:, tt * 4:(tt + 1) * 4, 64:128], v3[:, :, 64:128], [("ps", bv)], [("vpad", 1)])
                steps = [(qs, kb) for qs in range(8) for kb in range(4 * qs + 3, -1, -1)]
                ZB = [[0, 1], [2, 3]]
                RB = [4, 5]
                OB = 6

                def zmm(i):
                    qs, kb = steps[i]
                    for X in range(2):
                        b = ZB[X][i % 2]
                        self.mmgroup(ps[b][:, :], [(kpad[X][:, kb * 128:(kb + 1) * 128], qT[:, qs * 512:(qs + 1) * 512])],
                                     reads=[("kpad", X), "qT"], writes=[("ps", b)])
                zmm(0)
                for i, (qs, kb) in enumerate(steps):
                    first = (kb == 4 * qs + 3)
                    last = (kb == 0)
                    diag = kb >= 4 * qs
                    mi = kb - 4 * qs
                    if i + 1 < len(steps):
                        zmm(i + 1)
                    for X in range(2):
                        zb = ZB[X][i % 2]
                        bias = sbb[:, lp * 2 + X:lp * 2 + X + 1]
                        e_, sp_, t_, w_ = ebuf[X][i % 2], spb[X][i % 2], tbuf[X][i % 2], wbuf[X][i % 2]
                        ke, ksp, kt, kw = ("e", X, i % 2), ("sp", X, i % 2), ("t", X, i % 2), ("w", X, i % 2)
                        act(e_[:, :], ps[zb][:, :], AF.Exp, [("ps", zb), "sbb"], [ke], bias=bias, scale=0.125)
                        act(sp_[:, :], e_[:, :], AF.Ln, [ke, "cols"], [ksp], bias=one_col, scale=1.0)
                        if diag:
                            tt_("pool", sp_[:, :], sp_[:, :], maskd[:, mi, :], ALU.mult, [ksp, "maskd"], [ksp])
                        self.mmgroup(ps[RB[X]][:, :], [(triS, sp_[:, :])], reads=[ksp, "cbf"], writes=[("ps", RB[X])], start=first, stop=False)
                        stt_(t_[:, :], ps[zb][:, :], 0.125, sp_[:, :], ALU.mult, ALU.subtract, [("ps", zb), ksp], [kt])
                        tt_("dve", t_[:, :], t_[:, :], ps[RB[X]][:, :], ALU.subtract, [kt, ("ps", RB[X])], [kt])
                        act(w_[:, :], t_[:, :], AF.Exp, [kt, "sbb"], [kw], bias=bias, scale=1.0)
                        if diag:
                            tt_("pool", w_[:, :], w_[:, :], maskd[:, mi, :], ALU.mult, [kw, "maskd"], [kw])
                        self.mmgroup(ps[RB[X]][:, :], [(triC, sp_[:, :])], reads=[ksp, "cbf", ("ps", RB[X])], writes=[("ps", RB[X])], start=False, stop=last)
                        self.mmgroup(ps[OB][:, :], [(vpad[X][:, kb, :], w_[:, :])], reads=[kw, ("vpad", X)], writes=[("ps", OB)],
                                     start=(first and X == 0), stop=(last and X == 1))
                    if last:
                        ya = yat[qs % 2]
                        cp_("act", ya[:, :], ps[OB][:, :], [("ps", OB)], [("yat", qs % 2)])
                        self.dma("sp", yloc0[0][lp * 128:(lp + 1) * 128, qs * 512:(qs + 1) * 512], ya[:, :], [("yat", qs % 2)], [("yloc0", lp, qs)])
            P.barrier()
            A.release(m3)
            A.release(m1)
            if "nosample" not in DBG:
                P.force_sync = True
                ms = A.mark()
                qpadS = A.alloc([8, 64], BF16)
                knew = A.alloc([8, 128], BF16)
                vnew = A.alloc([512], BF16)
                hn_s = (A.alloc([64], BF16), A.alloc([64], F32), A.alloc([64], F32))
                qst = A.alloc([64], F32)
                ksts = A.alloc([64], F32)
                vsts = A.alloc([512], F32)
                xrhs = A.alloc([16, 7], F32)
                xcs = A.alloc([64], F32)
                tmps = [A.alloc([64], BF16)] + [A.alloc([64], F32) for _ in range(9)]
                hs_ = A.alloc([64], F32)
                hst = A.alloc([4, 16], F32)
                hls = A.alloc([16], F32)
                kpg = [A.alloc([512], F32) for _ in range(2)]
                vpg = [A.alloc([512], BF16) for _ in range(2)]
                KT = [A.alloc([4, 128], BF16) for _ in range(2)]
                zs_ = [A.alloc([32], F32) for _ in range(2)]
                es_ = [A.alloc([32], F32) for _ in range(2)]
                sps = [A.alloc([32], F32) for _ in range(2)]
                spm = [A.alloc([32], BF16) for _ in range(2)]
                ts2 = [A.alloc([32], F32) for _ in range(2)]
                ws_ = [A.alloc([32], BF16) for _ in range(2)]
                P.add("pool", lambda e: e.memset(qpadS[:, :, :], 0.0), writes=["qpadS"])
                P.add("pool", lambda e: e.memset(knew[:, :, :], 0.0), writes=["knew"])
                self.dma("sp", hst[:, :, :], lhs_d.ap().rearrange("p (a b) -> p a b", a=4), [], ["hst"])
                wq, kwq = load_w(w0full[0:128, :])
                for c in range(4):
                    b_ = self.bank(0, 4)
                    self.mmgroup(ps[b_][:, :64], [(wq[:, kc, c * 128:(c + 1) * 128], xs_pad[:, kc, 0:64]) for kc in range(8)], reads=["xs_pad", kwq], writes=[("ps", b_)])
                    headnorm(ps[b_][:, :64], ("ps", b_), qkcol[:, 0:1], qst[:, :], "qst", 64, hn_s)
                    cp_("act", qpadS[0:64, 2 * c, :], qst[0:64, :], ["qst"], ["qpadS"])
                    cp_("act", qpadS[64:128, 2 * c + 1, :], qst[64:128, :], ["qst", "qpadS"], ["qpadS"])
                wk, kwk = load_w(w0full[128:256, :])
                for c in range(4):
                    b_ = self.bank(0, 4)
                    self.mmgroup(ps[b_][:, :64], [(wk[:, kc, c * 128:(c + 1) * 128], xs_pad[:, kc, 0:64]) for kc in range(8)], reads=["xs_pad", kwk], writes=[("ps", b_)])
                    headnorm(ps[b_][:, :64], ("ps", b_), qkcol[:, 1:2], ksts[:, :], "ksts", 64, hn_s)
                    self.dma("sp", ks_out[c * 128:(c + 1) * 128, :], ksts[:, :], ["ksts"], [("ks_out", c)])
                    cp_("act", knew[0:64, 2 * c, 0:64], ksts[0:64, :], ["ksts"], ["knew"])
                    cp_("act", knew[64:128, 2 * c + 1, 0:64], ksts[64:128, :], ["ksts", "knew"], ["knew"])
                wv, kwv = load_w(w0full[256:384, :])
                b_ = self.bank(0, 4)
                self.mmgroup(ps[b_][:, :], [(xs_pad[:, kc, :], wv[:, kc, :]) for kc in range(8)], reads=["xs_pad", kwv], writes=[("ps", b_)])
                cp_("act", vsts[:, :], ps[b_][:, :], [("ps", b_)], ["vsts"])
                cp_("dve", vnew[:, :], ps[b_][:, :], [("ps", b_)], ["vnew"])
                self.dma("sp", vs_out.ap(), vsts[0:64, :], ["vsts"], ["vs_out"])
                wxr, kwxr = load_w(w0full[384:512, :])
                wxg, kwxg = load_w(w0full[512:640, :])
                xrh3 = xrhs
                xc3 = xcs[:, :].rearrange("p (b t) -> p b t", t=4)
                for ch in range(4):
                    br, bg = self.bank(0, 4), self.bank(0, 4)
                    self.mmgroup(ps[br][:, :64], [(wxr[:, kc, ch * 128:(ch + 1) * 128], xs_pad[:, kc, 0:64]) for kc in range(8)], reads=["xs_pad", kwxr], writes=[("ps", br)])
                    self.mmgroup(ps[bg][:, :64], [(wxg[:, kc, ch * 128:(ch + 1) * 128], xs_pad[:, kc, 0:64]) for kc in range(8)], reads=["xs_pad", kwxg], writes=[("ps", bg)])
                    self.dma("sp", xrh3[:, :, 0:3], lcs_d[:, ch * 48:(ch + 1) * 48].rearrange("p (b k) -> p b k", k=3), [], [("xrhs", 0)])
                    cp_("act", xrh3[:, :, 3:7], ps[br][:, :64].rearrange("p (b t) -> p b t", t=4), [("ps", br)], [("xrhs", 1)])
                    conv4(xc3, xrh3, 2 + ch, [("xrhs", 0), ("xrhs", 1)], ["xc"], three_d=True)
                    self.dma("sp", lrucs_out[ch * 128:(ch + 1) * 128, :].rearrange("p (b k) -> p b k", k=3), xrh3[:, :, 4:7], [("xrhs", 0), ("xrhs", 1)], [("lrucs_out", ch)])
                    lru_core(2 + ch, 64, xcs[:, :], ps[bg][:, :64], ("ps", bg), tmps, None, hs_[:, :], ymix_s[:, 4 + ch, :], ("ymix", 4 + ch, 4),
                             scan3d=(16, hst[:, ch, :]))
                    cp_("dve", hls[:, :], hs_[:, :].rearrange("p (b t) -> p b t", t=4)[:, :, 3], ["lh"], ["hls"])
                    self.dma("sp", lruhs_out[ch * 128:(ch + 1) * 128, :], hls[:, :], ["hls"], [("lruhs_out", ch)])
                ZBs, RBs, OBs, KBs = [0, 1], 2, 3, [4, 5]
                gi = {"n": 0}

                def gather(b, j):
                    s_ = gi["n"] % 2
                    gi["n"] += 1
                    col = b * 16 + j
                    P.add("pool", lambda e: e.indirect_dma_start(out=kpg[s_][:, :], out_offset=None, in_=ck_d.ap(),
                                                                 in_offset=bass.IndirectOffsetOnAxis(ap=pidx[:, col:col + 1], axis=0)),
                          reads=["pidx"], writes=[("kpg", s_)], dma=True)
                    P.add("pool", lambda e: e.indirect_dma_start(out=vpg[s_][:, :], out_offset=None, in_=cv_d.ap(),
                                                                 in_offset=bass.IndirectOffsetOnAxis(ap=pidx[:, col:col + 1], axis=0)),
                          reads=["pidx"], writes=[("vpg", s_)], dma=True)
                    return s_
                seq_steps = [(b, st) for b in range(NS) for st in range(17)]
                pend = {}
                for idx_, (b, st) in enumerate(seq_steps):
                    nxt = idx_ + 1
                    if idx_ == 0:
                        pend[(0, 1)] = gather(0, 15)
                    if nxt < len(seq_steps):
                        nb, nst = seq_steps[nxt]
                        if nst >= 1 and (nb, nst) not in pend:
                            pend[(nb, nst)] = gather(nb, 16 - nst)
                    i2 = idx_ % 2
                    if st == 0:
                        self.mmgroup(ps[OBs][:, 0:128], [(zero_bf[:, :], cbf[:, 4, :])], reads=["zero_bf", "cbf"], writes=[("ps", OBs)], start=True, stop=False)
                        vblk, kvb = vnew, "vnew"
                    else:
                        s_ = pend.pop((b, st))
                        kb_ = KBs[s_]
                        for c in range(4):
                            P.add("pe", lambda e, c=c, kb_=kb_, s_=s_: e.transpose(ps[kb_][:, c * 128:(c + 1) * 128], kpg[s_][:, c * 128:(c + 1) * 128], ident[:]),
                                  reads=[("kpg", s_), "ident"], writes=[("ps", kb_)])
                        cp_("act", KT[s_][:, :, :], ps[kb_][:, :].rearrange("p (a b) -> p a b", a=4), [("ps", kb_)], [("KT", s_)])
                        vblk, kvb = vpg[s_], ("vpg", s_)
                    zb = ZBs[i2]

                    def zfn(e, st=st, b=b, zb=zb, s_=(None if st == 0 else s_)):
                        ins = None
                        for h in range(8):
                            l = knew[:, h, :] if st == 0 else KT[s_][:, h // 2, :]
                            ins = e.matmul(ps[zb][:, h * 4:(h + 1) * 4], lhsT=l, rhs=qpadS[:, h, b * 4:(b + 1) * 4], start=True, stop=True)
                        return ins
                    P.add("pe", zfn, reads=["qpadS", "knew" if st == 0 else ("KT", s_)], writes=[("ps", zb)])
                    z_, e_, sp_, sm_, t_, w_ = zs_[i2], es_[i2], sps[i2], spm[i2], ts2[i2], ws_[i2]
                    kz, ke, ksp, ksm, kt, kw = ("szs", i2), ("ses", i2), ("ssp", i2), ("ssm", i2), ("sts", i2), ("sws", i2)
                    stt_(z_[:, :], ps[zb][:, 0:32], 0.125, sbrow[:, :], ALU.mult, ALU.add, [("ps", zb), "sbrow"], [kz])
                    act(e_[:, :], z_[:, :], AF.Exp, [kz], [ke])
                    if st == 0:
                        act(sp_[:, :], e_[:, :], AF.Ln, [ke, "cols"], [ksp], bias=one_col, scale=1.0)
                        tt_("dve", sm_[:, :], sp_[:, :], maskn[:, b, :], ALU.mult, [ksp, "maskn"], [ksm])
                    else:
                        act(sm_[:, :], e_[:, :], AF.Ln, [ke, "cols"], [ksm], bias=one_col, scale=1.0)
                    self.mmgroup(ps[RBs][:, 0:32], [(triS, sm_[:, :])], reads=[ksm, "cbf"], writes=[("ps", RBs)], start=(st == 0), stop=False)
                    tt_("dve", t_[:, :], z_[:, :], sm_[:, :], ALU.subtract, [kz, ksm], [kt])
                    tt_("dve", t_[:, :], t_[:, :], ps[RBs][:, 0:32], ALU.subtract, [kt, ("ps", RBs)], [kt])
                    act(w_[:, :], t_[:, :], AF.Exp, [kt], [kw])
                    if st == 0:
                        tt_("dve", w_[:, :], w_[:, :], maskn[:, b, :], ALU.mult, [kw, "maskn"], [kw])
                    self.mmgroup(ps[RBs][:, 0:32], [(triC, sm_[:, :])], reads=[ksm, "cbf"], writes=[("ps", RBs)], start=False, stop=(st == 16))

                    def pvfn(e, vblk=vblk, w_=w_, st=st):
                        ins = None
                        for c in range(4):
                            ins = e.matmul(ps[OBs][:, c * 32:(c + 1) * 32], lhsT=vblk[:, c * 128:(c + 1) * 128], rhs=w_[:, :], start=False, stop=(st == 16))
                        return ins
                    P.add("pe", pvfn, reads=[kw, kvb], writes=[("ps", OBs)])
                    if st == 16:
                        for c in range(4):
                            cp_("act", ymix_s[0:64, c, b * 4:(b + 1) * 4], ps[OBs][0:64, c * 32 + 8 * c:c * 32 + 8 * c + 4], [("ps", OBs)], [("ymix", c, 4)])
                            cp_("act", ymix_s[64:128, c, b * 4:(b + 1) * 4], ps[OBs][64:128, c * 32 + 8 * c + 4:c * 32 + 8 * c + 8], [("ps", OBs), ("ymix", c, 4)], [("ymix", c, 4)])
                P.barrier()
                P.force_sync = False
                A.release(ms)
            ymix = A.alloc([8, TP], BF16)

            for h in range(2):
                allgather(yloc0[h], yall0[h], [("yloc0", 2 * h + a, b) for a in range(2) for b in range(8)], [("yall0", h)], ccs[2 + h])
            for j in range(8):
                h = (j % 4) // 2
                yv = yall0[h].ap().rearrange("r (h t) -> (r h) t", h=2)
                P.add("pool", lambda e, j=j, yv=yv: e.indirect_dma_start(out=ymix[:, j, 0:TP], out_offset=None, in_=yv,
                                                                        in_offset=bass.IndirectOffsetOnAxis(ap=idxy[:, j:j + 1], axis=0)),
                      reads=[("yall0", h), "idxy"], writes=[("ymix", j, ti) for ti in range(4)], dma=True)
            if "dbgy" in DBG:
                dbg = self.outp("dbg", [1024, 2048])
                for j in range(8):
                    self.dma("pool", dbg[j * 128:(j + 1) * 128, :], ymix[:, j, :], [("ymix", j, ti) for ti in range(4)], [("dbg", j)])
            for grp, (wsrc, tis) in enumerate(((wo0loc, range(4)), (wo0g, range(4, 5)))):
                for cg in range(2):
                    wo, ko = load_w(wsrc[cg * 128:(cg + 1) * 128, :])
                    for c4 in range(4):
                        c = cg * 4 + c4
                        for ti in tis:
                            t0, nn = TT[ti]
                            P.force_sync = nn < 256
                            b = self.bank(4, 6)
                            srcs = [(ymix[:, j, t0:t0 + nn] if ti < 4 else ymix_s[:, j, :]) for j in range(8)]
                            self.mmgroup(ps[b][:, :nn], [(wo[:, j, c4 * 128:(c4 + 1) * 128], srcs[j]) for j in range(8)],
                                         reads=[("ymix", j, ti) for j in range(8)] + [ko], writes=[("ps", b)])
                            tt_("dve", x[:, c, t0:t0 + nn], ps[b][:, :nn], x[:, c, t0:t0 + nn], ALU.add, [("ps", b), ("x", c, ti)], [("x", c, ti)])
            P.force_sync = False
            P.barrier()
            A.release(m0)


        w1own = self.inp("w1own", [5 * 128, 4096])
        w1full = self.inp("w1full", [9 * 128, 4096])
        wo1loc = self.inp("wo1loc", [2 * 128, 4096])
        wo1g = self.inp("wo1g", [2 * 128, 4096])
        dcw_d = self.inp("dcw", [128, 36 * 4])
        bac_d = self.inp("bac", [128, 4])
        ogc_d = self.inp("ogc", [128, 1])
        dnk_d = self.inp("dnk", [128, (2 + 12) * 128 + 512 + 64])
        dcs_d = self.inp("dcs", [128, 24 * 48])
        dS_d = self.inp("dS", [NS * 8 * 128, 128])
        dnc_out = self.outp("dnc_out", [1536, 3])
        dnS_out = self.outp("dnS_out", [512, 128])
        dncs_out = self.outp("dncs_out", [3072, 48])
        dnSs_out = self.outp("dnSs_out", [NS * 8 * 128, 128])
        xin1 = [nc.dram_tensor(f"xin1{h}", [512, 2048], BF16) for h in range(2)]
        xall1 = [nc.dram_tensor(f"xall1{h}", [1024, 2048], BF16) for h in range(2)]
        yloc1 = [nc.dram_tensor(f"yloc1{h}", [256, 4096], BF16) for h in range(2)]
        yall1 = [nc.dram_tensor(f"yall1{h}", [512, 4096], BF16) for h in range(2)]
        dcw = self.sb("dcws", [128, 36, 4], F32)
        bac = self.sb("bacs", [128, 4], F32)
        nega = self.sb("negas", [128, 2], F32)
        ogc = self.sb("ogcs", [128, 1], F32)
        DK = {}

        def selrow(k):
            return DK["dnk"][:, (2 + k) * 128:(3 + k) * 128]
        self.dma("sp", dcw[:], dcw_d.ap().rearrange("p (a b) -> p a b", a=36), [], ["dcw"])
        self.dma("sp", bac[:], bac_d.ap(), [], ["bac"])
        self.dma("sp", ogc[:], ogc_d.ap(), [], ["ogc"])
        P.add("act", lambda e: e.activation(out=nega[:, 0:1], in_=bac[:, 1:2], func=AF.Exp), reads=["bac"], writes=["nega"])
        P.add("act", lambda e: e.activation(out=nega[:, 1:2], in_=bac[:, 3:4], func=AF.Exp), reads=["bac", "nega"], writes=["nega"])
        P.add("dve", lambda e: e.tensor_scalar(out=nega[:], in0=nega[:], scalar1=-1.0, scalar2=None, op0=ALU.mult), reads=["nega"], writes=["nega"])

        def tr_(out_ps, in_sb, reads, writes):
            return self.mmgroup(out_ps, [(in_sb, ident[:])], reads=reads + ["ident"], writes=writes)

        def mm1(out, lhsT, rhs, reads, writes, start=True, stop=True):
            return self.mmgroup(out, [(lhsT, rhs)], reads=reads, writes=writes, start=start, stop=stop)

        class DNU:
            def __init__(self, i):
                self.i = i
                self.t = {n: A.alloc([128], F32) for n in ("d1", "d2", "M", "MT", "IM", "TT", "PT", "kg", "vb", "V", "U", "O1", "O", "sq", "On")}
                self.c = A.alloc([16], F32)

        def dn_unit(W, qTc, kTc, vTc, kq, fld_beta, fld_G, kfld, Grhs, kG, sel, S, kS, nlev, yout, kyout, zsil, kz):
            t, c = W.t, W.c
            B = lambda: self.bank(0, 7)
            STOP = int(os.environ.get("UT_STOP", "99"))
            bG = B()
            mm1(ps[bG][:, 0:128], sel, Grhs, [kG, "dnk"], [("ps", bG)])
            if STOP <= 1:
                return
            ts_("dve", t["d1"][:, :], ps[bG][:, 0:128], fld_G, 0.0, ALU.subtract, ALU.max, [("ps", bG), kfld], [("u_d1", W.i)])
            ts_("dve", t["d2"][:, :], ps[bG][:, 0:128], fld_G, 0.0, ALU.subtract, ALU.min, [("ps", bG), kfld], [("u_d2", W.i)])
            ts_("dve", c[:, 0:1], ps[bG][:, 127:128], fld_G, None, ALU.subtract, None, [("ps", bG), kfld], [("u_c0", W.i)])
            act(c[:, 1:2], c[:, 0:1], AF.Exp, [("u_c0", W.i)], [("u_c1", W.i)])
            act(c[:, 2:3], ps[bG][:, 127:128], AF.Exp, [("ps", bG)], [("u_c2", W.i)])
            act(c[:, 3:4], fld_G, AF.Exp, [kfld], [("u_c3", W.i)])
            ts_("dve", c[:, 4:5], fld_beta, -1.0, None, ALU.mult, None, [kfld], [("u_c4", W.i)])
            tt_("dve", c[:, 5:6], c[:, 4:5], c[:, 3:4], ALU.mult, [("u_c4", W.i), ("u_c3", W.i)], [("u_c5", W.i)])
            act(t["d1"][:, :], t["d1"][:, :], AF.Exp, [("u_d1", W.i)], [("u_d1", W.i)], scale=-1.0)
            act(t["d2"][:, :], t["d2"][:, :], AF.Exp, [("u_d2", W.i)], [("u_d2", W.i)])
            tt_("pool", t["d1"][:, :], t["d1"][:, :], DK["dnk"][:, 0:128], ALU.mult, [("u_d1", W.i), "dnk"], [("u_d1", W.i)])
            tt_("pool", t["d2"][:, :], t["d2"][:, :], DK["dnk"][:, 128:256], ALU.mult, [("u_d2", W.i), "dnk"], [("u_d2", W.i)])
            if STOP <= 2:
                return
            bKK, bKQ = B(), B()
            mm1(ps[bKK][:, 0:128], kTc, kTc, [kq], [("ps", bKK)])
            mm1(ps[bKQ][:, 0:128], kTc, qTc, [kq], [("ps", bKQ)])
            stt_(t["M"][:, :], ps[bKK][:, 0:128], c[:, 4:5], t["d1"][:, :], ALU.mult, ALU.mult, [("ps", bKK), ("u_c4", W.i), ("u_d1", W.i)], [("u_M", W.i)])
            tt_("dve", t["PT"][:, :], ps[bKQ][:, 0:128], t["d2"][:, :], ALU.mult, [("ps", bKQ), ("u_d2", W.i)], [("u_PT", W.i)])
            if STOP <= 3:
                return
            b0 = B()
            tr_(ps[b0][:, 0:128], t["M"][:, :], [("u_M", W.i)], [("ps", b0)])
            if int(os.environ.get("UT_SUB", "99")) == -1:
                return
            cp_("act", t["MT"][:, :], ps[b0][:, 0:128], [("ps", b0)], [("u_MT", W.i)])
            if int(os.environ.get("UT_SUB", "99")) == -2:
                return
            tt_("dve", t["TT"][:, :], ps[b0][:, 0:128], ident[:], ALU.add, [("ps", b0), "ident"], [("u_TT", W.i)])
            SUB = int(os.environ.get("UT_SUB", "99"))
            if SUB <= 0:
                return
            for k in range(1, min(nlev, SUB) + 1):
                bm = B()
                mm1(ps[bm][:, 0:128], t["MT"][:, :], t["M"][:, :], [("u_MT", W.i), ("u_M", W.i)], [("ps", bm)])
                if k < nlev:
                    bt_ = B()
                    mm1(ps[bt_][:, 0:128], t["M"][:, :], t["MT"][:, :], [("u_MT", W.i), ("u_M", W.i)], [("ps", bt_)])
                tt_("dve", t["IM"][:, :], ps[bm][:, 0:128], ident[:], ALU.add, [("ps", bm), "ident"], [("u_IM", W.i)])
                if k < nlev:
                    cp_("act", t["M"][:, :], ps[bm][:, 0:128], [("ps", bm)], [("u_M", W.i)])
                    cp_("act", t["MT"][:, :], ps[bt_][:, 0:128], [("ps", bt_)], [("u_MT", W.i)])
                bx = B()
                mm1(ps[bx][:, 0:128], t["IM"][:, :], t["TT"][:, :], [("u_IM", W.i), ("u_TT", W.i)], [("ps", bx)])
                cp_("act", t["TT"][:, :], ps[bx][:, 0:128], [("ps", bx)], [("u_TT", W.i)])
            if STOP <= 4:
                return
            bk_, bv_ = B(), B()
            tr_(ps[bk_][:, 0:128], kTc, [kq], [("ps", bk_)])
            act(t["kg"][:, :], ps[bk_][:, 0:128], AF.Copy, [("ps", bk_), ("u_c1", W.i)], [("u_kg", W.i)], scale=c[:, 1:2])
            tr_(ps[bv_][:, 0:128], vTc, [kq], [("ps", bv_)])
            act(t["vb"][:, :], ps[bv_][:, 0:128], AF.Copy, [("ps", bv_), kfld], [("u_vb", W.i)], scale=fld_beta)
            if STOP <= 5:
                return
            b1 = B()
            mm1(ps[b1][:, 0:128], kTc, S, [kq, kS], [("ps", b1)])
            stt_(t["V"][:, :], ps[b1][:, 0:128], c[:, 5:6], t["vb"][:, :], ALU.mult, ALU.add, [("ps", b1), ("u_c5", W.i), ("u_vb", W.i)], [("u_V", W.i)])
            b2 = B()
            mm1(ps[b2][:, 0:128], t["TT"][:, :], t["V"][:, :], [("u_TT", W.i), ("u_V", W.i)], [("ps", b2)])
            cp_("act", t["U"][:, :], ps[b2][:, 0:128], [("ps", b2)], [("u_U", W.i)])
            b3, b4, b5 = B(), B(), B()
            mm1(ps[b3][:, 0:128], qTc, S, [kq, kS], [("ps", b3)])
            mm1(ps[b4][:, 0:128], t["PT"][:, :], t["U"][:, :], [("u_PT", W.i), ("u_U", W.i)], [("ps", b4)])
            mm1(ps[b5][:, 0:128], t["kg"][:, :], t["U"][:, :], [("u_kg", W.i), ("u_U", W.i)], [("ps", b5)])
            act(t["O1"][:, :], ps[b3][:, 0:128], AF.Copy, [("ps", b3), ("u_c3", W.i)], [("u_O1", W.i)], scale=c[:, 3:4])
            tt_("dve", t["O"][:, :], t["O1"][:, :], ps[b4][:, 0:128], ALU.add, [("u_O1", W.i), ("ps", b4)], [("u_O", W.i)])
            stt_(S, S, c[:, 2:3], ps[b5][:, 0:128], ALU.mult, ALU.add, [kS, ("u_c2", W.i), ("ps", b5)], [kS])
            if STOP <= 6:
                return
            act(t["sq"][:, :], t["O"][:, :], AF.Square, [("u_O", W.i)], [("u_sq", W.i)])
            P.add("dve", lambda e: e.reduce_sum(out=c[:, 6:7], in_=t["sq"][:, :], axis=mybir.AxisListType.X), reads=[("u_sq", W.i)], writes=[("u_c6", W.i)])
            act(c[:, 7:8], c[:, 6:7], AF.Sqrt, [("u_c6", W.i), "cols"], [("u_c7", W.i)], bias=eps_col, scale=1.0 / 128.0)
            P.add("dve", lambda e: e.reciprocal(out=c[:, 8:9], in_=c[:, 7:8]), reads=[("u_c7", W.i)], writes=[("u_c8", W.i)])
            act(t["On"][:, :], t["O"][:, :], AF.Copy, [("u_O", W.i), ("u_c8", W.i)], [("u_On", W.i)], scale=c[:, 8:9])
            b6 = B()
            tr_(ps[b6][:, 0:128], t["On"][:, :], [("u_On", W.i)], [("ps", b6)])
            stt_(yout, ps[b6][:, 0:128], ogc[:, 0:1], zsil, ALU.mult, ALU.mult, [("ps", b6), "ogc", kz], [kyout])


        def conv_dn(out, src, ci, reads, writes, three_d=False):
            def sl(k):
                return src[:, :, k:k + 4] if three_d else src[:, k:k + 512]
            ts_("dve", out, sl(0), dcw[:, ci, 0:1], None, ALU.mult, None, reads + ["dcw"], writes)
            for k in range(1, 4):
                stt_(out, sl(k), dcw[:, ci, k:k + 1], out, ALU.mult, ALU.add, reads + ["dcw"] + writes, writes)

        def l2n(buf, kbuf, nn, scale, tmp):
            sq, rt, rinv = tmp
            act(sq[:, :nn], buf, AF.Square, [kbuf], ["l2_sq"])
            self.mmgroup(ps[7][:, :nn], [(ones_bf, sq[:, :nn])], reads=["l2_sq", "cbf"], writes=[("ps", 7)])
            act(rt[:, :nn], ps[7][:, :nn], AF.Sqrt, [("ps", 7), "cols"], ["l2_rt"], bias=eps_col, scale=1.0)
            P.add("dve", lambda e: e.reciprocal(out=rinv[:, :nn], in_=rt[:, :nn]), reads=["l2_rt"], writes=["l2_ri"])
            stt_(buf, buf, scale, rinv[:, :nn], ALU.mult, ALU.mult, [kbuf, "l2_ri"], [kbuf])

        def ba_fields(ba_ps, kps, nn, bt, gt, e1, Gt, col, rm):
            act(bt[:, :nn], ba_ps, AF.Sigmoid, [kps], ["bt"])
            act(e1[:, :nn], ba_ps, AF.Exp, [kps, "bac"], ["e1"], bias=bac[:, 2 * col:2 * col + 1], scale=1.0)
            act(e1[:, :nn], e1[:, :nn], AF.Ln, ["e1", "cols"], ["e1"], bias=one_col, scale=1.0)
            ts_("dve", gt[:, :nn], e1[:, :nn], nega[:, col:col + 1], None, ALU.mult, None, ["e1", "nega"], ["gt"])
            P.add("dve", lambda e: e.tensor_tensor_scan(out=Gt[:, :nn], data0=rm, data1=gt[:, :nn], initial=0.0, op0=ALU.mult, op1=ALU.add),
                  reads=["gt", "dnk"], writes=["Gt"])

        def mixer1():
            m0 = A.mark()
            xs_pad = A.alloc([8, 128], BF16)
            ymix_s = A.alloc([8, 64], BF16)
            dnk = A.alloc([14 * 128 + 576], F32)
            DK["dnk"] = dnk
            rmask = dnk[:, 14 * 128:14 * 128 + 512]
            rmask_s = dnk[:, 14 * 128 + 512:14 * 128 + 576]
            self.dma("sp", dnk[:, :], dnk_d.ap(), [], ["dnk"])
            mx = A.mark()
            xn = A.alloc([8, T], BF16)
            rmsnorm(4, xn)
            for kc in range(8):
                self.dma("sp", xin1[kc // 4][(kc % 4) * 128:(kc % 4 + 1) * 128, :], xn[:, kc, 0:TP], [("xn", kc, ti) for ti in range(4)], [("xin1", kc)])
            for h in range(2):
                allgather(xin1[h], xall1[h], [("xin1", kc) for kc in range(4 * h, 4 * h + 4)], [("xall1", h)], ccs[4 + h])
            P.add("pool", lambda e: e.memset(xs_pad[:, :, :], 0.0), writes=["xs_pad"])
            P.add("pool", lambda e: e.tensor_copy(out=xs_pad[:, :, 0:64], in_=xn[:, :, TP:T]), reads=[("xn", kc, 4) for kc in range(8)] + ["xs_pad"], writes=["xs_pad"])
            P.add("pool", lambda e: e.memset(ymix_s[:, :, :], 0.0), writes=[("ymix", j, 4) for j in range(8)])
            P.barrier()
            A.release(mx)
            m1 = A.mark()
            hn = (A.alloc([512], BF16), A.alloc([512], F32), A.alloc([512], F32))
            bt, gt, e1, Gt = [A.alloc([512], F32) for _ in range(4)]
            W = [DNU(0), DNU(1), DNU(2)]
            if "noprompt1" not in DBG:
                m2 = A.mark()
                xf = A.alloc([8, 512], BF16)
                stage = [A.alloc([515], F32) for _ in range(2)]
                halo = A.alloc([12, 4], F32)
                qkv = [A.alloc([512], F32) for _ in range(12)]
                sz = [A.alloc([512], F32) for _ in range(4)]
                fld = A.alloc([4, 16], F32)
                Sst = [A.alloc([128], F32) for _ in range(4)]
                yt = [A.alloc([512], BF16) for _ in range(4)]
                P.add("pool", lambda e: e.memset(halo[:, :, :], 0.0), writes=["halo"])
                for hh in range(4):
                    P.add("pool", lambda e, hh=hh: e.memset(Sst[hh][:, :], 0.0), writes=[("S", hh)])
                sti = 0
                for tt in range(8):
                    rho, c0 = tt // 4, (tt % 4) * 512
                    for h in range(2):
                        self.dma("sp", xf[:, 4 * h:4 * h + 4, :], xall1[h][rho * 512:(rho + 1) * 512, c0:c0 + 512].rearrange("(kc p) t -> p kc t", p=128),
                                 [("xall1", h)], [("xf1", h)])
                    kxf = [("xf1", 0), ("xf1", 1)]
                    for grp in range(3):
                        wt, kwt = load_w(w1own[grp * 128:(grp + 1) * 128, :])
                        for hh in range(4):
                            ci = grp * 4 + hh
                            b_ = self.bank(0, 7)
                            self.mmgroup(ps[b_][:, :], [(wt[:, kc, hh * 128:(hh + 1) * 128], xf[:, kc, :]) for kc in range(8)], reads=kxf + [kwt], writes=[("ps", b_)])
                            st_ = stage[sti % 2]
                            kst_ = ("stage", sti % 2)
                            sti += 1
                            cp_("pool", st_[:, 0:3], halo[:, ci, 0:3], ["halo"], [kst_])
                            cp_("act", st_[:, 3:515], ps[b_][:, :], [("ps", b_), kst_], [kst_])
                            conv_dn(qkv[ci][:, :], st_, ci, [kst_], [("qkv", ci)])
                            cp_("pool", halo[:, ci, 0:3], st_[:, 512:515], [kst_, "halo"], ["halo"])
                            if tt == 7:
                                self.dma("sp", dnc_out[ci * 128:(ci + 1) * 128, :], st_[:, 512:515], [kst_], [("dnc_out", ci)])
                            act(qkv[ci][:, :], qkv[ci][:, :], AF.Silu, [("qkv", ci)], [("qkv", ci)])
                            if grp < 2:
                                l2n(qkv[ci][:, :], ("qkv", ci), 512, (128.0 ** -0.5) if grp == 0 else 1.0, hn)
                    wt, kwt = load_w(w1own[384:512, :])
                    for hh in range(4):
                        b_ = self.bank(0, 7)
                        self.mmgroup(ps[b_][:, :], [(wt[:, kc, hh * 128:(hh + 1) * 128], xf[:, kc, :]) for kc in range(8)], reads=kxf + [kwt], writes=[("ps", b_)])
                        act(sz[hh][:, :], ps[b_][:, :], AF.Silu, [("ps", b_)], [("sz", hh)])
                    wt, kwt = load_w(w1own[512:640, :])
                    b_ = self.bank(0, 7)
                    self.mmgroup(ps[b_][:, :], [(wt[:, kc, 0:128], xf[:, kc, :]) for kc in range(8)], reads=kxf + [kwt], writes=[("ps", b_)])
                    ba_fields(ps[b_][:, :], ("ps", b_), 512, bt, gt, e1, Gt, 0, rmask)
                    b_ = self.bank(0, 7)
                    for n in range(4):
                        mm1(ps[b_][:, n * 16:n * 16 + 8], bt[:, n * 128:(n + 1) * 128], ident[:, 0:8], ["bt", "ident"], [("ps", b_)])
                        mm1(ps[b_][:, n * 16 + 8:n * 16 + 16], Gt[:, n * 128:(n + 1) * 128], ident[:, 0:8], ["Gt", "ident"], [("ps", b_)])
                    cp_("act", fld[:, :, :], ps[b_][:, 0:64].rearrange("p (a b) -> p a b", a=4), [("ps", b_)], ["fld"])
                    u = 0
                    for n in range(4):
                        for hh in range(4):
                            cs = slice(n * 128, (n + 1) * 128)
                            dn_unit(W[u % 3], qkv[hh][:, cs], qkv[4 + hh][:, cs], qkv[8 + hh][:, cs], ("qkv", hh), fld[:, n, hh:hh + 1], fld[:, n, 12 + hh:13 + hh], "fld",
                                    Gt[:, cs], "Gt", selrow(hh), Sst[hh][:, :], ("S", hh), 6, yt[hh][:, cs], ("yt", hh), sz[hh][:, cs], ("sz", hh))
                            u += 1
                    for hh in range(4):
                        self.dma("sp", yloc1[hh // 2][(hh % 2) * 128:(hh % 2 + 1) * 128, tt * 512:(tt + 1) * 512], yt[hh][:, :], [("yt", hh)], [("yloc1", hh, tt)])
                for hh in range(4):
                    self.dma("sp", dnS_out[hh * 128:(hh + 1) * 128, :], Sst[hh][:, :], [("S", hh)], [("dnS_out", hh)])
                P.barrier()
                A.release(m2)
            if "nosample1" not in DBG:
                P.force_sync = True
                m3 = A.mark()
                cq = A.alloc([24, 64], F32)
                szs = A.alloc([8, 64], F32)
                st3 = [A.alloc([16, 7], F32) for _ in range(2)]
                qp, kp, vp, zp = [A.alloc([8, 128], F32) for _ in range(4)]
                ysp = A.alloc([8, 128], BF16)
                btp = A.alloc([128], F32)
                Gtp = A.alloc([128], F32)
                flds = A.alloc([32], F32)
                Ss = [A.alloc([128], F32) for _ in range(8)]
                W = W + [DNU(3)]
                for bufz in (qp, kp, vp, zp):
                    P.add("pool", lambda e, bufz=bufz: e.memset(bufz[:, :, :], 0.0), writes=["qkvp"])
                P.add("pool", lambda e: e.memset(btp[:, :], 0.0), writes=["btp"])
                sti = 0
                for grp in range(3):
                    for half in range(2):
                        wt, kwt = load_w(w1full[(grp * 2 + half) * 128:(grp * 2 + half + 1) * 128, :])
                        for h4 in range(4):
                            gc = grp * 8 + half * 4 + h4
                            b_ = self.bank(0, 7)
                            self.mmgroup(ps[b_][:, :64], [(wt[:, kc, h4 * 128:(h4 + 1) * 128], xs_pad[:, kc, 0:64]) for kc in range(8)], reads=["xs_pad", kwt], writes=[("ps", b_)])
                            st_ = st3[sti % 2]
                            kst_ = ("st3", sti % 2)
                            sti += 1
                            self.dma("sp", st_[:, :, 0:3], dcs_d[:, gc * 48:(gc + 1) * 48].rearrange("p (b k) -> p b k", k=3), [], [kst_ + (0,)])
                            cp_("act", st_[:, :, 3:7], ps[b_][:, :64].rearrange("p (b t) -> p b t", t=4), [("ps", b_)], [kst_ + (1,)])
                            conv_dn(cq[:, gc, :].rearrange("p (b t) -> p b t", t=4), st_, 12 + gc, [kst_ + (0,), kst_ + (1,)], [("cq", gc)], three_d=True)
                            self.dma("sp", dncs_out[gc * 128:(gc + 1) * 128, :].rearrange("p (b k) -> p b k", k=3), st_[:, :, 4:7], [kst_ + (0,), kst_ + (1,)], [("dncs_out", gc)])
                            act(cq[:, gc, :], cq[:, gc, :], AF.Silu, [("cq", gc)], [("cq", gc)])
                            if grp < 2:
                                l2n(cq[:, gc, :], ("cq", gc), 64, (128.0 ** -0.5) if grp == 0 else 1.0, hn)
                for half in range(2):
                    wt, kwt = load_w(w1full[(6 + half) * 128:(7 + half) * 128, :])
                    for h4 in range(4):
                        h = half * 4 + h4
                        b_ = self.bank(0, 7)
                        self.mmgroup(ps[b_][:, :64], [(wt[:, kc, h4 * 128:(h4 + 1) * 128], xs_pad[:, kc, 0:64]) for kc in range(8)], reads=["xs_pad", kwt], writes=[("ps", b_)])
                        act(szs[:, h, :], ps[b_][:, :64], AF.Silu, [("ps", b_)], ["szs"])
                wt, kwt = load_w(w1full[8 * 128:9 * 128, :])
                b_ = self.bank(0, 7)
                self.mmgroup(ps[b_][:, :64], [(wt[:, kc, 0:128], xs_pad[:, kc, 0:64]) for kc in range(8)], reads=["xs_pad", kwt], writes=[("ps", b_)])
                ba_fields(ps[b_][:, :64], ("ps", b_), 64, bt, gt, e1, Gt, 1, rmask_s)
                for b in range(NS):
                    cs = slice(4 * b, 4 * b + 4)
                    rk = [("cq", gc) for gc in range(24)]
                    cp_("dve", qp[:, :, 0:4], cq[:, 0:8, cs], rk + ["qkvp"], ["qkvp"])
                    cp_("dve", kp[:, :, 0:4], cq[:, 8:16, cs], rk + ["qkvp"], ["qkvp"])
                    cp_("pool", vp[:, :, 0:4], cq[:, 16:24, cs], rk + ["qkvp"], ["qkvp"])
                    cp_("pool", zp[:, :, 0:4], szs[:, :, cs], ["szs", "qkvp"], ["qkvp"])
                    cp_("pool", btp[:, 0:4], bt[:, cs], ["bt", "btp"], ["btp"])
                    cp_("pool", Gtp[:, 0:4], Gt[:, cs], ["Gt", "Gtp"], ["Gtp"])
                    cp_("pool", Gtp[:, 4:128], Gt[:, 4 * b + 3:4 * b + 4].to_broadcast([128, 124]), ["Gt", "Gtp"], ["Gtp"])
                    b_ = self.bank(0, 7)
                    mm1(ps[b_][:, 0:16], btp[:, :], ident[:, 0:16], ["btp", "ident"], [("ps", b_)])
                    mm1(ps[b_][:, 16:32], Gtp[:, :], ident[:, 0:16], ["Gtp", "ident"], [("ps", b_)])
                    cp_("act", flds[:, :], ps[b_][:, 0:32], [("ps", b_)], ["flds"])
                    for h in range(8):
                        self.dma("sp", Ss[h][:, :], dS_d[(b * 8 + h) * 128:(b * 8 + h + 1) * 128, :], [], [("Ss", h)])
                        dn_unit(W[h % 4], qp[:, h, :], kp[:, h, :], vp[:, h, :], "qkvp", flds[:, h:h + 1], flds[:, 24 + h:25 + h], "flds",
                                Gtp[:, :], "Gtp", selrow(4 + h), Ss[h][:, :], ("Ss", h), 1, ysp[:, h, :], "ysp", zp[:, h, :], "qkvp")
                        self.dma("sp", dnSs_out[(b * 8 + h) * 128:(b * 8 + h + 1) * 128, :], Ss[h][:, :], [("Ss", h)], [("dnSs_out", b, h)])
                    cp_("dve", ymix_s[:, :, cs], ysp[:, :, 0:4], ["ysp"], [("ymix", j, 4) for j in range(8)])
                P.barrier()
                P.force_sync = False
                A.release(m3)
            P.barrier()
            A.release(m1)
            ymix = A.alloc([8, TP], BF16)
            for h in range(2):
                allgather(yloc1[h], yall1[h], [("yloc1", 2 * h + a, b) for a in range(2) for b in range(8)], [("yall1", h)], ccs[6 + h])
            for j in range(8):
                h = (j % 4) // 2
                yv = yall1[h].ap().rearrange("r (h t) -> (r h) t", h=2)
                P.add("pool", lambda e, j=j, yv=yv: e.indirect_dma_start(out=ymix[:, j, 0:TP], out_offset=None, in_=yv,
                                                                        in_offset=bass.IndirectOffsetOnAxis(ap=idxy[:, j:j + 1], axis=0)),
                      reads=[("yall1", h), "idxy"], writes=[("ymix", j, ti) for ti in range(4)], dma=True)
            for grp, (wsrc, tis) in enumerate(((wo1loc, range(4)), (wo1g, range(4, 5)))):
                for cg in range(2):
                    wo, ko = load_w(wsrc[cg * 128:(cg + 1) * 128, :])
                    for c4 in range(4):
                        c = cg * 4 + c4
                        for ti in tis:
                            t0, nn = TT[ti]
                            P.force_sync = nn < 256
                            b = self.bank(4, 6)
                            srcs = [(ymix[:, j, t0:t0 + nn] if ti < 4 else ymix_s[:, j, :]) for j in range(8)]
                            self.mmgroup(ps[b][:, :nn], [(wo[:, j, c4 * 128:(c4 + 1) * 128], srcs[j]) for j in range(8)],
                                         reads=[("ymix", j, ti) for j in range(8)] + [ko], writes=[("ps", b)])
                            tt_("dve", x[:, c, t0:t0 + nn], ps[b][:, :nn], x[:, c, t0:t0 + nn], ALU.add, [("ps", b), ("x", c, ti)], [("x", c, ti)])
            P.force_sync = False
            P.barrier()
            A.release(m0)

        def finish():
            for kc in range(8):
                self.dma("sp", yT[kc * 128:(kc + 1) * 128, :], x[:, kc, :], [("x", kc, ti) for ti in range(5)], [("yT", kc)])
            P.finalize_dma_wait()
            with nc.Block() as block:
                st = P.emit(block, sems)
            self.stats = st

        if "unittest" in DBG:
            ut_in = self.inp("ut_in", [128, 7 * 128])
            ut_out = self.outp("ut_out", [128, 3 * 128])
            dnk = A.alloc([14 * 128 + 576], F32)
            DK["dnk"] = dnk
            self.dma("sp", dnk[:, :], dnk_d.ap(), [], ["dnk"])
            ui = A.alloc([7, 128], F32)
            uo = A.alloc([3, 128], F32)
            yb = A.alloc([128], BF16)
            fld = A.alloc([16], F32)
            self.dma("sp", ui[:, :, :], ut_in.ap().rearrange("p (a b) -> p a b", a=7), [], ["ui"])
            W0 = DNU(0)
            b_ = 7
            mm1(ps[b_][:, 0:8], ui[:, 4, :], ident[:, 0:8], ["ui", "ident"], [("ps", b_)])
            mm1(ps[b_][:, 8:16], ui[:, 5, :], ident[:, 0:8], ["ui", "ident"], [("ps", b_)])
            cp_("act", fld[:, :], ps[b_][:, 0:16], [("ps", b_)], ["fld"])
            nl = int(os.environ.get("UT_NLEV", "6"))
            dn_unit(W0, ui[:, 0, :], ui[:, 1, :], ui[:, 2, :], "ui", fld[:, 1:2], fld[:, 12 + 1:12 + 2], "fld", ui[:, 5, :], "ui", selrow(1),
                    ui[:, 3, :], "ui", nl, yb[:, :], "yb", ui[:, 6, :], "ui")
            cp_("dve", uo[:, 0, :], yb[:, :], ["yb"], ["uo"])
            cp_("dve", uo[:, 1, :], ui[:, 3, :], ["ui", "uo"], ["uo"])
            cp_("dve", uo[:, 2, :], W0.t["TT"][:, :], [("u_TT", 0), "uo"], ["uo"])
            self.dma("sp", ut_out.ap().rearrange("p (a b) -> p a b", a=3), uo[:, :, :], ["uo"], ["ut_out"])
            P.finalize_dma_wait()
            with nc.Block() as block:
                self.stats = P.emit(block, sems)
            return
        P.force_sync = False
        ffn(0, 0)
        if self.stage <= 1:
            return finish()
        P.barrier()
        mixer0()
        if self.stage <= 2:
            return finish()
        P.barrier()
        ffn(0, 1)
        if self.stage <= 3:
            return finish()
        P.barrier()
        ffn(1, 0)
        if self.stage <= 4:
            return finish()
        P.barrier()
        mixer1()
        if self.stage <= 5:
            return finish()
        P.barrier()
        ffn(1, 1)
        return finish()


_CACHE = {}


def _consts():
    ident = np.eye(128, dtype=np.float32)
    j = np.arange(128)[:, None]
    s = np.arange(128)[None, :]
    ones = np.ones((128, 128), np.float32)
    triS = (j > s).astype(np.float32)
    triC = (j <= s).astype(np.float32)
    blk = ((j // 64) == (s // 64)).astype(np.float32) / 64.0
    cbf = np.concatenate([ones, triS, triC, blk, ident], axis=1)
    return ident, cbf


def _host_prep(inputs):
    ident, cbf = _consts()
    gl = []
    for l in range(2):
        for nm in ("norm_ffn1", "norm_mix", "norm_ffn2"):
            gl.append(np.asarray(inputs[nm][l], np.float32).reshape(8, 128).T)
    gains = np.ascontiguousarray(np.concatenate(gl, axis=1))
    wfi = np.empty((2, 2, 4, 2, 128, 8, 512), np.float32)
    wfo = np.empty((2, 2, 2, 2, 128, 8, 512), np.float32)
    for l in range(2):
        for f, (ni, no) in enumerate((("w_ffn1_in", "w_ffn1_out"), ("w_ffn2_in", "w_ffn2_out"))):
            wi = np.asarray(inputs[ni][l], np.float32).reshape(8, 128, 2, 4, 512)
            wfi[l, f] = wi.transpose(3, 2, 1, 0, 4)
            wo = np.asarray(inputs[no][l], np.float32).reshape(2, 8, 128, 2, 512)
            wfo[l, f] = wo.transpose(0, 3, 2, 1, 4)
    common = {"gains": gains, "ident": ident, "cbf": cbf,
              "wfi": wfi.reshape(-1, 4096), "wfo": wfo.reshape(-1, 4096)}
    f32 = lambda a: np.asarray(a, np.float32)

    def wtile(w):
        return np.ascontiguousarray(w.reshape(8, 128, 512).transpose(1, 0, 2)).reshape(128, 4096)

    def wotiles(w):
        return np.ascontiguousarray(w.reshape(8, 128, 2, 512).transpose(2, 1, 0, 3)).reshape(256, 4096)

    def blockdiag(w8, gc):
        o = np.zeros((128, 128), np.float32)
        o[0:64, 0:64] = w8[2 * gc]
        o[64:128, 64:128] = w8[2 * gc + 1]
        return o

    We = f32(inputs["w_in_even"][0])
    Woe = f32(inputs["w_out_even"][0])
    wa8, wi8 = f32(inputs["lru_w_a"][0]), f32(inputs["lru_w_i"][0])
    cw, cb = f32(inputs["lru_conv_w"][0]), f32(inputs["lru_conv_b"][0])
    b_a, b_i, lam = f32(inputs["lru_b_a"][0]), f32(inputs["lru_b_i"][0]), f32(inputs["lru_lambda"][0])
    qg, kg = f32(inputs["sb_q_gain"][0]), f32(inputs["sb_k_gain"][0])
    sbias = f32(inputs["sb_bias"][0])

    def lccols(gc):
        sl = slice(gc * 128, (gc + 1) * 128)
        return np.stack([cw[0, sl], cw[1, sl], cw[2, sl], cw[3, sl], cb[sl], b_a[sl], b_i[sl], lam[sl]], axis=1)

    pp = np.arange(128)
    maskd = np.concatenate([(np.arange(512)[None, :] > (pp[:, None] + 128 * m)).astype(np.float32) for m in range(4)], axis=1)
    maskn = np.zeros((128, 16, 8, 4), np.float32)
    for b in range(16):
        for t in range(4):
            maskn[4 * b + t, b, :, :] = (t < np.arange(4))[None, :]
    common.update({
        "w0full": np.concatenate([wtile(We[:, i * 512:(i + 1) * 512]) for i in range(5)], axis=0),
        "wo0g": wotiles(Woe), "qkcol": np.stack([np.tile(qg, 2), np.tile(kg, 2)], axis=1),
        "sbrow": np.ascontiguousarray(np.broadcast_to(np.repeat(sbias, 4)[None, :], (128, 32))),
        "maskd": maskd, "maskn": maskn.reshape(128, 512),
        "iota": pp.astype(np.float32)[:, None].copy(),
        "ck": f32(inputs["cache_k"][0][:POOLN]).reshape(POOLN * 128, 512), "cv": f32(inputs["cache_v"][0][:POOLN]).reshape(POOLN * 128, 512),
    })
    Wd = f32(inputs["w_in_odd"][0])
    Wod = f32(inputs["w_out_odd"][0])
    dconv = f32(inputs["dn_conv_w"][0])
    Alog, dtb = f32(inputs["dn_A_log"][0]), f32(inputs["dn_dt_bias"][0])
    bag = np.zeros((1024, 512), np.float32)
    bag[:, 0:8] = Wd[:, 4096:4104]
    bag[:, 8:16] = Wd[:, 4104:4112]
    w1full = [wtile(Wd[:, i * 512:(i + 1) * 512]) for i in range(8)] + [wtile(bag)]
    jj = np.arange(128)[:, None]
    ff = np.arange(128)[None, :]
    dnk = [(jj > ff).astype(np.float32), (ff >= jj).astype(np.float32)]
    for k in range(12):
        sel = np.zeros((128, 128), np.float32)
        sel[4 + k, :] = 1.0
        dnk.append(sel)
    rm = np.ones((128, 512), np.float32)
    rm[:, ::128] = 0.0
    rms = np.ones((128, 64), np.float32)
    rms[:, ::4] = 0.0
    dnk += [rm, rms]
    common.update({"w1full": np.concatenate(w1full, axis=0), "wo1g": wotiles(Wod), "ogc": f32(inputs["dn_o_gain"][0])[:, None].copy(),
                   "dnk": np.concatenate(dnk, axis=1)})
    dcs_all = f32(inputs["state_dn_conv"][0])
    dS_all = f32(inputs["state_dn_S"][0])
    pt = np.asarray(inputs["page_table"], np.int32)
    lcs_all = f32(inputs["state_lru_conv"][0])
    lhs_all = f32(inputs["state_lru_h"][0])
    percore = []
    for c in range(NCORES):
        r = c % 2
        own = [2 * r, 2 * r + 1]
        oth = [2 * (1 - r), 2 * (1 - r) + 1]
        tiles = []
        for lp in range(2):
            gc = own[lp]
            tiles.append(wtile(np.concatenate([We[:, gc * 128:(gc + 1) * 128], We[:, 512 + gc * 128:512 + (gc + 1) * 128],
                                               We[:, 1024 + gc * 128:1024 + (gc + 1) * 128], np.zeros((1024, 128), np.float32)], axis=1)))
        tiles.append(wtile(np.concatenate([We[:, 1536 + own[0] * 128:1536 + own[0] * 128 + 128], We[:, 1536 + own[1] * 128:1536 + own[1] * 128 + 128],
                                           We[:, 2048 + own[0] * 128:2048 + own[0] * 128 + 128], We[:, 2048 + own[1] * 128:2048 + own[1] * 128 + 128]], axis=1)))
        rows = []
        for grp in (own, oth):
            rows += [Woe[g * 128:(g + 1) * 128] for g in grp] + [Woe[512 + g * 128:512 + (g + 1) * 128] for g in grp]
        gwl = [blockdiag(wa8, g) for g in own] + [blockdiag(wi8, g) for g in own] + [blockdiag(wa8, g) for g in range(4)] + [blockdiag(wi8, g) for g in range(4)]
        lcl = [lccols(g) for g in own] + [lccols(g) for g in range(4)]
        heads_own = [4 * r + i for i in range(4)]
        sbb = np.concatenate([np.broadcast_to(sbias[heads_own][None, :], (128, 4)), np.broadcast_to(sbias[None, :], (128, 8))], axis=1)
        idxy = np.zeros((128, 8), np.int32)
        for j in range(8):
            rho = r if j < 4 else 1 - r
            idxy[:, j] = (rho * 256 + (j % 2) * 128 + pp) * 2 + r
        sl = slice(c * NS, (c + 1) * NS)
        lcs = lcs_all[sl].reshape(NS, 3, 4, 128).transpose(3, 2, 0, 1)
        lhs = lhs_all[sl].reshape(NS, 4, 128).transpose(2, 1, 0)
        hown = [4 * r + i for i in range(4)]
        hoth = [4 * (1 - r) + i for i in range(4)]
        bao = np.zeros((1024, 512), np.float32)
        for i, h in enumerate(hown):
            bao[:, i] = Wd[:, 4096 + h]
            bao[:, 4 + i] = Wd[:, 4104 + h]
        w1own = [wtile(Wd[:, g * 1024 + 4 * r * 128:g * 1024 + 4 * r * 128 + 512]) for g in range(4)] + [wtile(bao)]
        dcwl = [dconv[:, g * 1024 + h * 128:g * 1024 + (h + 1) * 128].T for g in range(3) for h in hown] + \
               [dconv[:, gc * 128:(gc + 1) * 128].T for gc in range(24)]
        bacc = np.zeros((128, 4), np.float32)
        for i, h in enumerate(hown):
            bacc[4 + i, 0] = dtb[h]
            bacc[4 + i, 1] = Alog[h]
        bacc[8:16, 2] = dtb
        bacc[8:16, 3] = Alog
        rows1 = [Wod[h * 128:(h + 1) * 128] for h in hown + hoth]
        percore_l1 = {
            "w1own": np.concatenate(w1own, axis=0), "wo1loc": wotiles(np.concatenate(rows1, axis=0)),
            "dcw": np.ascontiguousarray(np.stack(dcwl, axis=1)).reshape(128, 144), "bac": bacc,
            "dcs": np.ascontiguousarray(dcs_all[c * NS:(c + 1) * NS].reshape(NS, 3, 24, 128).transpose(3, 2, 0, 1)).reshape(128, 24 * 48),
            "dS": np.ascontiguousarray(dS_all[c * NS:(c + 1) * NS]).reshape(NS * 8 * 128, 128),
        }
        percore.append({
            **percore_l1,
            "w0own": np.concatenate(tiles, axis=0), "wo0loc": wotiles(np.concatenate(rows, axis=0)),
            "gw": np.concatenate(gwl, axis=1), "lc": np.concatenate(lcl, axis=1), "sbb": np.ascontiguousarray(sbb),
            "idxy": idxy, "ptb": np.ascontiguousarray(np.broadcast_to(pt[sl].reshape(1, 256), (128, 256))),
            "lcs": np.ascontiguousarray(lcs).reshape(128, 192), "lhs": np.ascontiguousarray(lhs).reshape(128, 64),
        })
    xp = np.asarray(inputs["x_prompt"], np.float32)
    xs = np.asarray(inputs["x_sample"], np.float32)
    maps = []
    for c in range(NCORES):
        p, r = c // 2, c % 2
        xt = np.concatenate([xp[p, r * TP:(r + 1) * TP], xs[c * NS:(c + 1) * NS].reshape(TS, D)], axis=0)
        m = dict(common)
        m.update(percore[c])
        m["xT"] = np.ascontiguousarray(xt.T)
        maps.append(m)
    return maps


def _run(inputs, stage=99):
    if stage not in _CACHE:
        b = Builder(stage)
        b.build()
        _CACHE[stage] = b
    b = _CACHE[stage]
    maps = _host_prep(inputs)
    res = run_bass_kernel_spmd(b.nc, maps, core_ids=list(range(NCORES)))
    return res.results


def _assemble_dn(o, p, r, sl, dcp, dsp, dcs, dss):
    dnc = o["dnc_out"].reshape(3, 4, 128, 3)
    for g in range(3):
        dcp[0, p, :, g * 1024 + 4 * r * 128:g * 1024 + (4 * r + 4) * 128] = dnc[g].reshape(512, 3).T
    dsp[0, p, 4 * r:4 * r + 4] = o["dnS_out"].reshape(4, 128, 128)
    dcs[0, sl] = o["dncs_out"].reshape(3072, NS, 3).transpose(1, 2, 0)
    dss[0, sl] = o["dnSs_out"].reshape(NS, 8, 128, 128)


def kernel(**inputs):
    res = _run(inputs)
    f = np.float32
    yp = np.empty((4, 4096, D), f)
    ys = np.empty((128, 4, D), f)
    kp = np.zeros((1, 4, 4096, 8, 64), f); vp = np.zeros((1, 4, 4096, 8, 64), f)
    lcp = np.zeros((1, 4, 3, 512), f); lhp = np.zeros((1, 4, 512), f)
    dcp = np.zeros((1, 4, 3, 3072), f); dsp = np.zeros((1, 4, 8, 128, 128), f)
    ksm = np.zeros((1, 128, 4, 8, 64), f); vsm = np.zeros((1, 128, 4, 8, 64), f)
    lcs = np.zeros((1, 128, 3, 512), f); lhs = np.zeros((1, 128, 512), f)
    dcs = np.zeros((1, 128, 3, 3072), f); dss = np.zeros((1, 128, 8, 128, 128), f)
    for c in range(NCORES):
        p, r = c // 2, c % 2
        o = res[c]
        yt = o["yT"].T
        yp[p, r * TP:(r + 1) * TP] = yt[:TP]
        ys[c * NS:(c + 1) * NS] = yt[TP:].reshape(NS, 4, D)
        sl = slice(c * NS, (c + 1) * NS)
        kp[0, p, :, 4 * r:4 * r + 4, :] = o["kT_out"].reshape(4, 64, 4096).transpose(2, 0, 1)
        vp[0, p, :, 4 * r:4 * r + 4, :] = o["v_out"].reshape(4096, 4, 64)
        lcp[0, p, :, 256 * r:256 * r + 256] = o["lruc_out"].T
        lhp[0, p, 256 * r:256 * r + 256] = o["lruh_out"][:, 0]
        ksm[0, sl] = o["ks_out"].T.reshape(NS, 4, 8, 64)
        vsm[0, sl] = o["vs_out"].reshape(NS, 4, 8, 64)
        lcs[0, sl] = o["lrucs_out"].reshape(512, NS, 3).transpose(1, 2, 0)
        lhs[0, sl] = o["lruhs_out"].T
        if "dnc_out" in o:
            _assemble_dn(o, p, r, sl, dcp, dsp, dcs, dss)
    return (yp, ys, kp, vp, lcp, lhp, dcp, dsp, ksm, vsm, lcs, lhs, dcs, dss)
```
